# Optimizing a Trainium2 kernel written in Bass

```python
import math
import jax, jax.numpy as jnp
from jax import lax
import numpy as np

D_MODEL = 2048
BATCH = 8
SEQ = 2048
DEPTH = 1

A_HEADS = 16
A_HEAD_DIM = 64
A_KV_GROUPS = 4
A_REP = A_HEADS // A_KV_GROUPS
A_WIDTH = A_HEADS * A_HEAD_DIM
A_KV_WIDTH = A_KV_GROUPS * A_HEAD_DIM
CMP_BLOCK = 32
CMP_STRIDE = 16
CMP_HIDDEN = 4 * A_HEAD_DIM
SEL_BLOCK = 64
SEL_TOPK = 16
SEL_QCHUNK = 32
WIN_SIZE = 512
FORCE_BONUS = 1.0e4
B_HEADS = 8
B_HEAD_DIM = 128
B_WIDTH = B_HEADS * B_HEAD_DIM
DILATION_PATTERNS = ((128, 1), (512, 4), (2048, 16))
BAND_BLOCK = 128
ROPE_THETA = 500000.0
ROPE_FRACTION = 4
EPS = 1e-6
NEG = -1e30
IN_SIZES = (A_WIDTH,) + (A_KV_WIDTH,) * 6 + (A_HEADS * 3, A_WIDTH) + (B_WIDTH,) * 4 + (D_MODEL, D_MODEL)
N_IN = sum(IN_SIZES)
IN_SPLITS = [int(s) for s in np.cumsum(IN_SIZES)[:-1]]

kernel_name = 'hybrid_nsa_dilated_gated_block'

F32 = jnp.float32


def rms_norm(x, g):
    xf = x.astype(F32)
    y = xf * lax.rsqrt(jnp.mean(xf * xf, axis=-1, keepdims=True) + EPS)
    return (y * g.astype(F32)).astype(x.dtype)


def partial_rope(x, pos):
    dh = x.shape[-1]
    rot = dh // ROPE_FRACTION
    half = rot // 2
    inv = (ROPE_THETA ** (-np.arange(0, rot, 2) / rot)).astype(np.float32)
    ang = pos.astype(F32)[..., None] * inv
    ang = ang.reshape(ang.shape[:2] + (1,) * (x.ndim - 3) + (half,))
    cos, sin = jnp.cos(ang), jnp.sin(ang)
    x1 = x[..., :half].astype(F32)
    x2 = x[..., half:rot].astype(F32)
    out = jnp.concatenate([x1 * cos - x2 * sin, x2 * cos + x1 * sin, x[..., rot:].astype(F32)], axis=-1)
    return out.astype(x.dtype)


def banded_attention(q, k, v, max_dist):
    N, L, G, R, dh = q.shape
    bs = math.gcd(L, BAND_BLOCK)
    nb = L // bs
    span = max_dist + bs
    kp = jnp.pad(k, ((0, 0), (max_dist, 0), (0, 0), (0, 0)))
    vp = jnp.pad(v, ((0, 0), (max_dist, 0), (0, 0), (0, 0)))
    idx = np.arange(nb)[:, None] * bs + np.arange(span)[None, :]
    kb = kp[:, idx]
    vb = vp[:, idx]
    qb = q.reshape(N, nb, bs, G, R, dh)
    s = jnp.einsum('nbqgrd,nbkgd->nbgrqk', qb, kb, preferred_element_type=F32) * (dh ** -0.5)
    qpos = np.arange(nb)[:, None] * bs + np.arange(bs)[None, :]
    kpos = idx - max_dist
    diff = qpos[:, :, None] - kpos[:, None, :]
    mask = (diff >= 0) & (diff <= max_dist) & (kpos[:, None, :] >= 0)
    s = jnp.where(mask[None, :, None, None], s, NEG)
    lse = jax.nn.logsumexp(s, axis=-1)
    p = jnp.exp(s - lse[..., None])
    o = jnp.einsum('nbgrqk,nbkgd->nbqgrd', p.astype(vb.dtype), vb, preferred_element_type=F32)
    o = o.reshape(N, L, G, R, dh)
    lse = lse.transpose(0, 1, 4, 2, 3).reshape(N, L, G, R)
    return o, lse


def dilated_mixture(q, k, v):
    B, S, H, dh = q.shape
    outs, lses = [], []
    for w, r in DILATION_PATTERNS:
        L = S // r
        def to_cls(t):
            return t.reshape(B, L, r, H, dh).transpose(0, 2, 1, 3, 4).reshape(B * r, L, H, dh)
        o, lse = banded_attention(to_cls(q)[:, :, :, None], to_cls(k), to_cls(v), w // r)
        outs.append(o.reshape(B, r, L, H, dh).transpose(0, 2, 1, 3, 4).reshape(B, S, H, dh))
        lses.append(lse.reshape(B, r, L, H).transpose(0, 2, 1, 3).reshape(B, S, H))
    wts = jax.nn.softmax(jnp.stack(lses, axis=0), axis=0)
    return jnp.sum(wts[..., None] * jnp.stack(outs, axis=0), axis=0)


def nsa_attention(q, k_cmp, v_cmp, k_sel, v_sel, k_win, v_win, gate_logits, pos,
                  pe_ck, pe_cv, w_ck1, w_ck2, w_cv1, w_cv2):
    B, S, G, R, dh = q.shape
    scale = dh ** -0.5
    tq = np.arange(S)
    n_c = (S - CMP_BLOCK) // CMP_STRIDE + 1
    cidx = np.arange(n_c)[:, None] * CMP_STRIDE + np.arange(CMP_BLOCK)[None, :]
    cmp_end = cidx[:, -1]
    def compress(t, pe, w1, w2):
        blk = t[:, cidx] + pe[None, None, :, None, :]
        blk = blk.transpose(0, 1, 3, 2, 4).reshape(B, n_c, G, CMP_BLOCK * dh)
        return jax.nn.gelu(blk @ w1) @ w2
    kc = partial_rope(compress(k_cmp, pe_ck, w_ck1, w_ck2), pos[:, cmp_end])
    vc = compress(v_cmp, pe_cv, w_cv1, w_cv2)
    cmask = (cmp_end[None, :] <= tq[:, None])[None, :, None, None, :]
    s = jnp.einsum('bsgrd,bcgd->bsgrc', q, kc, preferred_element_type=F32) * scale
    s = jnp.where(cmask, s, NEG)
    e = jnp.exp(s - jnp.max(s, axis=-1, keepdims=True)) * cmask
    p_cmp = e / jnp.maximum(jnp.sum(e, axis=-1, keepdims=True), 1.0)
    o_cmp = jnp.einsum('bsgrc,bcgd->bsgrd', p_cmp.astype(vc.dtype), vc, preferred_element_type=F32)
    n_s = S // SEL_BLOCK
    cs = cidx[:, 0]
    ss = np.arange(n_s) * SEL_BLOCK
    overlap = np.clip(np.minimum(cs[:, None] + CMP_BLOCK, ss[None, :] + SEL_BLOCK)
                      - np.maximum(cs[:, None], ss[None, :]), 0, None).astype(np.float32) / CMP_BLOCK
    imp = jnp.einsum('bsgc,cj->bsgj', jnp.sum(p_cmp, axis=3), overlap)
    jb = np.arange(n_s)[None, :]
    cur = (tq // SEL_BLOCK)[:, None]
    valid = ss[None, :] <= tq[:, None]
    forced = (valid & ((jb == 0) | (jb == cur) | (jb == cur - 1))).astype(np.float32)
    score = jnp.where(valid[None, :, None, :], imp + FORCE_BONUS * forced[None, :, None, :], NEG)
    n_sel = min(SEL_TOPK, n_s)
    _, sel_idx = lax.top_k(score, n_sel)
    K = n_sel * SEL_BLOCK
    Qc = SEL_QCHUNK
    nq = S // Qc
    ksT = k_sel.transpose(0, 2, 1, 3)
    vsT = v_sel.transpose(0, 2, 1, 3)
    gather = jax.vmap(jax.vmap(lambda a, i: jnp.take(a, i, axis=0)))
    def sel_chunk(args):
        qb, ib, tb = args
        tok = ib[..., None] * SEL_BLOCK + jnp.arange(SEL_BLOCK, dtype=jnp.int32)
        tok = tok.reshape(B, Qc, G, K).transpose(0, 2, 1, 3)
        flat = tok.reshape(B, G, Qc * K)
        kg = gather(ksT, flat).reshape(B, G, Qc, K, dh)
        vg = gather(vsT, flat).reshape(B, G, Qc, K, dh)
        sc = jnp.einsum('bqgrd,bgqkd->bgrqk', qb, kg, preferred_element_type=F32) * scale
        m = (tok <= tb[None, None, :, None])[:, :, None]
        pr = jax.nn.softmax(jnp.where(m, sc, NEG), axis=-1)
        return jnp.einsum('bgrqk,bgqkd->bqgrd', pr.astype(vg.dtype), vg, preferred_element_type=F32)
    qc = q.reshape(B, nq, Qc, G, R, dh).transpose(1, 0, 2, 3, 4, 5)
    ic = sel_idx.reshape(B, nq, Qc, G, n_sel).transpose(1, 0, 2, 3, 4)
    tc = jnp.asarray(tq.reshape(nq, Qc), dtype=jnp.int32)
    o_sel = lax.map(sel_chunk, (qc, ic, tc)).transpose(1, 0, 2, 3, 4, 5).reshape(B, S, G, R, dh)
    o_win, _ = banded_attention(q, k_win, v_win, WIN_SIZE - 1)
    g = jax.nn.sigmoid(gate_logits.astype(F32)).reshape(B, S, G, R, 3)
    o = g[..., 0:1] * o_cmp + g[..., 1:2] * o_sel + g[..., 2:3] * o_win
    return o.reshape(B, S, G * R * dh)


def hybrid_layer(x, c, pos, w_ada, b_ada, g_pre, g_post, w_in, pe_ck, pe_cv, w_ck1, w_ck2,
                 w_cv1, w_cv2, w_br_a, w_br_b, w_out):
    B, S, D = x.shape
    ada = c @ w_ada + b_ada
    shift, scale, gate = jnp.split(ada, 3, axis=-1)
    h = rms_norm(x, g_pre) * (1.0 + scale[:, None, :]) + shift[:, None, :]
    proj = h @ w_in
    (qa, kc, vc, ks, vs, kw, vw, ga, za, qb, kb, vb, zb, ma, mb) = jnp.split(proj, IN_SPLITS, axis=-1)
    kv = lambda t: t.reshape(B, S, A_KV_GROUPS, A_HEAD_DIM)
    qa = partial_rope(qa.reshape(B, S, A_KV_GROUPS, A_REP, A_HEAD_DIM), pos)
    oa = nsa_attention(qa, kv(kc), kv(vc), partial_rope(kv(ks), pos), kv(vs),
                       partial_rope(kv(kw), pos), kv(vw), ga, pos,
                       pe_ck, pe_cv, w_ck1, w_ck2, w_cv1, w_cv2)
    ya = (oa * jax.nn.silu(za.astype(F32))).astype(x.dtype) @ w_br_a
    hd = lambda t: t.reshape(B, S, B_HEADS, B_HEAD_DIM)
    ob = dilated_mixture(partial_rope(hd(qb), pos), partial_rope(hd(kb), pos), hd(vb))
    ob = ob.reshape(B, S, B_WIDTH)
    yb = (ob * jax.nn.silu(zb.astype(F32))).astype(x.dtype) @ w_br_b
    merged = jax.nn.sigmoid(ma) * ya + jax.nn.sigmoid(mb) * yb
    out = merged @ w_out
    return x + (gate[:, None, :] * rms_norm(out, g_post)).astype(x.dtype)


def setup_inputs(seed: int = 0) -> dict:
    key = jax.random.key(seed)
    ks = jax.random.split(key, 20)
    nrm = lambda k, shape, s: jax.random.normal(k, shape, dtype=F32) * s
    D = D_MODEL
    x = nrm(ks[0], (BATCH, SEQ, D), 1.0)
    c = nrm(ks[1], (BATCH, D), 1.0)
    positions = (jnp.arange(SEQ, dtype=jnp.int32)[None, :]
                 + jax.random.randint(ks[2], (BATCH, 1), 0, 4096, dtype=jnp.int32))
    return {
        'x': x,
        'c': c,
        'positions': positions,
        'w_ada': nrm(ks[3], (DEPTH, D, 3 * D), 0.5 * D ** -0.5),
        'b_ada': nrm(ks[4], (DEPTH, 3 * D), 0.01),
        'g_pre': 1.0 + nrm(ks[5], (DEPTH, D), 0.02),
        'g_post': 1.0 + nrm(ks[6], (DEPTH, D), 0.02),
        'w_in': nrm(ks[7], (DEPTH, D, N_IN), D ** -0.5),
        'pe_ck': nrm(ks[8], (DEPTH, CMP_BLOCK, A_HEAD_DIM), 0.02),
        'pe_cv': nrm(ks[9], (DEPTH, CMP_BLOCK, A_HEAD_DIM), 0.02),
        'w_ck1': nrm(ks[10], (DEPTH, CMP_BLOCK * A_HEAD_DIM, CMP_HIDDEN), (CMP_BLOCK * A_HEAD_DIM) ** -0.5),
        'w_ck2': nrm(ks[11], (DEPTH, CMP_HIDDEN, A_HEAD_DIM), CMP_HIDDEN ** -0.5),
        'w_cv1': nrm(ks[12], (DEPTH, CMP_BLOCK * A_HEAD_DIM, CMP_HIDDEN), (CMP_BLOCK * A_HEAD_DIM) ** -0.5),
        'w_cv2': nrm(ks[13], (DEPTH, CMP_HIDDEN, A_HEAD_DIM), CMP_HIDDEN ** -0.5),
        'w_br_a': nrm(ks[14], (DEPTH, A_WIDTH, D), A_WIDTH ** -0.5),
        'w_br_b': nrm(ks[15], (DEPTH, B_WIDTH, D), B_WIDTH ** -0.5),
        'w_out': nrm(ks[16], (DEPTH, D, D), D ** -0.5),
    }


def reference(x, c, positions, w_ada, b_ada, g_pre, g_post, w_in, pe_ck, pe_cv, w_ck1, w_ck2,
              w_cv1, w_cv2, w_br_a, w_br_b, w_out):
    h = x
    for layer in range(DEPTH):
        h = hybrid_layer(h, c, positions, w_ada[layer], b_ada[layer], g_pre[layer], g_post[layer],
                         w_in[layer], pe_ck[layer], pe_cv[layer], w_ck1[layer], w_ck2[layer],
                         w_cv1[layer], w_cv2[layer], w_br_a[layer], w_br_b[layer], w_out[layer])
    return h
```

```python
import numpy as np
from contextlib import ExitStack
import ml_dtypes
import concourse.bass as bass
import concourse.mybir as mybir
from concourse.bass_utils import run_bass_kernel_spmd

F32 = mybir.dt.float32
BF16 = mybir.dt.bfloat16
I32 = mybir.dt.int32
AF = mybir.ActivationFunctionType
ALU = mybir.AluOpType

D = 2048
S = 2048
NT = 16
N_IN = 11824
EPS = 1e-6
BIG = 30000.0
THETA = 500000.0
C_QA, C_KC, C_VC, C_KS, C_VS, C_KW, C_VW, C_GA, C_ZA = 0, 1024, 1280, 1536, 1792, 2048, 2304, 2560, 2608
C_QB, C_KB, C_VB, C_ZB, C_MA, C_MB = 3632, 4656, 5680, 6704, 7728, 9776
NC_CMP = 127


class T:
    __slots__ = ("h", "name", "w", "r", "dsem", "dcnt", "excl")

    def __init__(self, h, name="", excl=False):
        self.h = h
        self.name = name
        self.excl = excl
        self.w = {}
        self.r = {}
        self.dsem = None
        self.dcnt = 0

    def __getitem__(self, idx):
        return self.h[idx]


class K:
    ROT = 20000

    def __init__(self, nc, stack):
        self.nc = nc
        self.stack = stack
        self.eng = {"pe": nc.tensor, "act": nc.scalar, "dve": nc.vector, "pool": nc.gpsimd, "sp": nc.sync}
        self.sem = {}
        self.cnt = {}
        self.waited = {e: {} for e in self.eng}
        self.nsem = 0
        self.all_dma = {}
        for e in self.eng:
            self.sem[e] = self.new_sem("e_" + e)
            self.cnt[e] = 0
        self.ninst = {e: 0 for e in self.eng}

    def new_sem(self, name):
        self.nsem += 1
        return self.stack.enter_context(self.nc.semaphore(f"{name}_{self.nsem}"))

    def sb(self, name, shape, dt, stack=None):
        return T((stack or self.stack).enter_context(self.nc.sbuf_tensor(name, list(shape), dt)), name)

    def ps(self, name, shape, dt=F32):
        return T(self.stack.enter_context(self.nc.psum_tensor(name, list(shape), dt)), name, excl=True)

    def alias(self, t, name=""):
        return T(t.h, name or t.name)

    def _wait(self, e, deps):
        need = {}
        for d in deps:
            s, v, src = d
            if e == "pe" and src == "pe":
                continue
            kk = id(s)
            if kk not in need or need[kk][1] < v:
                need[kk] = (s, v)
        for kk, (s, v) in need.items():
            if self.waited[e].get(kk, 0) < v:
                self.eng[e].wait_ge(s, v)
                self.waited[e][kk] = v

    @staticmethod
    def _deps(reads, writes):
        deps = []
        for t in reads:
            deps.extend(t.w.values())
            if t.excl:
                deps.extend(t.r.values())
        for t in writes:
            deps.extend(t.w.values())
            deps.extend(t.r.values())
        return deps

    def op(self, e, fn, reads=(), writes=()):
        self._wait(e, self._deps(reads, writes))
        inst = fn(self.eng[e])
        if self.cnt[e] >= self.ROT:
            self.sem[e] = self.new_sem("e_" + e)
            self.cnt[e] = 0
        self.cnt[e] += 1
        self.ninst[e] += 1
        inst.then_inc(self.sem[e], 1)
        d = (self.sem[e], self.cnt[e], e)
        for t in reads:
            t.r[id(d[0])] = d
        for t in writes:
            t.w[id(d[0])] = d
        return inst

    def dma(self, q, out, in_, dst, srcs=(), **kw):
        self._wait(q, self._deps(srcs, (dst,)))
        if dst.dsem is None or dst.dcnt >= 1500:
            dst.dsem = self.new_sem("d_" + dst.name)
            dst.dcnt = 0
        inst = self.eng[q].dma_start(out=out, in_=in_, **kw)
        dst.dcnt += 1
        inst.then_inc(dst.dsem, 16)
        d = (dst.dsem, 16 * dst.dcnt, "dma")
        for t in srcs:
            t.r[id(d[0])] = d
        dst.w[id(d[0])] = d
        self.all_dma[id(d[0])] = d
        return d

    def barrier(self):
        deps = [(self.sem[e], self.cnt[e], "bar") for e in self.eng if self.cnt[e] > 0]
        deps += list(self.all_dma.values())
        for e in self.eng:
            self._wait(e, [d for d in deps if not (d[0] is self.sem[e])])

    def tt(self, e, out, in0, in1, op, R, W):
        return self.op(e, lambda E: E.tensor_tensor(out=out, in0=in0, in1=in1, op=op), R, W)

    def ts(self, e, out, in0, s1, op0, R, W, s2=None, op1=None):
        if op1 is None:
            return self.op(e, lambda E: E.tensor_scalar(out=out, in0=in0, scalar1=s1, scalar2=None, op0=op0), R, W)
        return self.op(e, lambda E: E.tensor_scalar(out=out, in0=in0, scalar1=s1, scalar2=s2, op0=op0, op1=op1), R, W)

    def stt(self, e, out, in0, scalar, in1, op0, op1, R, W):
        return self.op(e, lambda E: E.scalar_tensor_tensor(out=out, in0=in0, scalar=scalar, in1=in1, op0=op0, op1=op1), R, W)

    def act(self, out, in_, func, R, W, bias=None, scale=None, accum=None):
        kw = {}
        if bias is not None:
            kw["bias"] = bias
        if scale is not None:
            kw["scale"] = scale
        if accum is not None:
            kw["accum_out"] = accum
        return self.op("act", lambda E: E.activation(out=out, in_=in_, func=func, **kw), R, W)

    def cp(self, e, out, in_, R, W):
        if e == "act":
            return self.op("act", lambda E: E.copy(out=out, in_=in_), R, W)
        return self.op(e, lambda E: E.tensor_copy(out=out, in_=in_), R, W)

    def mm(self, out, lhsT, rhs, start, stop, R, W):
        return self.op("pe", lambda E: E.matmul(out, lhsT=lhsT, rhs=rhs, start=start, stop=stop), R, W)

    def tr(self, out, in_, ident, R, W):
        return self.op("pe", lambda E: E.transpose(out=out, in_=in_, identity=ident), R, W)

    def ms(self, e, ap, val, W):
        return self.op(e, lambda E: E.memset(ap, val), (), W)


def _consts():
    bf = ml_dtypes.bfloat16
    c = {}
    c["ident"] = np.eye(128, dtype=np.float32).astype(bf)
    c["identF"] = np.eye(128, dtype=np.float32)
    kl = np.arange(128)[:, None]
    ql = np.arange(128)[None, :]
    c["tri_le"] = ((kl <= ql).astype(np.float32) * BIG - BIG).astype(bf)
    c["tri_gt"] = ((kl > ql).astype(np.float32) * BIG - BIG).astype(bf)
    d = np.arange(2816)[None, :] - kl - 384
    M = ((d >= 0) & (d <= 128)).astype(np.float32) + ((d >= 0) & (d % 4 == 0) & (d <= 512)) + ((d >= 0) & (d % 16 == 0))
    c["toep"] = M.astype(np.float32).astype(bf)
    cend = np.arange(127) * 16 + 31
    cm = np.zeros((128, S), np.float32)
    cm[:127] = (cend[:, None] <= np.arange(S)[None, :])
    c["cmaskT"] = cm.astype(bf)
    cs = np.arange(127) * 16
    ss = np.arange(32) * 64
    ov = np.clip(np.minimum(cs[:, None] + 32, ss[None, :] + 64) - np.maximum(cs[:, None], ss[None, :]), 0, None).astype(np.float32) / 32
    vx = np.zeros((128, 33), np.float32)
    vx[:127, 0] = 1.0
    vx[:127, 1:] = ov
    c["vcx_tail"] = vx.astype(bf)
    E = (np.arange(S)[None, :] // 64 == np.arange(32)[:, None]).astype(np.float32)
    c["emat"] = E.astype(bf)
    t = np.arange(S)
    jb = np.arange(32)[None, :]
    cur = (t // 64)[:, None]
    valid = ss[None, :] <= t[:, None]
    forced = valid & ((jb == 0) | (jb == cur) | (jb == cur - 1))
    add = np.where(valid, 1.0e4 * forced, -1.0e30).astype(np.float32)
    c["addtab"] = np.ascontiguousarray(add.reshape(NT, 128, 32).transpose(1, 0, 2))
    invA = (THETA ** (-np.arange(0, 16, 2) / 16)).astype(np.float32)
    invB = (THETA ** (-np.arange(0, 32, 2) / 32)).astype(np.float32)
    c["invA"] = np.ascontiguousarray(np.broadcast_to(invA, (128, 8)))
    c["invB"] = np.ascontiguousarray(np.broadcast_to(invB, (128, 16)))
    return c


_CONST_SPECS = {
    "ident": ([128, 128], BF16), "identF": ([128, 128], F32), "tri_le": ([128, 128], BF16), "tri_gt": ([128, 128], BF16),
    "toep": ([128, 2816], BF16), "cmaskT": ([128, S], BF16), "vcx_tail": ([128, 33], BF16),
    "emat": ([32, S], BF16), "addtab": ([128, NT, 32], F32), "invA": ([128, 8], F32), "invB": ([128, 16], F32),
}

_IN_SPECS = {
    "x": ([S, D], F32), "c_l": ([128, 16], F32), "pos_l": ([128, NT], I32), "posc": ([128, 1], I32),
    "w_ada": ([D, 3 * D], F32), "b_sh": ([1, D], F32), "b_sc": ([1, D], F32), "b_gate": ([1, D], F32),
    "g_pre_r": ([1, D], F32), "g_post": ([1, D], F32), "w_in": ([D, N_IN], F32),
    "pe_ckT": ([64, 32], F32), "pe_cvT": ([64, 32], F32), "w_ck1": ([2048, 256], F32), "w_ck2": ([256, 64], F32),
    "w_cv1": ([2048, 256], F32), "w_cv2": ([256, 64], F32), "w_br_a": ([1024, D], F32), "w_br_b": ([1024, D], F32),
    "w_out": ([D, D], F32),
}

_SCRATCH = {
    "QAT": ([1024, S], BF16), "KST": ([256, S], BF16), "KWT": ([256, S], BF16), "QBT": ([1024, S], BF16),
    "KBT": ([1024, S], BF16), "VS": ([S, 256], BF16), "VW": ([S, 256], BF16), "VB": ([S, 1024], BF16),
    "ZA": ([S, 1024], BF16), "ZB": ([S, 1024], BF16), "SMA": ([D, S], BF16), "SMB": ([D, S], BF16),
    "MT": ([D, S], BF16),
}


def build_nc(debug=None, stop_after=None, start_from=None):
    debug = debug or ()
    nc = bass.Bass("TRN2", target_bir_lowering=False)
    I = {n: nc.dram_tensor(n, sh, dt, kind="ExternalInput").ap() for n, (sh, dt) in {**_IN_SPECS, **_CONST_SPECS}.items()}
    y_out = nc.dram_tensor("y", [S, D], F32, kind="ExternalOutput").ap()
    SC = {}
    for n, (sh, dt) in _SCRATCH.items():
        kind = "ExternalOutput" if n in debug else "Internal"
        if start_from and n != "MT":
            kind = "ExternalInput"
        SC[n] = T(nc.dram_tensor(n, sh, dt, kind=kind).ap(), n)
    dbg_out = {}

    def dbg(name, shape, dt=F32):
        dbg_out[name] = T(nc.dram_tensor(name, shape, dt, kind="ExternalOutput").ap(), name)
        return dbg_out[name]

    with ExitStack() as st:
        k = K(nc, st)
        yT = T(y_out, "y")

        def const(name, q="sp"):
            sh, dt = _CONST_SPECS[name]
            t = k.sb("c_" + name, sh, dt)
            k.dma(q, t[:], I[name][:], t)
            return t

        ident = const("ident")
        identF = const("identF")
        tri_le = const("tri_le")
        tri_gt = const("tri_gt")
        toep = const("toep")
        cmaskT = const("cmaskT")
        addtab = const("addtab")
        invA = const("invA")
        invB = const("invB")

        PA = [k.ps(f"pa{i}", [128, 512], F32) for i in range(7)]
        PB = [k.ps(f"pb{i}", [128, 1024], BF16) for i in range(1)]
        pa_i = [0]
        pb_i = [0]

        def nextA(n=3):
            pa_i[0] = (pa_i[0] + 1) % n
            return PA[pa_i[0]]

        def nextB(n=1):
            pb_i[0] = (pb_i[0] + 1) % n
            return PB[pb_i[0]]

        G = k.sb("G", [128, NT, 48], F32)
        gg_bc = k.sb("gg_bc", [128, D], F32)
        cosA = k.sb("cosA", [128, NT, 8], F32)
        sinA = k.sb("sinA", [128, NT, 8], F32)
        cosB = k.sb("cosB", [128, NT, 16], F32)
        sinB = k.sb("sinB", [128, NT, 16], F32)
        cosC = k.sb("cosC", [128, 8], F32)
        sinC = k.sb("sinC", [128, 8], F32)
        kcmpT = k.sb("kcmpT", [64, 4, 128], BF16)
        vcx = k.sb("vcx", [128, 4, 97], BF16)
        k.ms("pool", kcmpT[:], 0.0, [kcmpT])
        k.ms("pool", vcx[:], 0.0, [vcx])
        eps_t = k.sb("eps_t", [128, 1], F32)
        k.ms("dve", eps_t[:], EPS, [eps_t])

        def sincos(pos_i32, ncol, inv, cos_t, sin_t, tagn, stk):
            nf = inv.h.shape[1]
            posf = k.sb("posf" + tagn, [128, ncol], F32, stk)
            k.cp("dve", posf[:], pos_i32[:], [pos_i32], [posf])
            ang = k.sb("ang" + tagn, [128, ncol, nf], F32, stk)
            k.tt("dve", ang[:], posf[:].rearrange("p (c o) -> p c o", o=1).to_broadcast([128, ncol, nf]),
                 inv[:].rearrange("p (o f) -> p o f", o=1).to_broadcast([128, ncol, nf]), ALU.mult, [posf, inv], [ang])
            for which, outt in ((0, sin_t), (1, cos_t)):
                a2 = k.sb(f"a2{tagn}{which}", [128, ncol, nf], F32, stk)
                if which == 1:
                    k.ts("dve", a2[:], ang[:], float(np.pi / 2), ALU.add, [ang], [a2])
                    src = a2
                else:
                    src = ang
                u = k.sb(f"u{tagn}{which}", [128, ncol, nf], F32, stk)
                k.ts("dve", u[:], src[:], float(1 / (2 * np.pi)), ALU.mult, [src], [u])
                ki = k.sb(f"ki{tagn}{which}", [128, ncol, nf], I32, stk)
                k.cp("dve", ki[:], u[:], [u], [ki])
                kf = k.sb(f"kf{tagn}{which}", [128, ncol, nf], F32, stk)
                k.cp("dve", kf[:], ki[:], [ki], [kf])
                rr = k.sb(f"rr{tagn}{which}", [128, ncol, nf], F32, stk)
                k.stt("dve", rr[:], kf[:], -float(2 * np.pi), src[:], ALU.mult, ALU.add, [kf, src], [rr])
                m = k.sb(f"m{tagn}{which}", [128, ncol, nf], F32, stk)
                k.ts("dve", m[:], rr[:], float(np.pi), ALU.is_gt, [rr], [m], s2=-float(2 * np.pi), op1=ALU.mult)
                k.tt("dve", rr[:], rr[:], m[:], ALU.add, [rr, m], [rr])
                k.ts("dve", m[:], rr[:], -float(np.pi), ALU.is_lt, [rr], [m], s2=float(2 * np.pi), op1=ALU.mult)
                k.tt("dve", rr[:], rr[:], m[:], ALU.add, [rr, m], [rr])
                k.ts("dve", rr[:], rr[:], 3.14159, ALU.min, [rr], [rr], s2=-3.14159, op1=ALU.max)
                k.act(outt[:] if len(outt.h.shape) == 3 else outt[:].rearrange("p (c f) -> p c f", c=1),
                      rr[:], AF.Sin, [rr], [outt])

        if start_from:
            dI = {n: nc.dram_tensor(n, sh, dt, kind="ExternalInput").ap() for n, (sh, dt) in
                  {"d_G": ([128, NT, 48], F32), "d_kcvT": ([8, 64, S], BF16), "d_gg": ([128, D], F32)}.items()}
            k.dma("sp", G[:], dI["d_G"][:, :, :], G)
            k.dma("sp", gg_bc[:], dI["d_gg"][:, :], gg_bc)
            sA = st.enter_context(ExitStack())
            kcvT = [k.sb(f"kcvT{m}", [64, S], BF16, sA) for m in range(8)]
            for m in range(8):
                k.dma("sp", kcvT[m][:], dI["d_kcvT"][m], kcvT[m])
            with ExitStack() as s0:
                posc_t = k.sb("posc_t", [128, 1], I32, s0)
                k.dma("sp", posc_t[:], I["posc"][:], posc_t)
                sincos(posc_t, 1, invA, cosC, sinC, "C", s0)
                k.barrier()
            return _tail(nc, k, st, I, SC, yT, dbg, dbg_out, debug, stop_after, locals())

        sA = st.enter_context(ExitStack())
        kcvT = [k.sb(f"kcvT{m}", [64, S], BF16, sA) for m in range(8)]
        sB = st.enter_context(ExitStack())
        hT_h = sB.enter_context(nc.sbuf_tensor("hT", [128, 16, S], BF16))
        hT = [T(hT_h, f"hT{tt}") for tt in range(NT)]
        sX = st.enter_context(ExitStack())
        gs_bc = k.sb("gs_bc", [128, D], F32, sX)
        sh_bc = k.sb("sh_bc", [128, D], F32, sX)

        with ExitStack() as s0:
            pos_t = k.sb("pos_t", [128, NT], I32, s0)
            k.dma("sp", pos_t[:], I["pos_l"][:], pos_t)
            posc_t = k.sb("posc_t", [128, 1], I32, s0)
            k.dma("sp", posc_t[:], I["posc"][:], posc_t)
            c32 = k.sb("c32", [128, 16], F32, s0)
            k.dma("sp", c32[:], I["c_l"][:], c32)
            c_bc = k.sb("c_bc", [128, 16, 128], BF16, s0)
            k.cp("dve", c_bc[:], c32[:].rearrange("p (k o) -> p k o", o=1).to_broadcast([128, 16, 128]), [c32], [c_bc])
            wa = [k.sb(f"wa{i}", [128, 16, 512], BF16, s0) for i in range(2)]
            r1 = [k.sb(f"r1_{i}", [128, 512], F32, s0) for i in range(2)]
            r2 = [k.sb(f"r2_{i}", [128, 512], F32, s0) for i in range(2)]
            wada_v = I["w_ada"].rearrange("(kc p) n -> p kc n", p=128)

            def load_wa(blk):
                k.dma("pool", wa[blk % 2][:], wada_v[:, :, blk * 512:(blk + 1) * 512], wa[blk % 2])
            load_wa(0)
            for blk in range(12):
                w = wa[blk % 2]
                if blk + 1 < 12:
                    load_wa(blk + 1)
                kind, cb = blk // 4, blk % 4
                cs = slice(cb * 512, (cb + 1) * 512)
                a1, a2 = r1[blk % 2], r2[blk % 2]
                k.dma("sp", a1[:], I[("b_sh", "b_sc", "b_gate")[kind]][0:1, cs].to_broadcast([128, 512]), a1)
                if kind >= 1:
                    k.dma("sp", a2[:], I[("g_pre_r", "g_post")[kind - 1]][0:1, cs].to_broadcast([128, 512]), a2)
                pg = nextA()
                for kc in range(16):
                    k.mm(pg[:], c_bc[:, kc, :], w[:, kc, :], kc == 0, kc == 15, [w, c_bc], [pg])
                if kind == 0:
                    k.tt("dve", sh_bc[:, cs], pg[:], a1[:], ALU.add, [pg, a1], [sh_bc])
                elif kind == 1:
                    k.tt("dve", gs_bc[:, cs], pg[:], a1[:], ALU.add, [pg, a1], [gs_bc])
                    k.stt("dve", gs_bc[:, cs], gs_bc[:, cs], 1.0, a2[:], ALU.add, ALU.mult, [gs_bc, a2], [gs_bc])
                else:
                    k.tt("dve", gg_bc[:, cs], pg[:], a1[:], ALU.add, [pg, a1], [gg_bc])
                    k.tt("dve", gg_bc[:, cs], gg_bc[:, cs], a2[:], ALU.mult, [gg_bc, a2], [gg_bc])
            sincos(pos_t, NT, invA, cosA, sinA, "A", s0)
            sincos(pos_t, NT, invB, cosB, sinB, "B", s0)
            sincos(posc_t, 1, invA, cosC, sinC, "C", s0)
            k.barrier()

        with ExitStack() as s1:
            xb = [k.sb(f"xb{i}", [128, D], F32, s1) for i in range(2)]
            xs = [k.sb(f"xs{i}", [128, D], F32, s1) for i in range(2)]
            xn = [k.sb(f"xn{i}", [128, D], BF16, s1) for i in range(2)]
            junk = k.sb("junk", [128, D], BF16, s1)
            ssq = k.sb("ssq", [128, NT], F32, s1)
            rstd = k.sb("rstd", [128, NT], F32, s1)
            k.ms("dve", ssq[:], 0.0, [ssq])
            for tt in range(NT):
                xt = xb[tt % 2]
                k.dma("sp", xt[:], I["x"][tt * 128:(tt + 1) * 128, :], xt)
                k.act(junk[:], xt[:], AF.Square, [xt, ssq], [junk, ssq], accum=ssq[:, tt:tt + 1])
                k.act(rstd[:, tt:tt + 1], ssq[:, tt:tt + 1], AF.Ln, [ssq, eps_t], [rstd], bias=eps_t[:, 0:1], scale=1.0 / D)
                k.act(rstd[:, tt:tt + 1], rstd[:, tt:tt + 1], AF.Exp, [rstd], [rstd], scale=-0.5)
                xst = xs[tt % 2]
                xnt = xn[tt % 2]
                k.stt("dve", xst[:], xt[:], rstd[:, tt:tt + 1], gs_bc[:], ALU.mult, ALU.mult, [xt, rstd, gs_bc], [xst])
                k.tt("pool", xnt[:], xst[:], sh_bc[:], ALU.add, [xst, sh_bc], [xnt])
                for half in range(2):
                    pb = nextB()
                    for j in range(8):
                        kc = half * 8 + j
                        k.tr(pb[:, j * 128:(j + 1) * 128], xnt[:, kc * 128:(kc + 1) * 128], ident[:], [xnt, ident], [pb])
                    k.cp("act" if half == 0 else "dve", hT_h[:, half * 8:(half + 1) * 8, tt * 128:(tt + 1) * 128],
                         pb[:, :].rearrange("p (j t) -> p j t", t=128), [pb], [hT[tt]])
            k.barrier()
        sX.close()

        if "hT" in debug:
            o = dbg("d_hT", [128, 16, S], BF16)
            k.dma("sp", o[:, :, :], hT_h[:], o, hT)

        if stop_after == "P1":
            return _finish(nc, k, yT, dbg_out, SC, debug)

        s2 = st.enter_context(ExitStack())
        wbf = [k.sb(f"wbf{i}", [128, 16, 512], BF16, s2) for i in range(3)]
        win_v = I["w_in"].rearrange("(kc p) n -> p kc n", p=128)
        blocks = []
        for i in range(2):
            blocks.append((C_QA + 512 * i, 512, "qa", i))
        blocks.append((C_KS, 512, "ksvs", 0))
        blocks.append((C_KW, 512, "kwvw", 0))
        blocks.append((C_GA, 48, "ga", 0))
        for i in range(2):
            blocks.append((C_ZA + 512 * i, 512, "za", i))
        for i in range(2):
            blocks.append((C_QB + 512 * i, 512, "qb", i))
        for i in range(2):
            blocks.append((C_KB + 512 * i, 512, "kb", i))
        for i in range(2):
            blocks.append((C_VB + 512 * i, 512, "vb", i))
        for i in range(2):
            blocks.append((C_ZB + 512 * i, 512, "zb", i))
        for i in range(4):
            blocks.append((C_MA + 512 * i, 512, "ma", i))
        for i in range(4):
            blocks.append((C_MB + 512 * i, 512, "mb", i))
        blocks.append((C_KC, 512, "kcvc", 0))
        if stop_after and stop_after.startswith("R:"):
            blocks = [b for b in blocks if b[2] in stop_after[2:].split("+")]
        if stop_after == "P2a":
            blocks = [b for b in blocks if b[2] in ("qa", "ksvs", "ga", "za")]
        if stop_after == "P2b":
            blocks = [b for b in blocks if b[2] in ("qb", "vb", "ma", "kcvc")]

        def load_w(bi):
            col0, ncols, _, _ = blocks[bi]
            w = wbf[bi % 3]
            k.dma("pool", w[:, :, 0:ncols], win_v[:, :, col0:col0 + ncols], w)

        tok16 = [k.sb(f"tok16_{i}", [128, 512], BF16, s2) for i in range(2)]
        rt = [k.sb(f"rt{i}", [128, 4, 128], F32, s2) for i in range(2)]
        stgT = [k.sb(f"stgT{i}", [128, 4, 512], BF16, s2) for i in range(2)]
        fm16 = [k.sb(f"fm16_{i}", [128, 512], BF16, s2) for i in range(2)]
        cnt = {"tok": 0, "stg": 0, "fm": 0}

        import os as _os
        _DBG = _os.environ.get("KDBG", "")

        def rope(ps, out16, tt, nh, dh, cos_t, sin_t):
            if "norope" in _DBG:
                return
            half = dh // 8
            pv = ps[:, 0:nh * dh].rearrange("p (h d) -> p h d", d=dh)
            ov = out16[:, 0:nh * dh].rearrange("p (h d) -> p h d", d=dh)
            r = rt[cnt["tok"] % 2]
            n = nh * half
            cb = cos_t[:, tt, :].rearrange("p (o f) -> p o f", o=1).to_broadcast([128, nh, half])
            sb_ = sin_t[:, tt, :].rearrange("p (o f) -> p o f", o=1).to_broadcast([128, nh, half])
            tv = [r[:, i, 0:n].rearrange("p (h f) -> p h f", f=half) for i in range(4)]
            x1 = pv[:, :, 0:half]
            x2 = pv[:, :, half:2 * half]
            lvl = 6
            for i_ in range(1, 7):
                if f"rope{i_}" in _DBG:
                    lvl = i_
            if lvl >= 1:
                k.tt("dve", tv[0], x1, cb, ALU.mult, [ps, cos_t], [r])
            if lvl >= 2:
                k.tt("dve", tv[1], x2, sb_, ALU.mult, [ps, sin_t], [r])
            if lvl >= 3:
                k.tt("dve", tv[2], x2, cb, ALU.mult, [ps, cos_t], [r])
            if lvl >= 4:
                k.tt("dve", tv[3], x1, sb_, ALU.mult, [ps, sin_t], [r])
            if lvl >= 5:
                k.tt("dve", ov[:, :, 0:half], tv[0], tv[1], ALU.subtract, [r], [out16])
            if lvl >= 6:
                k.tt("dve", ov[:, :, half:2 * half], tv[2], tv[3], ALU.add, [r], [out16])

        def proj_tok(w, ncols, tt):
            ps = nextA()
            for kc in range(16):
                k.mm(ps[:, 0:ncols], hT_h[:, kc, tt * 128:(tt + 1) * 128], w[:, kc, 0:ncols], kc == 0, kc == 15, [hT[tt], w], [ps])
            return ps

        def transposes_to_stage(src16, ncol128, tt, stg):
            pb = nextB()
            for j in range(ncol128):
                k.tr(pb[:, j * 128:(j + 1) * 128], src16[:, j * 128:(j + 1) * 128], ident[:], [src16, ident], [pb])
            q4 = tt % 4
            eng = "act" if tt % 2 == 0 else "dve"
            k.cp(eng, stg[:, 0:ncol128, q4 * 128:(q4 + 1) * 128],
                 pb[:, 0:ncol128 * 128].rearrange("p (j t) -> p j t", t=128), [pb], [stg])

        load_w(0)
        if len(blocks) > 1:
            load_w(1)
        for bi, (col0, ncols, role, idx) in enumerate(blocks):
            if bi + 2 < len(blocks):
                load_w(bi + 2)
            w = wbf[bi % 3]
            if role in ("qa", "qb", "kb"):
                nh, dh, cos_t, sin_t = (8, 64, cosA, sinA) if role == "qa" else (4, 128, cosB, sinB)
                dstT = SC["QAT"] if role == "qa" else (SC["QBT"] if role == "qb" else SC["KBT"])
                deferred = []
                for tt in range(NT):
                    ps = proj_tok(w, 512, tt)
                    for f_ in deferred:
                        f_()
                    deferred = []
                    o16 = tok16[cnt["tok"] % 2]
                    k.cp("act", o16[:], ps[:], [ps], [o16])
                    rope(ps, o16, tt, nh, dh, cos_t, sin_t)
                    cnt["tok"] += 1

                    def fin(o16=o16, tt=tt):
                        stg = stgT[cnt["stg"] % 2]
                        transposes_to_stage(o16, 4, tt, stg)
                        if tt % 4 == 3:
                            tc_ = tt // 4
                            k.dma("sp", dstT.h[idx * 512:(idx + 1) * 512, tc_ * 512:(tc_ + 1) * 512].rearrange("(j p) t -> p j t", p=128),
                                  stg[:], dstT, [stg])
                            cnt["stg"] += 1
                    deferred.append(fin)
                for f_ in deferred:
                    f_()
            elif role in ("ksvs", "kwvw"):
                dstK = SC["KST"] if role == "ksvs" else SC["KWT"]
                dstV = SC["VS"] if role == "ksvs" else SC["VW"]
                deferred = []
                for tt in range(NT):
                    ps = proj_tok(w, 512, tt)
                    for f_ in deferred:
                        f_()
                    deferred = []
                    o16 = tok16[cnt["tok"] % 2]
                    k.cp("act", o16[:], ps[:], [ps], [o16])
                    rope(ps, o16, tt, 4, 64, cosA, sinA)
                    cnt["tok"] += 1

                    def fin(o16=o16, tt=tt):
                        stg = stgT[cnt["stg"] % 2]
                        transposes_to_stage(o16, 2, tt, stg)
                        k.dma("sp", dstV.h[tt * 128:(tt + 1) * 128, :], o16[:, 256:512], dstV, [o16])
                        if tt % 4 == 3:
                            tc_ = tt // 4
                            k.dma("sp", dstK.h[:, tc_ * 512:(tc_ + 1) * 512].rearrange("(j p) t -> p j t", p=128),
                                  stg[:, 0:2, :], dstK, [stg])
                            cnt["stg"] += 1
                    deferred.append(fin)
                for f_ in deferred:
                    f_()
            elif role == "ga":
                for tt in range(NT):
                    ps = proj_tok(w, 48, tt)
                    k.act(G[:, tt, :], ps[:, 0:48], AF.Sigmoid, [ps], [G])
            elif role in ("za", "zb", "vb"):
                dst = SC["ZA"] if role == "za" else (SC["ZB"] if role == "zb" else SC["VB"])
                for tt in range(NT):
                    ps = proj_tok(w, 512, tt)
                    o16 = tok16[cnt["tok"] % 2]
                    if role == "vb":
                        k.cp("act", o16[:], ps[:], [ps], [o16])
                    else:
                        k.act(o16[:], ps[:], AF.Silu, [ps], [o16])
                    cnt["tok"] += 1
                    k.dma("sp", dst.h[tt * 128:(tt + 1) * 128, idx * 512:(idx + 1) * 512], o16[:], dst, [o16])
            elif role in ("ma", "mb"):
                dst = SC["SMA"] if role == "ma" else SC["SMB"]
                for j in range(4):
                    for tc_ in range(4):
                        ps = nextA()
                        for kc in range(16):
                            k.mm(ps[:], w[:, kc, j * 128:(j + 1) * 128], hT_h[:, kc, tc_ * 512:(tc_ + 1) * 512], kc == 0, kc == 15,
                                 [hT[4 * tc_ + i] for i in range(4)] + [w], [ps])
                        o16 = fm16[cnt["fm"] % 2]
                        k.act(o16[:], ps[:], AF.Sigmoid, [ps], [o16])
                        cnt["fm"] += 1
                        r0 = idx * 512 + j * 128
                        k.dma("sp", dst.h[r0:r0 + 128, tc_ * 512:(tc_ + 1) * 512], o16[:], dst, [o16])
            elif role == "kcvc":
                for m in range(8):
                    for tc_ in range(4):
                        ps = nextA()
                        for kc in range(16):
                            k.mm(ps[0:64, :], w[:, kc, m * 64:(m + 1) * 64], hT_h[:, kc, tc_ * 512:(tc_ + 1) * 512], kc == 0, kc == 15,
                                 [hT[4 * tc_ + i] for i in range(4)] + [w], [ps])
                        k.cp("act" if (m + tc_) % 2 == 0 else "dve", kcvT[m][:, tc_ * 512:(tc_ + 1) * 512], ps[0:64, :], [ps], [kcvT[m]])
        k.barrier()
        s2.close()
        sB.close()

        if "kcvT" in debug:
            o = dbg("d_kcvT", [8, 64, S], BF16)
            for m in range(8):
                k.dma("sp", o.h[m], kcvT[m][:], o, [kcvT[m]])
        if "G" in debug:
            o = dbg("d_G", [128, NT, 48])
            k.dma("sp", o[:, :, :], G[:], o, [G])

        if stop_after in ("P2", "P2a", "P2b") or (stop_after and stop_after.startswith("R:")):
            sA.close()
            return _finish(nc, k, yT, dbg_out, SC, debug)
        return _tail(nc, k, st, I, SC, yT, dbg, dbg_out, debug, stop_after, locals())


def _finish(nc, k, yT, dbg_out, SC, debug):
    deps = list(yT.w.values())
    for o in dbg_out.values():
        deps.extend(o.w.values())
    for n in debug:
        if n in SC:
            deps.extend(SC[n].w.values())
    k._wait("sp", deps)
    return nc


def _core_inputs(b, x, c, positions, w_ada, b_ada, g_pre, g_post, w_in, pe_ck, pe_cv, w_ck1, w_ck2,
                 w_cv1, w_cv2, w_br_a, w_br_b, w_out, consts):
    f = np.ascontiguousarray
    lay = lambda v: f(np.asarray(v, np.float32).reshape(16, 128).T)
    pos = np.asarray(positions[b], np.int32)
    posc = np.zeros((128, 1), np.int32)
    posc[:127, 0] = pos[31::16][:127]
    m = {
        "x": f(x[b]), "c_l": lay(c[b]), "pos_l": f(pos.reshape(NT, 128).T), "posc": posc,
        "w_ada": f(w_ada[0]), "b_sh": f(b_ada[0, 0:D].reshape(1, D)), "b_sc": f(b_ada[0, D:2 * D].reshape(1, D)),
        "b_gate": f(b_ada[0, 2 * D:3 * D].reshape(1, D)), "g_pre_r": f(g_pre[0].reshape(1, D)), "g_post": f(g_post[0].reshape(1, D)),
        "w_in": f(w_in[0]), "pe_ckT": f(pe_ck[0].T), "pe_cvT": f(pe_cv[0].T), "w_ck1": f(w_ck1[0]), "w_ck2": f(w_ck2[0]),
        "w_cv1": f(w_cv1[0]), "w_cv2": f(w_cv2[0]), "w_br_a": f(w_br_a[0]), "w_br_b": f(w_br_b[0]), "w_out": f(w_out[0]),
    }
    m.update(consts)
    return m


def kernel(**inputs):
    inputs = {k_: np.asarray(v) for k_, v in inputs.items()}
    consts = _consts()
    nc = build_nc()
    in_maps = [_core_inputs(b, consts=consts, **inputs) for b in range(8)]
    res = run_bass_kernel_spmd(nc, in_maps, core_ids=list(range(8)))
    return np.stack([np.asarray(r["y"], np.float32) for r in res.results], axis=0)


def _tail(nc, k, st, I, SC, yT, dbg, dbg_out, debug, stop_after, L):
    ident, tri_le, tri_gt, toep, cmaskT, addtab = (L[n] for n in ("ident", "tri_le", "tri_gt", "toep", "cmaskT", "addtab"))
    identF = L["identF"]
    G, gg_bc, cosC, sinC, kcmpT, vcx, kcvT, sA, PA, nextA, nextB, eps_t = (
        L[n] for n in ("G", "gg_bc", "cosC", "sinC", "kcmpT", "vcx", "kcvT", "sA", "PA", "nextA", "nextB", "eps_t"))
    y_out = yT.h
    X = mybir.AxisListType.X

    def bc(ap2, n):
        q = ap2.shape[1]
        return ap2.rearrange("p (q o) -> p q o", o=1).to_broadcast([128, q, n])

    def v3(ap2):
        return ap2.rearrange("p (q o) -> p q o", o=1)

    with ExitStack() as sC:
        wc1 = [k.sb(f"wc1_{i}", [64, 32, 256], BF16, sC) for i in range(2)]
        wc2 = [k.sb(f"wc2_{i}", [128, 2, 64], BF16, sC) for i in range(2)]
        peT = [k.sb(f"peT{i}", [64, 32], BF16, sC) for i in range(2)]
        for i, (n1, n2, npe) in enumerate((("w_ck1", "w_ck2", "pe_ckT"), ("w_cv1", "w_cv2", "pe_cvT"))):
            k.dma("pool", wc1[i][:], I[n1].rearrange("(l d) h -> d l h", d=64), wc1[i])
            k.dma("pool", wc2[i][:], I[n2].rearrange("(hh p) n -> p hh n", p=128), wc2[i])
            k.dma("pool", peT[i][:], I[npe][:, :], peT[i])
        for g in range(4):
            k.dma("sp", vcx[:, g, 64:97], I["vcx_tail"][:, :], vcx)
        cbias = k.sb("cbias", [128, 4], F32, sC)
        xh = k.sb("xh", [128, 127], F32, sC)
        x2 = k.sb("x2", [128, 127], F32, sC)
        sg = k.sb("sg", [128, 127], F32, sC)
        hcT = k.sb("hcT", [128, 2, 128], BF16, sC)
        kc16 = k.sb("kc16", [128, 64], BF16, sC)
        rtc = k.sb("rtc", [128, 4, 8], F32, sC)
        psb = PA[3]
        for kv in range(2):
            for half in range(2):
                col = kv * 2 + half
                for l in range(32):
                    k.mm(psb[:, col:col + 1], wc1[kv][:, l, half * 128:(half + 1) * 128], peT[kv][:, l:l + 1], l == 0, l == 31,
                         [wc1[kv], peT[kv]], [psb])
        k.cp("dve", cbias[:], psb[:, 0:4], [psb], [cbias])
        for kv in range(2):
            for g in range(4):
                src = kcvT[kv * 4 + g]
                for half in range(2):
                    ps = nextA()
                    for l in range(32):
                        k.mm(ps[:, 0:127], wc1[kv][:, l, half * 128:(half + 1) * 128], src[:, l:l + 2017:16], l == 0, l == 31,
                             [wc1[kv], src], [ps])
                    cc = kv * 2 + half
                    k.act(xh[:], ps[:, 0:127], AF.Identity, [ps, cbias], [xh], bias=cbias[:, cc:cc + 1], scale=1.0)
                    k.tt("dve", x2[:], xh[:], xh[:], ALU.mult, [xh], [x2])
                    k.ts("dve", x2[:], x2[:], 0.044715, ALU.mult, [x2], [x2], s2=1.0, op1=ALU.add)
                    k.tt("dve", x2[:], x2[:], xh[:], ALU.mult, [x2, xh], [x2])
                    k.act(sg[:], x2[:], AF.Sigmoid, [x2], [sg], scale=1.5957691216)
                    k.tt("dve", hcT[:, half, 0:127], xh[:], sg[:], ALU.mult, [xh, sg], [hcT])
                ps2 = nextA()
                for half in range(2):
                    k.mm(ps2[0:127, 0:64], hcT[:, half, 0:127], wc2[kv][:, half, :], half == 0, half == 1, [hcT, wc2[kv]], [ps2])
                if kv == 0:
                    k.cp("act", kc16[0:127, :], ps2[0:127, 0:64], [ps2], [kc16])
                    x1 = ps2[0:127, 0:8]
                    x2_ = ps2[0:127, 8:16]
                    k.tt("dve", rtc[0:127, 0, :], x1, cosC[0:127, :], ALU.mult, [ps2, cosC], [rtc])
                    k.tt("dve", rtc[0:127, 1, :], x2_, sinC[0:127, :], ALU.mult, [ps2, sinC], [rtc])
                    k.tt("dve", rtc[0:127, 2, :], x2_, cosC[0:127, :], ALU.mult, [ps2, cosC], [rtc])
                    k.tt("dve", rtc[0:127, 3, :], x1, sinC[0:127, :], ALU.mult, [ps2, sinC], [rtc])
                    k.tt("dve", kc16[0:127, 0:8], rtc[0:127, 0, :], rtc[0:127, 1, :], ALU.subtract, [rtc], [kc16])
                    k.tt("dve", kc16[0:127, 8:16], rtc[0:127, 2, :], rtc[0:127, 3, :], ALU.add, [rtc], [kc16])
                    pb = nextB()
                    k.tr(pb[0:64, 0:127], kc16[0:127, :], ident[0:127, 0:127], [kc16, ident], [pb])
                    k.cp("dve", kcmpT[:, g, 0:127], pb[0:64, 0:127], [pb], [kcmpT])
                else:
                    k.cp("act", vcx[0:127, g, 0:64], ps2[0:127, 0:64], [ps2], [vcx])
        k.barrier()
    sA.close()

    if "cmp" in debug:
        o = dbg("d_kcmpT", [64, 4, 128], BF16)
        k.dma("sp", o[:, :, :], kcmpT[:], o, [kcmpT])
        o = dbg("d_vcx", [128, 4, 97], BF16)
        k.dma("sp", o[:, :, :], vcx[:], o, [vcx])
    if stop_after == "P3":
        return _finish(nc, k, yT, dbg_out, SC, debug)

    gatedAT = k.sb("gatedAT", [128, 8, S], BF16)
    gatedBT = k.sb("gatedBT", [128, 8, S], BF16)

    with ExitStack() as s4:
        ksT = [k.sb(f"ksT{i}", [96, S], BF16, s4) for i in range(2)]
        kwT = [k.sb(f"kwT{i}", [64, S], BF16, s4) for i in range(2)]
        vsg = [k.sb(f"vsg{i}", [128, 16, 65], BF16, s4) for i in range(2)]
        vwg = [k.sb(f"vwg{i}", [128, 16, 65], BF16, s4) for i in range(2)]
        qTh = [[s4.enter_context(nc.sbuf_tensor(f"qT{i}_{r}", [96, S], BF16)) for r in range(4)] for i in range(2)]
        qTq = [[T(qTh[i][r], f"qTq{i}_{r}") for r in range(4)] for i in range(2)]
        qTm = [[[T(qTh[i][r], f"qTm{i}_{r}_{qc}") for qc in range(4)] for r in range(4)] for i in range(2)]
        for i in range(2):
            k.dma("sp", ksT[i][64:96, :], I["emat"][:, :], ksT[i])
            k.ms("pool", vsg[i][:, :, 64:65], 1.0, [vsg[i]])
            k.ms("pool", vwg[i][:, :, 64:65], 1.0, [vwg[i]])
        zag = [k.sb(f"zag{i}", [128, 4, 256], BF16, s4) for i in range(2)]
        Pcs = [k.sb(f"Pc{i}", [128, 512], BF16, s4) for i in range(2)]
        Pt = [k.sb(f"Pt{i}", [128, 512], BF16, s4) for i in range(4)]
        oacc_h = s4.enter_context(nc.sbuf_tensor("oacc", [128, 4, 4, 64], F32))
        oacc = [T(oacc_h, f"oacc{r}") for r in range(4)]
        imp = k.sb("imp", [128, 4, 32], F32, s4)
        tmpi = k.sb("tmpi", [128, 4, 32], F32, s4)
        tmpo = [k.sb(f"tmpo{i}", [128, 4, 64], F32, s4) for i in range(2)]
        rden = k.sb("rden", [128, 4], F32, s4)
        fsc = k.sb("fsc", [128, 4], F32, s4)
        sc = k.sb("sc", [128, 32], F32, s4)
        sc2 = k.sb("sc2", [128, 32], F32, s4)
        m8 = k.sb("m8", [128, 8], F32, s4)
        m8b = k.sb("m8b", [128, 8], F32, s4)
        msk = k.sb("msk", [128, 32], F32, s4)
        mb16 = k.sb("mb16", [128, 4, 32], BF16, s4)
        gtok = [k.sb(f"gtok{i}", [128, 4, 256], BF16, s4) for i in range(2)]
        gAT = T(gatedAT.h, "gatedAT")
        cn = {"pt": 0, "to": 0, "acc": 0, "first": True, "pc": 0, "ot": 0}
        oTsb = [k.sb(f"oTs{i}", [65, 512], F32, s4) for i in range(2)]

        def load_group(g):
            i = g % 2
            k.dma("sp", ksT[i][0:64, :], SC["KST"].h[g * 64:(g + 1) * 64, :], ksT[i], [SC["KST"]])
            k.dma("sp", kwT[i][:, :], SC["KWT"].h[g * 64:(g + 1) * 64, :], kwT[i], [SC["KWT"]])
            k.dma("sp", vsg[i][:, :, 0:64], SC["VS"].h[:, g * 64:(g + 1) * 64].rearrange("(kt p) d -> p kt d", p=128), vsg[i], [SC["VS"]])
            k.dma("sp", vwg[i][:, :, 0:64], SC["VW"].h[:, g * 64:(g + 1) * 64].rearrange("(kt p) d -> p kt d", p=128), vwg[i], [SC["VW"]])
            for r in range(4):
                h = 4 * g + r
                k.dma("sp", qTh[i][r][0:64, :], SC["QAT"].h[h * 64:(h + 1) * 64, :], qTq[i][r], [SC["QAT"]])

        def evac_branch(acc_bank, width, r, h, qc, br, first_write):
            av = acc_bank[:, 0:4 * width].rearrange("p (q c) -> p q c", c=width)
            if br == 0:
                k.ts("dve", v3(rden[:]), av[:, :, 64:65], 1e-30, ALU.max, [acc_bank], [rden])
                k.op("dve", lambda E: E.reciprocal(out=rden[:], in_=rden[:]), [rden], [rden])
            else:
                k.op("dve", lambda E: E.reciprocal(out=v3(rden[:]), in_=av[:, :, 64:65]), [acc_bank], [rden])
            k.tt("dve", fsc[:], rden[:], G[:, 4 * qc:4 * qc + 4, 3 * h + br], ALU.mult, [rden, G], [fsc])
            if first_write:
                k.tt("dve", oacc_h[:, r, :, :], av[:, :, 0:64], bc(fsc[:], 64), ALU.mult, [acc_bank, fsc], [oacc[r]])
            else:
                tm = tmpo[cn["to"] % 2]
                cn["to"] += 1
                k.tt("dve", tm[:], av[:, :, 0:64], bc(fsc[:], 64), ALU.mult, [acc_bank, fsc], [tm])
                k.tt("pool", oacc_h[:, r, :, :], oacc_h[:, r, :, :], tm[:], ALU.add, [oacc[r], tm], [oacc[r]])

        load_group(0)
        for g in range(4):
            i = g % 2
            if g + 1 < 4:
                load_group(g + 1)
            for qc in range(4):
                zi = (g * 4 + qc) % 2
                gi = zi
                qs = slice(qc * 512, (qc + 1) * 512)
                k.dma("sp", zag[zi][:], SC["ZA"].h[qc * 512:(qc + 1) * 512, g * 256:(g + 1) * 256].rearrange("(qi p) c -> p qi c", p=128),
                      zag[zi], [SC["ZA"]])
                use_mask = qc >= 2
                def hookA():
                    for qi in range(4):
                        tt_ = 4 * qc + qi
                        k.tt("dve", sc[:], imp[:, qi, :], addtab[:, tt_, :], ALU.add, [imp, addtab], [sc])
                        k.op("dve", lambda E: E.max(out=m8[:], in_=sc[:]), [sc], [m8])
                        k.op("dve", lambda E: E.match_replace(out=sc2[:], in_to_replace=m8[:], in_values=sc[:], imm_value=-3.0e38),
                             [sc, m8], [sc2])
                        k.op("dve", lambda E: E.max(out=m8b[:], in_=sc2[:]), [sc2], [m8b])
                        k.ts("dve", msk[:], sc[:], m8b[:, 7:8], ALU.is_ge, [sc, m8b], [msk])
                        k.ts("dve", mb16[:, qi, :], msk[:], 1.0, ALU.subtract, [msk], [mb16], s2=BIG, op1=ALU.mult)

                def hookB():
                    pbm = nextB()
                    for qi in range(4):
                        k.tr(pbm[0:32, qi * 128:(qi + 1) * 128], mb16[:, qi, :], ident[:], [mb16, ident], [pbm])
                    for r in range(4):
                        k.cp("dve" if r % 2 == 0 else "act", qTh[i][r][64:96, qs], pbm[0:32, 0:512], [pbm], [qTm[i][r][qc]])

                KK = 96 if use_mask else 64
                items = [("cmp", r, 0, True, True) for r in range(4)]
                for r in range(4):
                    kts = list(range(max(0, 4 * qc - 4), 4 * qc + 4))
                    for n_, kt in enumerate(kts):
                        items.append(("win", r, kt, n_ == 0, n_ == len(kts) - 1))
                for r in range(4):
                    for kt in range(4 * qc + 4):
                        items.append(("sel", r, kt, kt == 0, kt == 4 * qc + 3))

                def score(it):
                    br, r, kt, _, _ = it
                    if br == "sel" and r == 0 and kt == 0 and use_mask:
                        hookB()
                    ps_ = nextA()
                    if br == "cmp":
                        k.mm(ps_[0:127, :], kcmpT[:, g, 0:127], qTh[i][r][0:64, qs], True, True, [kcmpT, qTq[i][r]], [ps_])
                    elif br == "win":
                        k.mm(ps_[:], kwT[i][:, kt * 128:(kt + 1) * 128], qTh[i][r][0:64, qs], True, True, [kwT[i], qTq[i][r]], [ps_])
                    else:
                        dq = [qTq[i][r]] + ([qTm[i][r][qc]] if use_mask else [])
                        k.mm(ps_[:], ksT[i][0:KK, kt * 128:(kt + 1) * 128], qTh[i][r][0:KK, qs], True, True, [ksT[i]] + dq, [ps_])
                    if br != "cmp":
                        if kt >= 4 * qc:
                            qd = kt - 4 * qc
                            k.op("pe", lambda E: E.matmul(ps_[:, qd * 128:(qd + 1) * 128], lhsT=ident[:], rhs=tri_le[:], start=False, stop=True,
                                                          skip_group_check=True), [ident, tri_le], [ps_])
                        ql = kt + 4 - 4 * qc
                        if br == "win" and 0 <= ql <= 3:
                            k.op("pe", lambda E: E.matmul(ps_[:, ql * 128:(ql + 1) * 128], lhsT=ident[:], rhs=tri_gt[:], start=False, stop=True,
                                                          skip_group_check=True), [ident, tri_gt], [ps_])
                    return ps_

                def post(it, ps_):
                    br, r, kt, is_first, is_last = it
                    h = 4 * g + r
                    if br == "cmp":
                        acc = PA[6]
                        Pc = Pcs[cn["pc"] % 2]
                        cn["pc"] += 1
                        pov = acc[:, 0:388].rearrange("p (q c) -> p q c", c=97)
                        k.act(Pc[0:127, :], ps_[0:127, :], AF.Exp, [ps_], [Pc], scale=0.125)
                        k.tt("pool", Pc[0:127, :], Pc[0:127, :], cmaskT[0:127, qs], ALU.mult, [Pc, cmaskT], [Pc])
                        for qi in range(4):
                            k.mm(acc[:, qi * 97:(qi + 1) * 97], Pc[0:127, qi * 128:(qi + 1) * 128], vcx[0:127, g, :], True, True, [Pc, vcx], [acc])
                        evac_branch(acc, 97, r, h, qc, 0, True)
                        if use_mask:
                            if r == 0:
                                k.tt("dve", imp[:], pov[:, :, 65:97], bc(rden[:], 32), ALU.mult, [acc, rden], [imp])
                            else:
                                k.tt("dve", tmpi[:], pov[:, :, 65:97], bc(rden[:], 32), ALU.mult, [acc, rden], [tmpi])
                                k.tt("pool", imp[:], imp[:], tmpi[:], ALU.add, [imp, tmpi], [imp])
                            if r == 3:
                                hookA()
                        return
                    if is_first:
                        cn["acc"] += 1
                        cn["first"] = True
                    accT = PA[3 + cn["acc"] % 2]
                    j0 = max(0, kt - 4 * qc)
                    j1 = 3 if br == "sel" else min(3, kt + 4 - 4 * qc)
                    P = Pt[cn["pt"] % 4]
                    cn["pt"] += 1
                    c0, c1 = j0 * 128, (j1 + 1) * 128
                    k.act(P[:, c0:c1], ps_[:, c0:c1], AF.Exp, [ps_], [P], scale=0.125)
                    vt = vwg[i] if br == "win" else vsg[i]
                    fst = cn["first"]
                    k.op("pe", lambda E: E.matmul(accT[0:65, c0:c1], lhsT=vt[:, kt, :], rhs=P[:, c0:c1], start=fst, stop=is_last,
                                                  skip_group_check=True), [P, vt], [accT])
                    cn["first"] = False
                    if is_last:
                        oTs = oTsb[cn["ot"] % 2]
                        cn["ot"] += 1
                        k.cp("dve", oTs[0:65, :], accT[0:65, :], [accT], [oTs])
                        tok = PA[5]
                        for qi in range(4):
                            k.tr(tok[:, qi * 65:(qi + 1) * 65], oTs[0:65, qi * 128:(qi + 1) * 128], identF[0:65, 0:65], [oTs, identF], [tok])
                        evac_branch(tok, 65, r, h, qc, 2 if br == "win" else 1, False)
                        if br == "sel":
                            k.tt("dve", gtok[gi][:, :, r * 64:(r + 1) * 64], oacc_h[:, r, :, :], zag[zi][:, :, r * 64:(r + 1) * 64], ALU.mult,
                                 [oacc[r], zag[zi]], [gtok[gi]])

                LA = 2
                pend = [score(items[n_]) for n_ in range(min(LA, len(items)))]
                for n_, it in enumerate(items):
                    ps_ = pend.pop(0)
                    if n_ + LA < len(items):
                        pend.append(score(items[n_ + LA]))
                    post(it, ps_)
                pbt = nextB()
                for j in range(2):
                    for qi in range(4):
                        c0 = (j * 4 + qi) * 128
                        k.tr(pbt[:, c0:c0 + 128], gtok[gi][:, qi, j * 128:(j + 1) * 128], ident[:], [gtok[gi], ident], [pbt])
                k.cp("act", gatedAT.h[:, 2 * g:2 * g + 2, qs], pbt[:, :].rearrange("p (j q) -> p j q", j=2), [pbt], [gAT])
        k.barrier()

    if "gA" in debug:
        o = dbg("d_gAT", [128, 8, S], BF16)
        k.dma("sp", o[:, :, :], gatedAT.h[:], o, [gAT])
    if stop_after == "P4":
        return _finish(nc, k, yT, dbg_out, SC, debug)

    gBT = T(gatedBT.h, "gatedBT")
    with ExitStack() as s5:
        kbT = [k.sb(f"kbT{i}", [128, S], BF16, s5) for i in range(2)]
        qbT = [k.sb(f"qbT{i}", [128, S], BF16, s5) for i in range(2)]
        vbh = [k.sb(f"vbh{i}", [128, 16, 129], BF16, s5) for i in range(2)]
        for i in range(2):
            k.ms("pool", vbh[i][:, :, 128:129], 1.0, [vbh[i]])
        zbg = [k.sb(f"zbg{i}", [128, 4, 128], BF16, s5) for i in range(2)]
        Pt = [k.sb(f"PtB{i}", [128, 512], BF16, s5) for i in range(6)]
        gtb = [k.sb(f"gtb{i}", [128, 4, 128], BF16, s5) for i in range(2)]
        tmpb = [k.sb(f"tmpb{i}", [128, 2, 128], F32, s5) for i in range(2)]
        rdb = [k.sb(f"rdb{i}", [128, 2], F32, s5) for i in range(2)]
        cn = {"pt": 0, "tb": 0, "first": [True, True]}
        SCB = 128.0 ** -0.5

        def load_head(h):
            i = h % 2
            k.dma("sp", kbT[i][:, :], SC["KBT"].h[h * 128:(h + 1) * 128, :], kbT[i], [SC["KBT"]])
            k.dma("sp", qbT[i][:, :], SC["QBT"].h[h * 128:(h + 1) * 128, :], qbT[i], [SC["QBT"]])
            k.dma("sp", vbh[i][:, :, 0:128], SC["VB"].h[:, h * 128:(h + 1) * 128].rearrange("(kt p) d -> p kt d", p=128), vbh[i], [SC["VB"]])

        load_head(0)
        for h in range(8):
            i = h % 2
            if h + 1 < 8:
                load_head(h + 1)
            items = []
            for qc in range(4):
                for kt in range(4 * qc + 4):
                    items.append((qc, kt))

            def score(it):
                qc, kt = it
                ps_ = nextA()
                k.mm(ps_[:], kbT[i][:, kt * 128:(kt + 1) * 128], qbT[i][:, qc * 512:(qc + 1) * 512], True, True, [kbT[i], qbT[i]], [ps_])
                return ps_

            def post(it, ps_):
                qc, kt = it
                zi = (h * 4 + qc) % 2
                qs = slice(qc * 512, (qc + 1) * 512)
                banks = [PA[3 + 2 * (qc % 2)], PA[4 + 2 * (qc % 2)]]
                if kt == 0:
                    k.dma("sp", zbg[zi][:], SC["ZB"].h[qc * 512:(qc + 1) * 512, h * 128:(h + 1) * 128].rearrange("(qi p) c -> p qi c", p=128),
                          zbg[zi], [SC["ZB"]])
                    cn["first"] = [True, True]
                first = cn["first"]
                j0 = max(0, kt - 4 * qc)
                P = Pt[cn["pt"] % 6]
                cn["pt"] += 1
                k.act(P[:, j0 * 128:512], ps_[:, j0 * 128:512], AF.Exp, [ps_], [P], scale=SCB)
                off = 512 * qc - 128 * kt + 384 + j0 * 128
                k.tt("dve" if cn["pt"] % 2 == 0 else "pool", P[:, j0 * 128:512], P[:, j0 * 128:512], toep[:, off:off + (4 - j0) * 128], ALU.mult,
                     [P, toep], [P])
                for qi in range(j0, 4):
                    b_ = qi // 2
                    c0 = (qi % 2) * 129
                    fst = first[b_]
                    k.op("pe", lambda E: E.matmul(banks[b_][:, c0:c0 + 129], lhsT=P[:, qi * 128:(qi + 1) * 128], rhs=vbh[i][:, kt, :],
                                                  start=fst, stop=(kt == 4 * qc + qi), skip_group_check=True), [P, vbh[i]], [banks[b_]])
                    first[b_] = False
                if kt == 4 * qc + 3:
                    for b_ in range(2):
                        bv = banks[b_][:, 0:258].rearrange("p (q c) -> p q c", c=129)
                        rd = rdb[cn["tb"] % 2]
                        tb = tmpb[cn["tb"] % 2]
                        cn["tb"] += 1
                        k.op("dve", lambda E: E.reciprocal(out=v3(rd[:]), in_=bv[:, :, 128:129]), [banks[b_]], [rd])
                        k.tt("dve", tb[:], bv[:, :, 0:128], bc(rd[:], 128), ALU.mult, [banks[b_], rd], [tb])
                        k.tt("pool", gtb[zi][:, 2 * b_:2 * b_ + 2, :], tb[:], zbg[zi][:, 2 * b_:2 * b_ + 2, :], ALU.mult, [tb, zbg[zi]], [gtb[zi]])
                    pbt = nextB()
                    for qi in range(4):
                        k.tr(pbt[:, qi * 128:(qi + 1) * 128], gtb[zi][:, qi, :], ident[:], [gtb[zi], ident], [pbt])
                    k.cp("act", gatedBT.h[:, h, qs], pbt[:, 0:512], [pbt], [gBT])

            LA = 2
            pend = [score(items[n_]) for n_ in range(min(LA, len(items)))]
            for n_, it in enumerate(items):
                ps_ = pend.pop(0)
                if n_ + LA < len(items):
                    pend.append(score(items[n_ + LA]))
                post(it, ps_)
        k.barrier()

    if "gB" in debug:
        o = dbg("d_gBT", [128, 8, S], BF16)
        k.dma("sp", o[:, :, :], gatedBT.h[:], o, [gBT])
    if stop_after == "P5":
        return _finish(nc, k, yT, dbg_out, SC, debug)

    gAT = T(gatedAT.h, "gatedAT2")
    with ExitStack() as s6o:
        wo = k.sb("wo", [128, 16, D], BF16, s6o)
        wov = I["w_out"].rearrange("(kc p) n -> p kc n", p=128)
        for cb in range(4):
            k.dma("pool", wo[:, :, cb * 512:(cb + 1) * 512], wov[:, :, cb * 512:(cb + 1) * 512], wo)
        with ExitStack() as s6:
            wj = [[k.sb(f"wj{a}{i}", [128, 8, 128], BF16, s6) for i in range(2)] for a in range(2)]
            sm = [[k.sb(f"sm{a}{i}", [128, 512], BF16, s6) for i in range(2)] for a in range(2)]
            m1 = [k.sb(f"m1_{i}", [128, 512], F32, s6) for i in range(2)]
            m2 = [k.sb(f"m2_{i}", [128, 512], F32, s6) for i in range(2)]
            m16 = [k.sb(f"m16_{i}", [128, 512], BF16, s6) for i in range(2)]
            wv = [I["w_br_a"].rearrange("(wc p) n -> p wc n", p=128), I["w_br_b"].rearrange("(wc p) n -> p wc n", p=128)]
            gsrc = [(gatedAT.h, gAT), (gatedBT.h, gBT)]
            smn = ["SMA", "SMB"]
            def load_wj(j):
                for a in range(2):
                    k.dma("pool", wj[a][j % 2][:], wv[a][:, :, j * 128:(j + 1) * 128], wj[a][j % 2])
            load_wj(0)
            for j in range(16):
                i = j % 2
                if j + 1 < 16:
                    load_wj(j + 1)
                for tc_ in range(4):
                    ii = (j * 4 + tc_) % 2
                    tcs = slice(tc_ * 512, (tc_ + 1) * 512)
                    pss = []
                    for a in range(2):
                        k.dma("sp", sm[a][ii][:], SC[smn[a]].h[j * 128:(j + 1) * 128, tcs], sm[a][ii], [SC[smn[a]]])
                        ps = nextA()
                        for wc in range(8):
                            k.mm(ps[:], wj[a][i][:, wc, :], gsrc[a][0][:, wc, tcs], wc == 0, wc == 7, [wj[a][i], gsrc[a][1]], [ps])
                        pss.append(ps)
                    k.tt("dve", m1[ii][:], pss[0][:], sm[0][ii][:], ALU.mult, [pss[0], sm[0][ii]], [m1[ii]])
                    k.tt("dve", m2[ii][:], pss[1][:], sm[1][ii][:], ALU.mult, [pss[1], sm[1][ii]], [m2[ii]])
                    k.tt("pool", m16[ii][:], m1[ii][:], m2[ii][:], ALU.add, [m1[ii], m2[ii]], [m16[ii]])
                    k.dma("act", SC["MT"].h[j * 128:(j + 1) * 128, tcs], m16[ii][:], SC["MT"], [m16[ii]])
            k.barrier()
        with ExitStack() as s7:
            mt = [k.sb(f"mt{i}", [128, 16, 128], BF16, s7) for i in range(2)]
            xt = [k.sb(f"xt{i}", [128, D], F32, s7) for i in range(2)]
            yt = [k.sb(f"yt{i}", [128, D], F32, s7) for i in range(2)]
            ssq4 = k.sb("ssq4", [128, 4], F32, s7)
            junk = k.sb("junk6", [128, 512], BF16, s7)
            rs = k.sb("rs6", [128, 1], F32, s7)
            def load6(tt):
                i = tt % 2
                k.dma("sp", mt[i][:], SC["MT"].h[:, tt * 128:(tt + 1) * 128].rearrange("(kc p) t -> p kc t", p=128), mt[i], [SC["MT"]])
                k.dma("sp", xt[i][:], I["x"][tt * 128:(tt + 1) * 128, :], xt[i])
            load6(0)
            for tt in range(NT):
                i = tt % 2
                if tt + 1 < NT:
                    load6(tt + 1)
                k.ms("dve", ssq4[:], 0.0, [ssq4])
                for cb in range(4):
                    ps = PA[cb]
                    for kc in range(16):
                        k.mm(ps[:], mt[i][:, kc, :], wo[:, kc, cb * 512:(cb + 1) * 512], kc == 0, kc == 15, [mt[i], wo], [ps])
                    k.act(junk[:], ps[:], AF.Square, [ps, ssq4], [junk, ssq4], accum=ssq4[:, cb:cb + 1])
                k.op("dve", lambda E: E.reduce_sum(out=rs[:], in_=ssq4[:], axis=X), [ssq4], [rs])
                k.act(rs[:], rs[:], AF.Ln, [rs, eps_t], [rs], bias=eps_t[:, 0:1], scale=1.0 / D)
                k.act(rs[:], rs[:], AF.Exp, [rs], [rs], scale=-0.5)
                for cb in range(4):
                    cs = slice(cb * 512, (cb + 1) * 512)
                    k.stt("dve", yt[i][:, cs], PA[cb][:], rs[:, 0:1], gg_bc[:, cs], ALU.mult, ALU.mult, [PA[cb], rs, gg_bc], [yt[i]])
                k.tt("pool", yt[i][:], yt[i][:], xt[i][:], ALU.add, [yt[i], xt[i]], [yt[i]])
                k.dma("sp", y_out[tt * 128:(tt + 1) * 128, :], yt[i][:], yT, [yt[i]])
            k.barrier()
    return _finish(nc, k, yT, dbg_out, SC, debug)
```

```python
import numpy as np
from contextlib import ExitStack
import ml_dtypes
import concourse.bass as bass
import concourse.mybir as mybir
from concourse.bass_utils import run_bass_kernel_spmd

F32 = mybir.dt.float32
BF16 = mybir.dt.bfloat16
I32 = mybir.dt.int32
AF = mybir.ActivationFunctionType
ALU = mybir.AluOpType

D = 2048
S = 2048
NT = 16
N_IN = 11824
EPS = 1e-6
BIG = 30000.0
THETA = 500000.0
C_QA, C_KC, C_VC, C_KS, C_VS, C_KW, C_VW, C_GA, C_ZA = 0, 1024, 1280, 1536, 1792, 2048, 2304, 2560, 2608
C_QB, C_KB, C_VB, C_ZB, C_MA, C_MB = 3632, 4656, 5680, 6704, 7728, 9776
NC_CMP = 127


class T:
    __slots__ = ("h", "name", "w", "r", "dsem", "dcnt", "excl")

    def __init__(self, h, name="", excl=False):
        self.h = h
        self.name = name
        self.excl = excl
        self.w = {}
        self.r = {}
        self.dsem = None
        self.dcnt = 0

    def __getitem__(self, idx):
        return self.h[idx]


class K:
    ROT = 20000

    def __init__(self, nc, stack):
        self.nc = nc
        self.stack = stack
        self.eng = {"pe": nc.tensor, "act": nc.scalar, "dve": nc.vector, "pool": nc.gpsimd, "sp": nc.sync}
        self.sem = {}
        self.cnt = {}
        self.waited = {e: {} for e in self.eng}
        self.nsem = 0
        self.all_dma = {}
        for e in self.eng:
            self.sem[e] = self.new_sem("e_" + e)
            self.cnt[e] = 0
        self.ninst = {e: 0 for e in self.eng}

    def new_sem(self, name):
        self.nsem += 1
        return self.stack.enter_context(self.nc.semaphore(f"{name}_{self.nsem}"))

    def sb(self, name, shape, dt, stack=None):
        return T((stack or self.stack).enter_context(self.nc.sbuf_tensor(name, list(shape), dt)), name)

    def ps(self, name, shape, dt=F32):
        return T(self.stack.enter_context(self.nc.psum_tensor(name, list(shape), dt)), name, excl=True)

    def alias(self, t, name=""):
        return T(t.h, name or t.name)

    def _wait(self, e, deps):
        need = {}
        for d in deps:
            s, v, src = d
            if e == "pe" and src == "pe":
                continue
            kk = id(s)
            if kk not in need or need[kk][1] < v:
                need[kk] = (s, v)
        for kk, (s, v) in need.items():
            if self.waited[e].get(kk, 0) < v:
                self.eng[e].wait_ge(s, v)
                self.waited[e][kk] = v

    @staticmethod
    def _deps(reads, writes):
        deps = []
        for t in reads:
            deps.extend(t.w.values())
            if t.excl:
                deps.extend(t.r.values())
        for t in writes:
            deps.extend(t.w.values())
            deps.extend(t.r.values())
        return deps

    def op(self, e, fn, reads=(), writes=()):
        self._wait(e, self._deps(reads, writes))
        inst = fn(self.eng[e])
        if self.cnt[e] >= self.ROT:
            self.sem[e] = self.new_sem("e_" + e)
            self.cnt[e] = 0
        self.cnt[e] += 1
        self.ninst[e] += 1
        inst.then_inc(self.sem[e], 1)
        d = (self.sem[e], self.cnt[e], e)
        for t in reads:
            t.r[id(d[0])] = d
        for t in writes:
            t.w[id(d[0])] = d
        return inst

    def dma(self, q, out, in_, dst, srcs=(), **kw):
        self._wait(q, self._deps(srcs, (dst,)))
        if dst.dsem is None or dst.dcnt >= 1500:
            dst.dsem = self.new_sem("d_" + dst.name)
            dst.dcnt = 0
        inst = self.eng[q].dma_start(out=out, in_=in_, **kw)
        dst.dcnt += 1
        inst.then_inc(dst.dsem, 16)
        d = (dst.dsem, 16 * dst.dcnt, "dma")
        for t in srcs:
            t.r[id(d[0])] = d
        dst.w[id(d[0])] = d
        self.all_dma[id(d[0])] = d
        return d

    def barrier(self):
        deps = [(self.sem[e], self.cnt[e], "bar") for e in self.eng if self.cnt[e] > 0]
        deps += list(self.all_dma.values())
        for e in self.eng:
            self._wait(e, [d for d in deps if not (d[0] is self.sem[e])])

    def tt(self, e, out, in0, in1, op, R, W):
        return self.op(e, lambda E: E.tensor_tensor(out=out, in0=in0, in1=in1, op=op), R, W)

    def ts(self, e, out, in0, s1, op0, R, W, s2=None, op1=None):
        if op1 is None:
            return self.op(e, lambda E: E.tensor_scalar(out=out, in0=in0, scalar1=s1, scalar2=None, op0=op0), R, W)
        return self.op(e, lambda E: E.tensor_scalar(out=out, in0=in0, scalar1=s1, scalar2=s2, op0=op0, op1=op1), R, W)

    def stt(self, e, out, in0, scalar, in1, op0, op1, R, W):
        return self.op(e, lambda E: E.scalar_tensor_tensor(out=out, in0=in0, scalar=scalar, in1=in1, op0=op0, op1=op1), R, W)

    def act(self, out, in_, func, R, W, bias=None, scale=None, accum=None):
        kw = {}
        if bias is not None:
            kw["bias"] = bias
        if scale is not None:
            kw["scale"] = scale
        if accum is not None:
            kw["accum_out"] = accum
        return self.op("act", lambda E: E.activation(out=out, in_=in_, func=func, **kw), R, W)

    def cp(self, e, out, in_, R, W):
        if e == "act":
            return self.op("act", lambda E: E.copy(out=out, in_=in_), R, W)
        return self.op(e, lambda E: E.tensor_copy(out=out, in_=in_), R, W)

    def mm(self, out, lhsT, rhs, start, stop, R, W):
        return self.op("pe", lambda E: E.matmul(out, lhsT=lhsT, rhs=rhs, start=start, stop=stop), R, W)

    def tr(self, out, in_, ident, R, W):
        return self.op("pe", lambda E: E.transpose(out=out, in_=in_, identity=ident), R, W)

    def ms(self, e, ap, val, W):
        return self.op(e, lambda E: E.memset(ap, val), (), W)


def _consts():
    bf = ml_dtypes.bfloat16
    c = {}
    c["ident"] = np.eye(128, dtype=np.float32).astype(bf)
    kl = np.arange(128)[:, None]
    ql = np.arange(128)[None, :]
    c["tri_le"] = ((kl <= ql).astype(np.float32) * BIG - BIG).astype(bf)
    c["tri_gt"] = ((kl > ql).astype(np.float32) * BIG - BIG).astype(bf)
    d = np.arange(2816)[None, :] - kl - 384
    M = ((d >= 0) & (d <= 128)).astype(np.float32) + ((d >= 0) & (d % 4 == 0) & (d <= 512)) + ((d >= 0) & (d % 16 == 0))
    c["toep"] = M.astype(np.float32).astype(bf)
    cend = np.arange(127) * 16 + 31
    cm = np.zeros((128, S), np.float32)
    cm[:127] = (cend[:, None] <= np.arange(S)[None, :])
    c["cmaskT"] = cm.astype(bf)
    cs = np.arange(127) * 16
    ss = np.arange(32) * 64
    ov = np.clip(np.minimum(cs[:, None] + 32, ss[None, :] + 64) - np.maximum(cs[:, None], ss[None, :]), 0, None).astype(np.float32) / 32
    vx = np.zeros((128, 33), np.float32)
    vx[:127, 0] = 1.0
    vx[:127, 1:] = ov
    c["vcx_tail"] = vx.astype(bf)
    E = (np.arange(S)[None, :] // 64 == np.arange(32)[:, None]).astype(np.float32)
    c["emat"] = E.astype(bf)
    t = np.arange(S)
    jb = np.arange(32)[None, :]
    cur = (t // 64)[:, None]
    valid = ss[None, :] <= t[:, None]
    forced = valid & ((jb == 0) | (jb == cur) | (jb == cur - 1))
    add = np.where(valid, 1.0e4 * forced, -1.0e30).astype(np.float32)
    c["addtab"] = np.ascontiguousarray(add.reshape(NT, 128, 32).transpose(1, 0, 2))
    invA = (THETA ** (-np.arange(0, 16, 2) / 16)).astype(np.float32)
    invB = (THETA ** (-np.arange(0, 32, 2) / 32)).astype(np.float32)
    c["invA"] = np.ascontiguousarray(np.broadcast_to(invA, (128, 8)))
    c["invB"] = np.ascontiguousarray(np.broadcast_to(invB, (128, 16)))
    return c


_CONST_SPECS = {
    "ident": ([128, 128], BF16), "tri_le": ([128, 128], BF16), "tri_gt": ([128, 128], BF16),
    "toep": ([128, 2816], BF16), "cmaskT": ([128, S], BF16), "vcx_tail": ([128, 33], BF16),
    "emat": ([32, S], BF16), "addtab": ([128, NT, 32], F32), "invA": ([128, 8], F32), "invB": ([128, 16], F32),
}

_IN_SPECS = {
    "x": ([S, D], F32), "c_l": ([128, 16], F32), "pos_l": ([128, NT], I32), "posc": ([128, 1], I32),
    "w_ada": ([D, 3 * D], F32), "b_sh": ([1, D], F32), "b_sc": ([1, D], F32), "b_gate": ([1, D], F32),
    "g_pre_r": ([1, D], F32), "g_post": ([1, D], F32), "w_in": ([D, N_IN], F32),
    "pe_ckT": ([64, 32], F32), "pe_cvT": ([64, 32], F32), "w_ck1": ([2048, 256], F32), "w_ck2": ([256, 64], F32),
    "w_cv1": ([2048, 256], F32), "w_cv2": ([256, 64], F32), "w_br_a": ([1024, D], F32), "w_br_b": ([1024, D], F32),
    "w_out": ([D, D], F32),
}

_SCRATCH = {
    "QAT": ([1024, S], BF16), "KST": ([256, S], BF16), "KWT": ([256, S], BF16), "QBT": ([1024, S], BF16),
    "KBT": ([1024, S], BF16), "VS": ([S, 256], BF16), "VW": ([S, 256], BF16), "VB": ([S, 1024], BF16),
    "ZA": ([S, 1024], BF16), "ZB": ([S, 1024], BF16), "SMA": ([D, S], BF16), "SMB": ([D, S], BF16),
    "MT": ([D, S], BF16),
}


def build_nc(debug=None, stop_after=None, start_from=None):
    debug = debug or ()
    nc = bass.Bass("TRN2", target_bir_lowering=False)
    I = {n: nc.dram_tensor(n, sh, dt, kind="ExternalInput").ap() for n, (sh, dt) in {**_IN_SPECS, **_CONST_SPECS}.items()}
    y_out = nc.dram_tensor("y", [S, D], F32, kind="ExternalOutput").ap()
    SC = {}
    for n, (sh, dt) in _SCRATCH.items():
        kind = "ExternalOutput" if n in debug else "Internal"
        if start_from and n != "MT":
            kind = "ExternalInput"
        SC[n] = T(nc.dram_tensor(n, sh, dt, kind=kind).ap(), n)
    dbg_out = {}

    def dbg(name, shape, dt=F32):
        dbg_out[name] = T(nc.dram_tensor(name, shape, dt, kind="ExternalOutput").ap(), name)
        return dbg_out[name]

    with ExitStack() as st:
        k = K(nc, st)
        yT = T(y_out, "y")

        def const(name, q="sp"):
            sh, dt = _CONST_SPECS[name]
            t = k.sb("c_" + name, sh, dt)
            k.dma(q, t[:], I[name][:], t)
            return t

        ident = const("ident")
        tri_le = const("tri_le")
        tri_gt = const("tri_gt")
        toep = const("toep")
        cmaskT = const("cmaskT")
        addtab = const("addtab")
        invA = const("invA")
        invB = const("invB")

        PA = [k.ps(f"pa{i}", [128, 512], F32) for i in range(7)]
        PB = [k.ps(f"pb{i}", [128, 1024], BF16) for i in range(1)]
        pa_i = [0]
        pb_i = [0]

        def nextA(n=3):
            pa_i[0] = (pa_i[0] + 1) % n
            return PA[pa_i[0]]

        def nextB(n=1):
            pb_i[0] = (pb_i[0] + 1) % n
            return PB[pb_i[0]]

        G = k.sb("G", [128, NT, 48], F32)
        gg_bc = k.sb("gg_bc", [128, D], F32)
        cosA = k.sb("cosA", [128, NT, 8], F32)
        sinA = k.sb("sinA", [128, NT, 8], F32)
        cosB = k.sb("cosB", [128, NT, 16], F32)
        sinB = k.sb("sinB", [128, NT, 16], F32)
        cosC = k.sb("cosC", [128, 8], F32)
        sinC = k.sb("sinC", [128, 8], F32)
        kcmpT = k.sb("kcmpT", [64, 4, 128], BF16)
        vcx = k.sb("vcx", [128, 4, 97], BF16)
        k.ms("pool", kcmpT[:], 0.0, [kcmpT])
        k.ms("pool", vcx[:], 0.0, [vcx])
        eps_t = k.sb("eps_t", [128, 1], F32)
        k.ms("dve", eps_t[:], EPS, [eps_t])

        def sincos(pos_i32, ncol, inv, cos_t, sin_t, tagn, stk):
            nf = inv.h.shape[1]
            posf = k.sb("posf" + tagn, [128, ncol], F32, stk)
            k.cp("dve", posf[:], pos_i32[:], [pos_i32], [posf])
            ang = k.sb("ang" + tagn, [128, ncol, nf], F32, stk)
            k.tt("dve", ang[:], posf[:].rearrange("p (c o) -> p c o", o=1).to_broadcast([128, ncol, nf]),
                 inv[:].rearrange("p (o f) -> p o f", o=1).to_broadcast([128, ncol, nf]), ALU.mult, [posf, inv], [ang])
            for which, outt in ((0, sin_t), (1, cos_t)):
                a2 = k.sb(f"a2{tagn}{which}", [128, ncol, nf], F32, stk)
                if which == 1:
                    k.ts("dve", a2[:], ang[:], float(np.pi / 2), ALU.add, [ang], [a2])
                    src = a2
                else:
                    src = ang
                u = k.sb(f"u{tagn}{which}", [128, ncol, nf], F32, stk)
                k.ts("dve", u[:], src[:], float(1 / (2 * np.pi)), ALU.mult, [src], [u])
                ki = k.sb(f"ki{tagn}{which}", [128, ncol, nf], I32, stk)
                k.cp("dve", ki[:], u[:], [u], [ki])
                kf = k.sb(f"kf{tagn}{which}", [128, ncol, nf], F32, stk)
                k.cp("dve", kf[:], ki[:], [ki], [kf])
                rr = k.sb(f"rr{tagn}{which}", [128, ncol, nf], F32, stk)
                k.stt("dve", rr[:], kf[:], -float(2 * np.pi), src[:], ALU.mult, ALU.add, [kf, src], [rr])
                m = k.sb(f"m{tagn}{which}", [128, ncol, nf], F32, stk)
                k.ts("dve", m[:], rr[:], float(np.pi), ALU.is_gt, [rr], [m], s2=-float(2 * np.pi), op1=ALU.mult)
                k.tt("dve", rr[:], rr[:], m[:], ALU.add, [rr, m], [rr])
                k.ts("dve", m[:], rr[:], -float(np.pi), ALU.is_lt, [rr], [m], s2=float(2 * np.pi), op1=ALU.mult)
                k.tt("dve", rr[:], rr[:], m[:], ALU.add, [rr, m], [rr])
                k.ts("dve", rr[:], rr[:], 3.14159, ALU.min, [rr], [rr], s2=-3.14159, op1=ALU.max)
                k.act(outt[:] if len(outt.h.shape) == 3 else outt[:].rearrange("p (c f) -> p c f", c=1),
                      rr[:], AF.Sin, [rr], [outt])

        if start_from:
            dI = {n: nc.dram_tensor(n, sh, dt, kind="ExternalInput").ap() for n, (sh, dt) in
                  {"d_G": ([128, NT, 48], F32), "d_kcvT": ([8, 64, S], BF16), "d_gg": ([128, D], F32)}.items()}
            k.dma("sp", G[:], dI["d_G"][:, :, :], G)
            k.dma("sp", gg_bc[:], dI["d_gg"][:, :], gg_bc)
            sA = st.enter_context(ExitStack())
            kcvT = [k.sb(f"kcvT{m}", [64, S], BF16, sA) for m in range(8)]
            for m in range(8):
                k.dma("sp", kcvT[m][:], dI["d_kcvT"][m], kcvT[m])
            with ExitStack() as s0:
                posc_t = k.sb("posc_t", [128, 1], I32, s0)
                k.dma("sp", posc_t[:], I["posc"][:], posc_t)
                sincos(posc_t, 1, invA, cosC, sinC, "C", s0)
                k.barrier()
            return _tail(nc, k, st, I, SC, yT, dbg, dbg_out, debug, stop_after, locals())

        sA = st.enter_context(ExitStack())
        kcvT = [k.sb(f"kcvT{m}", [64, S], BF16, sA) for m in range(8)]
        sB = st.enter_context(ExitStack())
        hT_h = sB.enter_context(nc.sbuf_tensor("hT", [128, 16, S], BF16))
        hT = [T(hT_h, f"hT{tt}") for tt in range(NT)]
        sX = st.enter_context(ExitStack())
        gs_bc = k.sb("gs_bc", [128, D], F32, sX)
        sh_bc = k.sb("sh_bc", [128, D], F32, sX)

        with ExitStack() as s0:
            pos_t = k.sb("pos_t", [128, NT], I32, s0)
            k.dma("sp", pos_t[:], I["pos_l"][:], pos_t)
            posc_t = k.sb("posc_t", [128, 1], I32, s0)
            k.dma("sp", posc_t[:], I["posc"][:], posc_t)
            c32 = k.sb("c32", [128, 16], F32, s0)
            k.dma("sp", c32[:], I["c_l"][:], c32)
            c_bc = k.sb("c_bc", [128, 16, 128], BF16, s0)
            k.cp("dve", c_bc[:], c32[:].rearrange("p (k o) -> p k o", o=1).to_broadcast([128, 16, 128]), [c32], [c_bc])
            wa = [k.sb(f"wa{i}", [128, 16, 512], BF16, s0) for i in range(2)]
            r1 = [k.sb(f"r1_{i}", [128, 512], F32, s0) for i in range(2)]
            r2 = [k.sb(f"r2_{i}", [128, 512], F32, s0) for i in range(2)]
            wada_v = I["w_ada"].rearrange("(kc p) n -> p kc n", p=128)

            def load_wa(blk):
                k.dma("pool", wa[blk % 2][:], wada_v[:, :, blk * 512:(blk + 1) * 512], wa[blk % 2])
            load_wa(0)
            for blk in range(12):
                w = wa[blk % 2]
                if blk + 1 < 12:
                    load_wa(blk + 1)
                kind, cb = blk // 4, blk % 4
                cs = slice(cb * 512, (cb + 1) * 512)
                a1, a2 = r1[blk % 2], r2[blk % 2]
                k.dma("sp", a1[:], I[("b_sh", "b_sc", "b_gate")[kind]][0:1, cs].to_broadcast([128, 512]), a1)
                if kind >= 1:
                    k.dma("sp", a2[:], I[("g_pre_r", "g_post")[kind - 1]][0:1, cs].to_broadcast([128, 512]), a2)
                pg = nextA()
                for kc in range(16):
                    k.mm(pg[:], c_bc[:, kc, :], w[:, kc, :], kc == 0, kc == 15, [w, c_bc], [pg])
                if kind == 0:
                    k.tt("dve", sh_bc[:, cs], pg[:], a1[:], ALU.add, [pg, a1], [sh_bc])
                elif kind == 1:
                    k.tt("dve", gs_bc[:, cs], pg[:], a1[:], ALU.add, [pg, a1], [gs_bc])
                    k.stt("dve", gs_bc[:, cs], gs_bc[:, cs], 1.0, a2[:], ALU.add, ALU.mult, [gs_bc, a2], [gs_bc])
                else:
                    k.tt("dve", gg_bc[:, cs], pg[:], a1[:], ALU.add, [pg, a1], [gg_bc])
                    k.tt("dve", gg_bc[:, cs], gg_bc[:, cs], a2[:], ALU.mult, [gg_bc, a2], [gg_bc])
            sincos(pos_t, NT, invA, cosA, sinA, "A", s0)
            sincos(pos_t, NT, invB, cosB, sinB, "B", s0)
            sincos(posc_t, 1, invA, cosC, sinC, "C", s0)
            k.barrier()

        with ExitStack() as s1:
            xb = [k.sb(f"xb{i}", [128, D], F32, s1) for i in range(3)]
            xs = [k.sb(f"xs{i}", [128, D], F32, s1) for i in range(2)]
            xn = [k.sb(f"xn{i}", [128, D], BF16, s1) for i in range(2)]
            junk = k.sb("junk", [128, D], BF16, s1)
            ssq = k.sb("ssq", [128, NT], F32, s1)
            rstd = k.sb("rstd", [128, NT], F32, s1)
            k.ms("dve", ssq[:], 0.0, [ssq])
            for tt in range(NT):
                xt = xb[tt % 3]
                k.dma("sp", xt[:], I["x"][tt * 128:(tt + 1) * 128, :], xt)
                k.act(junk[:], xt[:], AF.Square, [xt, ssq], [junk, ssq], accum=ssq[:, tt:tt + 1])
                k.act(rstd[:, tt:tt + 1], ssq[:, tt:tt + 1], AF.Ln, [ssq, eps_t], [rstd], bias=eps_t[:, 0:1], scale=1.0 / D)
                k.act(rstd[:, tt:tt + 1], rstd[:, tt:tt + 1], AF.Exp, [rstd], [rstd], scale=-0.5)
                xst = xs[tt % 2]
                xnt = xn[tt % 2]
                k.stt("dve", xst[:], xt[:], rstd[:, tt:tt + 1], gs_bc[:], ALU.mult, ALU.mult, [xt, rstd, gs_bc], [xst])
                k.tt("dve", xnt[:], xst[:], sh_bc[:], ALU.add, [xst, sh_bc], [xnt])
                for half in range(2):
                    pb = nextB()
                    for j in range(8):
                        kc = half * 8 + j
                        k.tr(pb[:, j * 128:(j + 1) * 128], xnt[:, kc * 128:(kc + 1) * 128], ident[:], [xnt, ident], [pb])
                    k.cp("act" if half == 0 else "dve", hT_h[:, half * 8:(half + 1) * 8, tt * 128:(tt + 1) * 128],
                         pb[:, :].rearrange("p (j t) -> p j t", t=128), [pb], [hT[tt]])
            k.barrier()
        sX.close()

        if "hT" in debug:
            o = dbg("d_hT", [128, 16, S], BF16)
            k.dma("sp", o[:, :, :], hT_h[:], o, hT)

        if stop_after == "P1":
            return _finish(nc, k, yT, dbg_out, SC, debug)

        s2 = st.enter_context(ExitStack())
        wbf = [k.sb(f"wbf{i}", [128, 16, 512], BF16, s2) for i in range(3)]
        win_v = I["w_in"].rearrange("(kc p) n -> p kc n", p=128)
        blocks = []
        for i in range(2):
            blocks.append((C_QA + 512 * i, 512, "qa", i))
        blocks.append((C_KS, 512, "ksvs", 0))
        blocks.append((C_KW, 512, "kwvw", 0))
        blocks.append((C_GA, 48, "ga", 0))
        for i in range(2):
            blocks.append((C_ZA + 512 * i, 512, "za", i))
        for i in range(2):
            blocks.append((C_QB + 512 * i, 512, "qb", i))
        for i in range(2):
            blocks.append((C_KB + 512 * i, 512, "kb", i))
        for i in range(2):
            blocks.append((C_VB + 512 * i, 512, "vb", i))
        for i in range(2):
            blocks.append((C_ZB + 512 * i, 512, "zb", i))
        for i in range(4):
            blocks.append((C_MA + 512 * i, 512, "ma", i))
        for i in range(4):
            blocks.append((C_MB + 512 * i, 512, "mb", i))
        blocks.append((C_KC, 512, "kcvc", 0))
        if stop_after and stop_after.startswith("R:"):
            blocks = [b for b in blocks if b[2] in stop_after[2:].split("+")]
        if stop_after == "P2a":
            blocks = [b for b in blocks if b[2] in ("qa", "ksvs", "ga", "za")]
        if stop_after == "P2b":
            blocks = [b for b in blocks if b[2] in ("qb", "vb", "ma", "kcvc")]

        def load_w(bi):
            col0, ncols, _, _ = blocks[bi]
            w = wbf[bi % 3]
            k.dma("pool", w[:, :, 0:ncols], win_v[:, :, col0:col0 + ncols], w)

        tok16 = [k.sb(f"tok16_{i}", [128, 512], BF16, s2) for i in range(2)]
        rt = [k.sb(f"rt{i}", [128, 4, 128], F32, s2) for i in range(2)]
        stgT = [k.sb(f"stgT{i}", [128, 4, 512], BF16, s2) for i in range(2)]
        fm16 = [k.sb(f"fm16_{i}", [128, 512], BF16, s2) for i in range(2)]
        cnt = {"tok": 0, "stg": 0, "fm": 0}

        import os as _os
        _DBG = _os.environ.get("KDBG", "")

        def rope(ps, out16, tt, nh, dh, cos_t, sin_t):
            if "norope" in _DBG:
                return
            half = dh // 8
            pv = ps[:, 0:nh * dh].rearrange("p (h d) -> p h d", d=dh)
            ov = out16[:, 0:nh * dh].rearrange("p (h d) -> p h d", d=dh)
            r = rt[cnt["tok"] % 2]
            n = nh * half
            cb = cos_t[:, tt, :].rearrange("p (o f) -> p o f", o=1).to_broadcast([128, nh, half])
            sb_ = sin_t[:, tt, :].rearrange("p (o f) -> p o f", o=1).to_broadcast([128, nh, half])
            tv = [r[:, i, 0:n].rearrange("p (h f) -> p h f", f=half) for i in range(4)]
            x1 = pv[:, :, 0:half]
            x2 = pv[:, :, half:2 * half]
            lvl = 6
            for i_ in range(1, 7):
                if f"rope{i_}" in _DBG:
                    lvl = i_
            if lvl >= 1:
                k.tt("dve", tv[0], x1, cb, ALU.mult, [ps, cos_t], [r])
            if lvl >= 2:
                k.tt("dve", tv[1], x2, sb_, ALU.mult, [ps, sin_t], [r])
            if lvl >= 3:
                k.tt("dve", tv[2], x2, cb, ALU.mult, [ps, cos_t], [r])
            if lvl >= 4:
                k.tt("dve", tv[3], x1, sb_, ALU.mult, [ps, sin_t], [r])
            if lvl >= 5:
                k.tt("dve", ov[:, :, 0:half], tv[0], tv[1], ALU.subtract, [r], [out16])
            if lvl >= 6:
                k.tt("dve", ov[:, :, half:2 * half], tv[2], tv[3], ALU.add, [r], [out16])

        def proj_tok(w, ncols, tt):
            ps = nextA()
            for kc in range(16):
                k.mm(ps[:, 0:ncols], hT_h[:, kc, tt * 128:(tt + 1) * 128], w[:, kc, 0:ncols], kc == 0, kc == 15, [hT[tt], w], [ps])
            return ps

        def transposes_to_stage(src16, ncol128, tt, stg):
            pb = nextB()
            for j in range(ncol128):
                k.tr(pb[:, j * 128:(j + 1) * 128], src16[:, j * 128:(j + 1) * 128], ident[:], [src16, ident], [pb])
            q4 = tt % 4
            eng = "act" if tt % 2 == 0 else "dve"
            k.cp(eng, stg[:, 0:ncol128, q4 * 128:(q4 + 1) * 128],
                 pb[:, 0:ncol128 * 128].rearrange("p (j t) -> p j t", t=128), [pb], [stg])

        load_w(0)
        if len(blocks) > 1:
            load_w(1)
        for bi, (col0, ncols, role, idx) in enumerate(blocks):
            if bi + 2 < len(blocks):
                load_w(bi + 2)
            w = wbf[bi % 3]
            if role in ("qa", "qb", "kb"):
                nh, dh, cos_t, sin_t = (8, 64, cosA, sinA) if role == "qa" else (4, 128, cosB, sinB)
                dstT = SC["QAT"] if role == "qa" else (SC["QBT"] if role == "qb" else SC["KBT"])
                deferred = []
                for tt in range(NT):
                    ps = proj_tok(w, 512, tt)
                    for f_ in deferred:
                        f_()
                    deferred = []
                    o16 = tok16[cnt["tok"] % 2]
                    k.cp("act", o16[:], ps[:], [ps], [o16])
                    rope(ps, o16, tt, nh, dh, cos_t, sin_t)
                    cnt["tok"] += 1

                    def fin(o16=o16, tt=tt):
                        stg = stgT[cnt["stg"] % 2]
                        transposes_to_stage(o16, 4, tt, stg)
                        if tt % 4 == 3:
                            tc_ = tt // 4
                            k.dma("sp", dstT.h[idx * 512:(idx + 1) * 512, tc_ * 512:(tc_ + 1) * 512].rearrange("(j p) t -> p j t", p=128),
                                  stg[:], dstT, [stg])
                            cnt["stg"] += 1
                    deferred.append(fin)
                for f_ in deferred:
                    f_()
            elif role in ("ksvs", "kwvw"):
                dstK = SC["KST"] if role == "ksvs" else SC["KWT"]
                dstV = SC["VS"] if role == "ksvs" else SC["VW"]
                deferred = []
                for tt in range(NT):
                    ps = proj_tok(w, 512, tt)
                    for f_ in deferred:
                        f_()
                    deferred = []
                    o16 = tok16[cnt["tok"] % 2]
                    k.cp("act", o16[:], ps[:], [ps], [o16])
                    rope(ps, o16, tt, 4, 64, cosA, sinA)
                    cnt["tok"] += 1

                    def fin(o16=o16, tt=tt):
                        stg = stgT[cnt["stg"] % 2]
                        transposes_to_stage(o16, 2, tt, stg)
                        k.dma("sp", dstV.h[tt * 128:(tt + 1) * 128, :], o16[:, 256:512], dstV, [o16])
                        if tt % 4 == 3:
                            tc_ = tt // 4
                            k.dma("sp", dstK.h[:, tc_ * 512:(tc_ + 1) * 512].rearrange("(j p) t -> p j t", p=128),
                                  stg[:, 0:2, :], dstK, [stg])
                            cnt["stg"] += 1
                    deferred.append(fin)
                for f_ in deferred:
                    f_()
            elif role == "ga":
                for tt in range(NT):
                    ps = proj_tok(w, 48, tt)
                    k.act(G[:, tt, :], ps[:, 0:48], AF.Sigmoid, [ps], [G])
            elif role in ("za", "zb", "vb"):
                dst = SC["ZA"] if role == "za" else (SC["ZB"] if role == "zb" else SC["VB"])
                for tt in range(NT):
                    ps = proj_tok(w, 512, tt)
                    o16 = tok16[cnt["tok"] % 2]
                    if role == "vb":
                        k.cp("act", o16[:], ps[:], [ps], [o16])
                    else:
                        k.act(o16[:], ps[:], AF.Silu, [ps], [o16])
                    cnt["tok"] += 1
                    k.dma("sp", dst.h[tt * 128:(tt + 1) * 128, idx * 512:(idx + 1) * 512], o16[:], dst, [o16])
            elif role in ("ma", "mb"):
                dst = SC["SMA"] if role == "ma" else SC["SMB"]
                for j in range(4):
                    for tc_ in range(4):
                        ps = nextA()
                        for kc in range(16):
                            k.mm(ps[:], w[:, kc, j * 128:(j + 1) * 128], hT_h[:, kc, tc_ * 512:(tc_ + 1) * 512], kc == 0, kc == 15,
                                 [hT[4 * tc_ + i] for i in range(4)] + [w], [ps])
                        o16 = fm16[cnt["fm"] % 2]
                        k.act(o16[:], ps[:], AF.Sigmoid, [ps], [o16])
                        cnt["fm"] += 1
                        r0 = idx * 512 + j * 128
                        k.dma("sp", dst.h[r0:r0 + 128, tc_ * 512:(tc_ + 1) * 512], o16[:], dst, [o16])
            elif role == "kcvc":
                for m in range(8):
                    for tc_ in range(4):
                        ps = nextA()
                        for kc in range(16):
                            k.mm(ps[0:64, :], w[:, kc, m * 64:(m + 1) * 64], hT_h[:, kc, tc_ * 512:(tc_ + 1) * 512], kc == 0, kc == 15,
                                 [hT[4 * tc_ + i] for i in range(4)] + [w], [ps])
                        k.cp("act" if (m + tc_) % 2 == 0 else "dve", kcvT[m][:, tc_ * 512:(tc_ + 1) * 512], ps[0:64, :], [ps], [kcvT[m]])
        k.barrier()
        s2.close()
        sB.close()

        if "kcvT" in debug:
            o = dbg("d_kcvT", [8, 64, S], BF16)
            for m in range(8):
                k.dma("sp", o.h[m], kcvT[m][:], o, [kcvT[m]])
        if "G" in debug:
            o = dbg("d_G", [128, NT, 48])
            k.dma("sp", o[:, :, :], G[:], o, [G])

        if stop_after in ("P2", "P2a", "P2b") or (stop_after and stop_after.startswith("R:")):
            sA.close()
            return _finish(nc, k, yT, dbg_out, SC, debug)
        return _tail(nc, k, st, I, SC, yT, dbg, dbg_out, debug, stop_after, locals())


def _finish(nc, k, yT, dbg_out, SC, debug):
    deps = list(yT.w.values())
    for o in dbg_out.values():
        deps.extend(o.w.values())
    for n in debug:
        if n in SC:
            deps.extend(SC[n].w.values())
    k._wait("sp", deps)
    return nc


def _core_inputs(b, x, c, positions, w_ada, b_ada, g_pre, g_post, w_in, pe_ck, pe_cv, w_ck1, w_ck2,
                 w_cv1, w_cv2, w_br_a, w_br_b, w_out, consts):
    f = np.ascontiguousarray
    lay = lambda v: f(np.asarray(v, np.float32).reshape(16, 128).T)
    pos = np.asarray(positions[b], np.int32)
    posc = np.zeros((128, 1), np.int32)
    posc[:127, 0] = pos[31::16][:127]
    m = {
        "x": f(x[b]), "c_l": lay(c[b]), "pos_l": f(pos.reshape(NT, 128).T), "posc": posc,
        "w_ada": f(w_ada[0]), "b_sh": f(b_ada[0, 0:D].reshape(1, D)), "b_sc": f(b_ada[0, D:2 * D].reshape(1, D)),
        "b_gate": f(b_ada[0, 2 * D:3 * D].reshape(1, D)), "g_pre_r": f(g_pre[0].reshape(1, D)), "g_post": f(g_post[0].reshape(1, D)),
        "w_in": f(w_in[0]), "pe_ckT": f(pe_ck[0].T), "pe_cvT": f(pe_cv[0].T), "w_ck1": f(w_ck1[0]), "w_ck2": f(w_ck2[0]),
        "w_cv1": f(w_cv1[0]), "w_cv2": f(w_cv2[0]), "w_br_a": f(w_br_a[0]), "w_br_b": f(w_br_b[0]), "w_out": f(w_out[0]),
    }
    m.update(consts)
    return m


def kernel(**inputs):
    inputs = {k_: np.asarray(v) for k_, v in inputs.items()}
    consts = _consts()
    nc = build_nc()
    in_maps = [_core_inputs(b, consts=consts, **inputs) for b in range(8)]
    res = run_bass_kernel_spmd(nc, in_maps, core_ids=list(range(8)))
    return np.stack([np.asarray(r["y"], np.float32) for r in res.results], axis=0)


def _tail(nc, k, st, I, SC, yT, dbg, dbg_out, debug, stop_after, L):
    ident, tri_le, tri_gt, toep, cmaskT, addtab = (L[n] for n in ("ident", "tri_le", "tri_gt", "toep", "cmaskT", "addtab"))
    G, gg_bc, cosC, sinC, kcmpT, vcx, kcvT, sA, PA, nextA, nextB, eps_t = (
        L[n] for n in ("G", "gg_bc", "cosC", "sinC", "kcmpT", "vcx", "kcvT", "sA", "PA", "nextA", "nextB", "eps_t"))
    y_out = yT.h
    X = mybir.AxisListType.X

    def bc(ap2, n):
        q = ap2.shape[1]
        return ap2.rearrange("p (q o) -> p q o", o=1).to_broadcast([128, q, n])

    def v3(ap2):
        return ap2.rearrange("p (q o) -> p q o", o=1)

    with ExitStack() as sC:
        wc1 = [k.sb(f"wc1_{i}", [64, 32, 256], BF16, sC) for i in range(2)]
        wc2 = [k.sb(f"wc2_{i}", [128, 2, 64], BF16, sC) for i in range(2)]
        peT = [k.sb(f"peT{i}", [64, 32], BF16, sC) for i in range(2)]
        for i, (n1, n2, npe) in enumerate((("w_ck1", "w_ck2", "pe_ckT"), ("w_cv1", "w_cv2", "pe_cvT"))):
            k.dma("pool", wc1[i][:], I[n1].rearrange("(l d) h -> d l h", d=64), wc1[i])
            k.dma("pool", wc2[i][:], I[n2].rearrange("(hh p) n -> p hh n", p=128), wc2[i])
            k.dma("pool", peT[i][:], I[npe][:, :], peT[i])
        for g in range(4):
            k.dma("sp", vcx[:, g, 64:97], I["vcx_tail"][:, :], vcx)
        cbias = k.sb("cbias", [128, 4], F32, sC)
        xh = k.sb("xh", [128, 127], F32, sC)
        x2 = k.sb("x2", [128, 127], F32, sC)
        sg = k.sb("sg", [128, 127], F32, sC)
        hcT = k.sb("hcT", [128, 2, 128], BF16, sC)
        kc16 = k.sb("kc16", [128, 64], BF16, sC)
        rtc = k.sb("rtc", [128, 4, 8], F32, sC)
        psb = PA[3]
        for kv in range(2):
            for half in range(2):
                col = kv * 2 + half
                for l in range(32):
                    k.mm(psb[:, col:col + 1], wc1[kv][:, l, half * 128:(half + 1) * 128], peT[kv][:, l:l + 1], l == 0, l == 31,
                         [wc1[kv], peT[kv]], [psb])
        k.cp("dve", cbias[:], psb[:, 0:4], [psb], [cbias])
        for kv in range(2):
            for g in range(4):
                src = kcvT[kv * 4 + g]
                for half in range(2):
                    ps = nextA()
                    for l in range(32):
                        k.mm(ps[:, 0:127], wc1[kv][:, l, half * 128:(half + 1) * 128], src[:, l:l + 2017:16], l == 0, l == 31,
                             [wc1[kv], src], [ps])
                    cc = kv * 2 + half
                    k.act(xh[:], ps[:, 0:127], AF.Identity, [ps, cbias], [xh], bias=cbias[:, cc:cc + 1], scale=1.0)
                    k.tt("dve", x2[:], xh[:], xh[:], ALU.mult, [xh], [x2])
                    k.ts("dve", x2[:], x2[:], 0.044715, ALU.mult, [x2], [x2], s2=1.0, op1=ALU.add)
                    k.tt("dve", x2[:], x2[:], xh[:], ALU.mult, [x2, xh], [x2])
                    k.act(sg[:], x2[:], AF.Sigmoid, [x2], [sg], scale=1.5957691216)
                    k.tt("dve", hcT[:, half, 0:127], xh[:], sg[:], ALU.mult, [xh, sg], [hcT])
                ps2 = nextA()
                for half in range(2):
                    k.mm(ps2[0:127, 0:64], hcT[:, half, 0:127], wc2[kv][:, half, :], half == 0, half == 1, [hcT, wc2[kv]], [ps2])
                if kv == 0:
                    k.cp("act", kc16[0:127, :], ps2[0:127, 0:64], [ps2], [kc16])
                    x1 = ps2[0:127, 0:8]
                    x2_ = ps2[0:127, 8:16]
                    k.tt("dve", rtc[0:127, 0, :], x1, cosC[0:127, :], ALU.mult, [ps2, cosC], [rtc])
                    k.tt("dve", rtc[0:127, 1, :], x2_, sinC[0:127, :], ALU.mult, [ps2, sinC], [rtc])
                    k.tt("dve", rtc[0:127, 2, :], x2_, cosC[0:127, :], ALU.mult, [ps2, cosC], [rtc])
                    k.tt("dve", rtc[0:127, 3, :], x1, sinC[0:127, :], ALU.mult, [ps2, sinC], [rtc])
                    k.tt("dve", kc16[0:127, 0:8], rtc[0:127, 0, :], rtc[0:127, 1, :], ALU.subtract, [rtc], [kc16])
                    k.tt("dve", kc16[0:127, 8:16], rtc[0:127, 2, :], rtc[0:127, 3, :], ALU.add, [rtc], [kc16])
                    pb = nextB()
                    k.tr(pb[0:64, 0:127], kc16[0:127, :], ident[0:127, 0:127], [kc16, ident], [pb])
                    k.cp("dve", kcmpT[:, g, 0:127], pb[0:64, 0:127], [pb], [kcmpT])
                else:
                    k.cp("act", vcx[0:127, g, 0:64], ps2[0:127, 0:64], [ps2], [vcx])
        k.barrier()
    sA.close()

    if "cmp" in debug:
        o = dbg("d_kcmpT", [64, 4, 128], BF16)
        k.dma("sp", o[:, :, :], kcmpT[:], o, [kcmpT])
        o = dbg("d_vcx", [128, 4, 97], BF16)
        k.dma("sp", o[:, :, :], vcx[:], o, [vcx])
    if stop_after == "P3":
        return _finish(nc, k, yT, dbg_out, SC, debug)

    gatedAT = k.sb("gatedAT", [128, 8, S], BF16)
    gatedBT = k.sb("gatedBT", [128, 8, S], BF16)

    with ExitStack() as s4:
        ksT = [k.sb(f"ksT{i}", [96, S], BF16, s4) for i in range(2)]
        kwT = [k.sb(f"kwT{i}", [64, S], BF16, s4) for i in range(2)]
        vsg = [k.sb(f"vsg{i}", [128, 16, 65], BF16, s4) for i in range(2)]
        vwg = [k.sb(f"vwg{i}", [128, 16, 65], BF16, s4) for i in range(2)]
        qTh = [[s4.enter_context(nc.sbuf_tensor(f"qT{i}_{r}", [96, S], BF16)) for r in range(4)] for i in range(2)]
        qTq = [[T(qTh[i][r], f"qTq{i}_{r}") for r in range(4)] for i in range(2)]
        qTm = [[[T(qTh[i][r], f"qTm{i}_{r}_{qc}") for qc in range(4)] for r in range(4)] for i in range(2)]
        for i in range(2):
            k.dma("sp", ksT[i][64:96, :], I["emat"][:, :], ksT[i])
            k.ms("pool", vsg[i][:, :, 64:65], 1.0, [vsg[i]])
            k.ms("pool", vwg[i][:, :, 64:65], 1.0, [vwg[i]])
        zag = [k.sb(f"zag{i}", [128, 4, 256], BF16, s4) for i in range(2)]
        Pcs = [k.sb(f"Pc{i}", [128, 512], BF16, s4) for i in range(2)]
        Pt = [k.sb(f"Pt{i}", [128, 512], BF16, s4) for i in range(4)]
        oacc_h = s4.enter_context(nc.sbuf_tensor("oacc", [128, 4, 4, 64], F32))
        oacc = [T(oacc_h, f"oacc{r}") for r in range(4)]
        imp = k.sb("imp", [128, 4, 32], F32, s4)
        tmpi = k.sb("tmpi", [128, 4, 32], F32, s4)
        tmpo = [k.sb(f"tmpo{i}", [128, 4, 64], F32, s4) for i in range(2)]
        rden = k.sb("rden", [128, 4], F32, s4)
        fsc = k.sb("fsc", [128, 4], F32, s4)
        sc = k.sb("sc", [128, 32], F32, s4)
        sc2 = k.sb("sc2", [128, 32], F32, s4)
        m8 = k.sb("m8", [128, 8], F32, s4)
        m8b = k.sb("m8b", [128, 8], F32, s4)
        msk = k.sb("msk", [128, 32], F32, s4)
        mb16 = k.sb("mb16", [128, 4, 32], BF16, s4)
        gtok = [k.sb(f"gtok{i}", [128, 4, 256], BF16, s4) for i in range(2)]
        gAT = T(gatedAT.h, "gatedAT")
        cn = {"pt": 0, "to": 0, "acc": 0, "first": True, "pc": 0}

        def load_group(g):
            i = g % 2
            k.dma("sp", ksT[i][0:64, :], SC["KST"].h[g * 64:(g + 1) * 64, :], ksT[i], [SC["KST"]])
            k.dma("sp", kwT[i][:, :], SC["KWT"].h[g * 64:(g + 1) * 64, :], kwT[i], [SC["KWT"]])
            k.dma("sp", vsg[i][:, :, 0:64], SC["VS"].h[:, g * 64:(g + 1) * 64].rearrange("(kt p) d -> p kt d", p=128), vsg[i], [SC["VS"]])
            k.dma("sp", vwg[i][:, :, 0:64], SC["VW"].h[:, g * 64:(g + 1) * 64].rearrange("(kt p) d -> p kt d", p=128), vwg[i], [SC["VW"]])
            for r in range(4):
                h = 4 * g + r
                k.dma("sp", qTh[i][r][0:64, :], SC["QAT"].h[h * 64:(h + 1) * 64, :], qTq[i][r], [SC["QAT"]])

        def evac_branch(acc_bank, width, r, h, qc, br, first_write):
            av = acc_bank[:, 0:4 * width].rearrange("p (q c) -> p q c", c=width)
            if br == 0:
                k.ts("dve", v3(rden[:]), av[:, :, 64:65], 1e-30, ALU.max, [acc_bank], [rden])
                k.op("dve", lambda E: E.reciprocal(out=rden[:], in_=rden[:]), [rden], [rden])
            else:
                k.op("dve", lambda E: E.reciprocal(out=v3(rden[:]), in_=av[:, :, 64:65]), [acc_bank], [rden])
            k.tt("dve", fsc[:], rden[:], G[:, 4 * qc:4 * qc + 4, 3 * h + br], ALU.mult, [rden, G], [fsc])
            if first_write:
                k.tt("dve", oacc_h[:, r, :, :], av[:, :, 0:64], bc(fsc[:], 64), ALU.mult, [acc_bank, fsc], [oacc[r]])
            else:
                tm = tmpo[cn["to"] % 2]
                cn["to"] += 1
                k.tt("dve", tm[:], av[:, :, 0:64], bc(fsc[:], 64), ALU.mult, [acc_bank, fsc], [tm])
                k.tt("pool", oacc_h[:, r, :, :], oacc_h[:, r, :, :], tm[:], ALU.add, [oacc[r], tm], [oacc[r]])

        load_group(0)
        for g in range(4):
            i = g % 2
            if g + 1 < 4:
                load_group(g + 1)
            for qc in range(4):
                zi = (g * 4 + qc) % 2
                gi = zi
                qs = slice(qc * 512, (qc + 1) * 512)
                k.dma("sp", zag[zi][:], SC["ZA"].h[qc * 512:(qc + 1) * 512, g * 256:(g + 1) * 256].rearrange("(qi p) c -> p qi c", p=128),
                      zag[zi], [SC["ZA"]])
                use_mask = qc >= 2
                def hookA():
                    for qi in range(4):
                        tt_ = 4 * qc + qi
                        k.tt("dve", sc[:], imp[:, qi, :], addtab[:, tt_, :], ALU.add, [imp, addtab], [sc])
                        k.op("dve", lambda E: E.max(out=m8[:], in_=sc[:]), [sc], [m8])
                        k.op("dve", lambda E: E.match_replace(out=sc2[:], in_to_replace=m8[:], in_values=sc[:], imm_value=-3.0e38),
                             [sc, m8], [sc2])
                        k.op("dve", lambda E: E.max(out=m8b[:], in_=sc2[:]), [sc2], [m8b])
                        k.ts("dve", msk[:], sc[:], m8b[:, 7:8], ALU.is_ge, [sc, m8b], [msk])
                        k.ts("dve", mb16[:, qi, :], msk[:], 1.0, ALU.subtract, [msk], [mb16], s2=BIG, op1=ALU.mult)

                def hookB():
                    pbm = nextB()
                    for qi in range(4):
                        k.tr(pbm[0:32, qi * 128:(qi + 1) * 128], mb16[:, qi, :], ident[:], [mb16, ident], [pbm])
                    for r in range(4):
                        k.cp("dve" if r % 2 == 0 else "act", qTh[i][r][64:96, qs], pbm[0:32, 0:512], [pbm], [qTm[i][r][qc]])

                KK = 96 if use_mask else 64
                items = [("cmp", r, 0, True, True) for r in range(4)]
                for r in range(4):
                    kts = list(range(max(0, 4 * qc - 4), 4 * qc + 4))
                    for n_, kt in enumerate(kts):
                        items.append(("win", r, kt, n_ == 0, n_ == len(kts) - 1))
                for r in range(4):
                    for kt in range(4 * qc + 4):
                        items.append(("sel", r, kt, kt == 0, kt == 4 * qc + 3))

                def score(it):
                    br, r, kt, _, _ = it
                    if br == "sel" and r == 0 and kt == 0 and use_mask:
                        hookB()
                    ps_ = nextA()
                    if br == "cmp":
                        k.mm(ps_[0:127, :], kcmpT[:, g, 0:127], qTh[i][r][0:64, qs], True, True, [kcmpT, qTq[i][r]], [ps_])
                    elif br == "win":
                        k.mm(ps_[:], kwT[i][:, kt * 128:(kt + 1) * 128], qTh[i][r][0:64, qs], True, True, [kwT[i], qTq[i][r]], [ps_])
                    else:
                        dq = [qTq[i][r]] + ([qTm[i][r][qc]] if use_mask else [])
                        k.mm(ps_[:], ksT[i][0:KK, kt * 128:(kt + 1) * 128], qTh[i][r][0:KK, qs], True, True, [ksT[i]] + dq, [ps_])
                    if br != "cmp":
                        if kt >= 4 * qc:
                            qd = kt - 4 * qc
                            k.op("pe", lambda E: E.matmul(ps_[:, qd * 128:(qd + 1) * 128], lhsT=ident[:], rhs=tri_le[:], start=False, stop=True,
                                                          skip_group_check=True), [ident, tri_le], [ps_])
                        ql = kt + 4 - 4 * qc
                        if br == "win" and 0 <= ql <= 3:
                            k.op("pe", lambda E: E.matmul(ps_[:, ql * 128:(ql + 1) * 128], lhsT=ident[:], rhs=tri_gt[:], start=False, stop=True,
                                                          skip_group_check=True), [ident, tri_gt], [ps_])
                    return ps_

                def post(it, ps_):
                    br, r, kt, is_first, is_last = it
                    h = 4 * g + r
                    if is_first:
                        cn["acc"] += 1
                        cn["first"] = True
                    acc = PA[4 + cn["acc"] % 3]
                    if br == "cmp":
                        Pc = Pcs[cn["pc"] % 2]
                        cn["pc"] += 1
                        pov = acc[:, 0:388].rearrange("p (q c) -> p q c", c=97)
                        k.act(Pc[0:127, :], ps_[0:127, :], AF.Exp, [ps_], [Pc], scale=0.125)
                        k.tt("dve", Pc[0:127, :], Pc[0:127, :], cmaskT[0:127, qs], ALU.mult, [Pc, cmaskT], [Pc])
                        for qi in range(4):
                            k.mm(acc[:, qi * 97:(qi + 1) * 97], Pc[0:127, qi * 128:(qi + 1) * 128], vcx[0:127, g, :], True, True, [Pc, vcx], [acc])
                        evac_branch(acc, 97, r, h, qc, 0, True)
                        if use_mask:
                            if r == 0:
                                k.tt("dve", imp[:], pov[:, :, 65:97], bc(rden[:], 32), ALU.mult, [acc, rden], [imp])
                            else:
                                k.tt("dve", tmpi[:], pov[:, :, 65:97], bc(rden[:], 32), ALU.mult, [acc, rden], [tmpi])
                                k.tt("pool", imp[:], imp[:], tmpi[:], ALU.add, [imp, tmpi], [imp])
                            if r == 3:
                                hookA()
                        return
                    j0 = max(0, kt - 4 * qc)
                    j1 = 3 if br == "sel" else min(3, kt + 4 - 4 * qc)
                    P = Pt[cn["pt"] % 4]
                    cn["pt"] += 1
                    k.act(P[:, j0 * 128:(j1 + 1) * 128], ps_[:, j0 * 128:(j1 + 1) * 128], AF.Exp, [ps_], [P], scale=0.125)
                    vt = vwg[i] if br == "win" else vsg[i]
                    for qi in range(j0, j1 + 1):
                        fst = cn["first"]
                        k.op("pe", lambda E: E.matmul(acc[:, qi * 65:(qi + 1) * 65], lhsT=P[:, qi * 128:(qi + 1) * 128], rhs=vt[:, kt, :],
                                                      start=fst, stop=(kt == 4 * qc + qi), skip_group_check=True), [P, vt], [acc])
                        cn["first"] = False
                    if is_last:
                        evac_branch(acc, 65, r, h, qc, 2 if br == "win" else 1, False)
                        if br == "sel":
                            k.tt("dve", gtok[gi][:, :, r * 64:(r + 1) * 64], oacc_h[:, r, :, :], zag[zi][:, :, r * 64:(r + 1) * 64], ALU.mult,
                                 [oacc[r], zag[zi]], [gtok[gi]])

                LA = 2
                pend = [score(items[n_]) for n_ in range(min(LA, len(items)))]
                for n_, it in enumerate(items):
                    ps_ = pend.pop(0)
                    if n_ + LA < len(items):
                        pend.append(score(items[n_ + LA]))
                    post(it, ps_)
                pbt = nextB()
                for j in range(2):
                    for qi in range(4):
                        c0 = (j * 4 + qi) * 128
                        k.tr(pbt[:, c0:c0 + 128], gtok[gi][:, qi, j * 128:(j + 1) * 128], ident[:], [gtok[gi], ident], [pbt])
                k.cp("act", gatedAT.h[:, 2 * g:2 * g + 2, qs], pbt[:, :].rearrange("p (j q) -> p j q", j=2), [pbt], [gAT])
        k.barrier()

    if "gA" in debug:
        o = dbg("d_gAT", [128, 8, S], BF16)
        k.dma("sp", o[:, :, :], gatedAT.h[:], o, [gAT])
    if stop_after == "P4":
        return _finish(nc, k, yT, dbg_out, SC, debug)

    gBT = T(gatedBT.h, "gatedBT")
    with ExitStack() as s5:
        kbT = [k.sb(f"kbT{i}", [128, S], BF16, s5) for i in range(2)]
        qbT = [k.sb(f"qbT{i}", [128, S], BF16, s5) for i in range(2)]
        vbh = [k.sb(f"vbh{i}", [128, 16, 129], BF16, s5) for i in range(2)]
        for i in range(2):
            k.ms("pool", vbh[i][:, :, 128:129], 1.0, [vbh[i]])
        zbg = [k.sb(f"zbg{i}", [128, 4, 128], BF16, s5) for i in range(2)]
        Pt = [k.sb(f"PtB{i}", [128, 512], BF16, s5) for i in range(6)]
        gtb = [k.sb(f"gtb{i}", [128, 4, 128], BF16, s5) for i in range(2)]
        tmpb = [k.sb(f"tmpb{i}", [128, 2, 128], F32, s5) for i in range(2)]
        rdb = [k.sb(f"rdb{i}", [128, 2], F32, s5) for i in range(2)]
        cn = {"pt": 0, "tb": 0, "first": [True, True]}
        SCB = 128.0 ** -0.5

        def load_head(h):
            i = h % 2
            k.dma("sp", kbT[i][:, :], SC["KBT"].h[h * 128:(h + 1) * 128, :], kbT[i], [SC["KBT"]])
            k.dma("sp", qbT[i][:, :], SC["QBT"].h[h * 128:(h + 1) * 128, :], qbT[i], [SC["QBT"]])
            k.dma("sp", vbh[i][:, :, 0:128], SC["VB"].h[:, h * 128:(h + 1) * 128].rearrange("(kt p) d -> p kt d", p=128), vbh[i], [SC["VB"]])

        load_head(0)
        for h in range(8):
            i = h % 2
            if h + 1 < 8:
                load_head(h + 1)
            items = []
            for qc in range(4):
                for kt in range(4 * qc + 4):
                    items.append((qc, kt))

            def score(it):
                qc, kt = it
                ps_ = nextA()
                k.mm(ps_[:], kbT[i][:, kt * 128:(kt + 1) * 128], qbT[i][:, qc * 512:(qc + 1) * 512], True, True, [kbT[i], qbT[i]], [ps_])
                return ps_

            def post(it, ps_):
                qc, kt = it
                zi = (h * 4 + qc) % 2
                qs = slice(qc * 512, (qc + 1) * 512)
                banks = [PA[3 + 2 * (qc % 2)], PA[4 + 2 * (qc % 2)]]
                if kt == 0:
                    k.dma("sp", zbg[zi][:], SC["ZB"].h[qc * 512:(qc + 1) * 512, h * 128:(h + 1) * 128].rearrange("(qi p) c -> p qi c", p=128),
                          zbg[zi], [SC["ZB"]])
                    cn["first"] = [True, True]
                first = cn["first"]
                j0 = max(0, kt - 4 * qc)
                P = Pt[cn["pt"] % 6]
                cn["pt"] += 1
                k.act(P[:, j0 * 128:512], ps_[:, j0 * 128:512], AF.Exp, [ps_], [P], scale=SCB)
                off = 512 * qc - 128 * kt + 384 + j0 * 128
                k.tt("dve", P[:, j0 * 128:512], P[:, j0 * 128:512], toep[:, off:off + (4 - j0) * 128], ALU.mult,
                     [P, toep], [P])
                for qi in range(j0, 4):
                    b_ = qi // 2
                    c0 = (qi % 2) * 129
                    fst = first[b_]
                    k.op("pe", lambda E: E.matmul(banks[b_][:, c0:c0 + 129], lhsT=P[:, qi * 128:(qi + 1) * 128], rhs=vbh[i][:, kt, :],
                                                  start=fst, stop=(kt == 4 * qc + qi), skip_group_check=True), [P, vbh[i]], [banks[b_]])
                    first[b_] = False
                if kt == 4 * qc + 3:
                    for b_ in range(2):
                        bv = banks[b_][:, 0:258].rearrange("p (q c) -> p q c", c=129)
                        rd = rdb[cn["tb"] % 2]
                        tb = tmpb[cn["tb"] % 2]
                        cn["tb"] += 1
                        k.op("dve", lambda E: E.reciprocal(out=v3(rd[:]), in_=bv[:, :, 128:129]), [banks[b_]], [rd])
                        k.tt("dve", tb[:], bv[:, :, 0:128], bc(rd[:], 128), ALU.mult, [banks[b_], rd], [tb])
                        k.tt("pool", gtb[zi][:, 2 * b_:2 * b_ + 2, :], tb[:], zbg[zi][:, 2 * b_:2 * b_ + 2, :], ALU.mult, [tb, zbg[zi]], [gtb[zi]])
                    pbt = nextB()
                    for qi in range(4):
                        k.tr(pbt[:, qi * 128:(qi + 1) * 128], gtb[zi][:, qi, :], ident[:], [gtb[zi], ident], [pbt])
                    k.cp("act", gatedBT.h[:, h, qs], pbt[:, 0:512], [pbt], [gBT])

            LA = 2
            pend = [score(items[n_]) for n_ in range(min(LA, len(items)))]
            for n_, it in enumerate(items):
                ps_ = pend.pop(0)
                if n_ + LA < len(items):
                    pend.append(score(items[n_ + LA]))
                post(it, ps_)
        k.barrier()

    if "gB" in debug:
        o = dbg("d_gBT", [128, 8, S], BF16)
        k.dma("sp", o[:, :, :], gatedBT.h[:], o, [gBT])
    if stop_after == "P5":
        return _finish(nc, k, yT, dbg_out, SC, debug)

    gAT = T(gatedAT.h, "gatedAT2")
    with ExitStack() as s6o:
        wo = k.sb("wo", [128, 16, D], BF16, s6o)
        wov = I["w_out"].rearrange("(kc p) n -> p kc n", p=128)
        for cb in range(4):
            k.dma("pool", wo[:, :, cb * 512:(cb + 1) * 512], wov[:, :, cb * 512:(cb + 1) * 512], wo)
        with ExitStack() as s6:
            wj = [[k.sb(f"wj{a}{i}", [128, 8, 128], BF16, s6) for i in range(2)] for a in range(2)]
            sm = [[k.sb(f"sm{a}{i}", [128, 512], BF16, s6) for i in range(2)] for a in range(2)]
            m1 = [k.sb(f"m1_{i}", [128, 512], F32, s6) for i in range(2)]
            m2 = [k.sb(f"m2_{i}", [128, 512], F32, s6) for i in range(2)]
            m16 = [k.sb(f"m16_{i}", [128, 512], BF16, s6) for i in range(2)]
            wv = [I["w_br_a"].rearrange("(wc p) n -> p wc n", p=128), I["w_br_b"].rearrange("(wc p) n -> p wc n", p=128)]
            gsrc = [(gatedAT.h, gAT), (gatedBT.h, gBT)]
            smn = ["SMA", "SMB"]
            def load_wj(j):
                for a in range(2):
                    k.dma("pool", wj[a][j % 2][:], wv[a][:, :, j * 128:(j + 1) * 128], wj[a][j % 2])
            load_wj(0)
            for j in range(16):
                i = j % 2
                if j + 1 < 16:
                    load_wj(j + 1)
                for tc_ in range(4):
                    ii = (j * 4 + tc_) % 2
                    tcs = slice(tc_ * 512, (tc_ + 1) * 512)
                    pss = []
                    for a in range(2):
                        k.dma("sp", sm[a][ii][:], SC[smn[a]].h[j * 128:(j + 1) * 128, tcs], sm[a][ii], [SC[smn[a]]])
                        ps = nextA()
                        for wc in range(8):
                            k.mm(ps[:], wj[a][i][:, wc, :], gsrc[a][0][:, wc, tcs], wc == 0, wc == 7, [wj[a][i], gsrc[a][1]], [ps])
                        pss.append(ps)
                    k.tt("dve", m1[ii][:], pss[0][:], sm[0][ii][:], ALU.mult, [pss[0], sm[0][ii]], [m1[ii]])
                    k.tt("dve", m2[ii][:], pss[1][:], sm[1][ii][:], ALU.mult, [pss[1], sm[1][ii]], [m2[ii]])
                    k.tt("pool", m16[ii][:], m1[ii][:], m2[ii][:], ALU.add, [m1[ii], m2[ii]], [m16[ii]])
                    k.dma("act", SC["MT"].h[j * 128:(j + 1) * 128, tcs], m16[ii][:], SC["MT"], [m16[ii]])
            k.barrier()
        with ExitStack() as s7:
            mt = [k.sb(f"mt{i}", [128, 16, 128], BF16, s7) for i in range(2)]
            xt = [k.sb(f"xt{i}", [128, D], F32, s7) for i in range(2)]
            yt = [k.sb(f"yt{i}", [128, D], F32, s7) for i in range(2)]
            ssq4 = k.sb("ssq4", [128, 4], F32, s7)
            junk = k.sb("junk6", [128, 512], BF16, s7)
            rs = k.sb("rs6", [128, 1], F32, s7)
            def load6(tt):
                i = tt % 2
                k.dma("sp", mt[i][:], SC["MT"].h[:, tt * 128:(tt + 1) * 128].rearrange("(kc p) t -> p kc t", p=128), mt[i], [SC["MT"]])
                k.dma("sp", xt[i][:], I["x"][tt * 128:(tt + 1) * 128, :], xt[i])
            load6(0)
            for tt in range(NT):
                i = tt % 2
                if tt + 1 < NT:
                    load6(tt + 1)
                k.ms("dve", ssq4[:], 0.0, [ssq4])
                for cb in range(4):
                    ps = PA[cb]
                    for kc in range(16):
                        k.mm(ps[:], mt[i][:, kc, :], wo[:, kc, cb * 512:(cb + 1) * 512], kc == 0, kc == 15, [mt[i], wo], [ps])
                    k.act(junk[:], ps[:], AF.Square, [ps, ssq4], [junk, ssq4], accum=ssq4[:, cb:cb + 1])
                k.op("dve", lambda E: E.reduce_sum(out=rs[:], in_=ssq4[:], axis=X), [ssq4], [rs])
                k.act(rs[:], rs[:], AF.Ln, [rs, eps_t], [rs], bias=eps_t[:, 0:1], scale=1.0 / D)
                k.act(rs[:], rs[:], AF.Exp, [rs], [rs], scale=-0.5)
                for cb in range(4):
                    cs = slice(cb * 512, (cb + 1) * 512)
                    k.stt("dve", yt[i][:, cs], PA[cb][:], rs[:, 0:1], gg_bc[:, cs], ALU.mult, ALU.mult, [PA[cb], rs, gg_bc], [yt[i]])
                k.tt("pool", yt[i][:], yt[i][:], xt[i][:], ALU.add, [yt[i], xt[i]], [yt[i]])
                k.dma("sp", y_out[tt * 128:(tt + 1) * 128, :], yt[i][:], yT, [yt[i]])
            k.barrier()
    return _finish(nc, k, yT, dbg_out, SC, debug)
```

```python
import numpy as np
from contextlib import ExitStack
import ml_dtypes
import concourse.bass as bass
import concourse.mybir as mybir
from concourse.bass_utils import run_bass_kernel_spmd

F32 = mybir.dt.float32
BF16 = mybir.dt.bfloat16
I32 = mybir.dt.int32
AF = mybir.ActivationFunctionType
ALU = mybir.AluOpType

D = 2048
S = 2048
NT = 16
N_IN = 11824
EPS = 1e-6
BIG = 30000.0
THETA = 500000.0
C_QA, C_KC, C_VC, C_KS, C_VS, C_KW, C_VW, C_GA, C_ZA = 0, 1024, 1280, 1536, 1792, 2048, 2304, 2560, 2608
C_QB, C_KB, C_VB, C_ZB, C_MA, C_MB = 3632, 4656, 5680, 6704, 7728, 9776
NC_CMP = 127


class T:
    __slots__ = ("h", "name", "w", "r", "dsem", "dcnt", "excl")

    def __init__(self, h, name="", excl=False):
        self.h = h
        self.name = name
        self.excl = excl
        self.w = {}
        self.r = {}
        self.dsem = None
        self.dcnt = 0

    def __getitem__(self, idx):
        return self.h[idx]


class K:
    ROT = 20000

    def __init__(self, nc, stack):
        self.nc = nc
        self.stack = stack
        self.eng = {"pe": nc.tensor, "act": nc.scalar, "dve": nc.vector, "pool": nc.gpsimd, "sp": nc.sync}
        self.sem = {}
        self.cnt = {}
        self.waited = {e: {} for e in self.eng}
        self.nsem = 0
        self.all_dma = {}
        for e in self.eng:
            self.sem[e] = self.new_sem("e_" + e)
            self.cnt[e] = 0
        self.ninst = {e: 0 for e in self.eng}

    def new_sem(self, name):
        self.nsem += 1
        return self.stack.enter_context(self.nc.semaphore(f"{name}_{self.nsem}"))

    def sb(self, name, shape, dt, stack=None):
        return T((stack or self.stack).enter_context(self.nc.sbuf_tensor(name, list(shape), dt)), name)

    def ps(self, name, shape, dt=F32):
        return T(self.stack.enter_context(self.nc.psum_tensor(name, list(shape), dt)), name, excl=True)

    def alias(self, t, name=""):
        return T(t.h, name or t.name)

    def _wait(self, e, deps):
        need = {}
        for d in deps:
            s, v, src = d
            if e == "pe" and src == "pe":
                continue
            kk = id(s)
            if kk not in need or need[kk][1] < v:
                need[kk] = (s, v)
        for kk, (s, v) in need.items():
            if self.waited[e].get(kk, 0) < v:
                self.eng[e].wait_ge(s, v)
                self.waited[e][kk] = v

    @staticmethod
    def _deps(reads, writes):
        deps = []
        for t in reads:
            deps.extend(t.w.values())
            if t.excl:
                deps.extend(t.r.values())
        for t in writes:
            deps.extend(t.w.values())
            deps.extend(t.r.values())
        return deps

    def op(self, e, fn, reads=(), writes=()):
        self._wait(e, self._deps(reads, writes))
        inst = fn(self.eng[e])
        if self.cnt[e] >= self.ROT:
            self.sem[e] = self.new_sem("e_" + e)
            self.cnt[e] = 0
        self.cnt[e] += 1
        self.ninst[e] += 1
        inst.then_inc(self.sem[e], 1)
        d = (self.sem[e], self.cnt[e], e)
        for t in reads:
            t.r[id(d[0])] = d
        for t in writes:
            t.w[id(d[0])] = d
        return inst

    def dma(self, q, out, in_, dst, srcs=(), **kw):
        self._wait(q, self._deps(srcs, (dst,)))
        if dst.dsem is None or dst.dcnt >= 1500:
            dst.dsem = self.new_sem("d_" + dst.name)
            dst.dcnt = 0
        inst = self.eng[q].dma_start(out=out, in_=in_, **kw)
        dst.dcnt += 1
        inst.then_inc(dst.dsem, 16)
        d = (dst.dsem, 16 * dst.dcnt, "dma")
        for t in srcs:
            t.r[id(d[0])] = d
        dst.w[id(d[0])] = d
        self.all_dma[id(d[0])] = d
        return d

    def barrier(self):
        deps = [(self.sem[e], self.cnt[e], "bar") for e in self.eng if self.cnt[e] > 0]
        deps += list(self.all_dma.values())
        for e in self.eng:
            self._wait(e, [d for d in deps if not (d[0] is self.sem[e])])

    def tt(self, e, out, in0, in1, op, R, W):
        return self.op(e, lambda E: E.tensor_tensor(out=out, in0=in0, in1=in1, op=op), R, W)

    def ts(self, e, out, in0, s1, op0, R, W, s2=None, op1=None):
        if op1 is None:
            return self.op(e, lambda E: E.tensor_scalar(out=out, in0=in0, scalar1=s1, scalar2=None, op0=op0), R, W)
        return self.op(e, lambda E: E.tensor_scalar(out=out, in0=in0, scalar1=s1, scalar2=s2, op0=op0, op1=op1), R, W)

    def stt(self, e, out, in0, scalar, in1, op0, op1, R, W):
        return self.op(e, lambda E: E.scalar_tensor_tensor(out=out, in0=in0, scalar=scalar, in1=in1, op0=op0, op1=op1), R, W)

    def act(self, out, in_, func, R, W, bias=None, scale=None, accum=None):
        kw = {}
        if bias is not None:
            kw["bias"] = bias
        if scale is not None:
            kw["scale"] = scale
        if accum is not None:
            kw["accum_out"] = accum
        return self.op("act", lambda E: E.activation(out=out, in_=in_, func=func, **kw), R, W)

    def cp(self, e, out, in_, R, W):
        if e == "act":
            return self.op("act", lambda E: E.copy(out=out, in_=in_), R, W)
        return self.op(e, lambda E: E.tensor_copy(out=out, in_=in_), R, W)

    def mm(self, out, lhsT, rhs, start, stop, R, W):
        return self.op("pe", lambda E: E.matmul(out, lhsT=lhsT, rhs=rhs, start=start, stop=stop), R, W)

    def tr(self, out, in_, ident, R, W):
        return self.op("pe", lambda E: E.transpose(out=out, in_=in_, identity=ident), R, W)

    def ms(self, e, ap, val, W):
        return self.op(e, lambda E: E.memset(ap, val), (), W)


def _consts():
    bf = ml_dtypes.bfloat16
    c = {}
    c["ident"] = np.eye(128, dtype=np.float32).astype(bf)
    kl = np.arange(128)[:, None]
    ql = np.arange(128)[None, :]
    c["tri_le"] = ((kl <= ql).astype(np.float32) * BIG - BIG).astype(bf)
    c["tri_gt"] = ((kl > ql).astype(np.float32) * BIG - BIG).astype(bf)
    d = np.arange(2816)[None, :] - kl - 384
    M = ((d >= 0) & (d <= 128)).astype(np.float32) + ((d >= 0) & (d % 4 == 0) & (d <= 512)) + ((d >= 0) & (d % 16 == 0))
    c["toep"] = M.astype(np.float32).astype(bf)
    cend = np.arange(127) * 16 + 31
    cm = np.zeros((128, S), np.float32)
    cm[:127] = (cend[:, None] <= np.arange(S)[None, :])
    c["cmaskT"] = cm.astype(bf)
    cs = np.arange(127) * 16
    ss = np.arange(32) * 64
    ov = np.clip(np.minimum(cs[:, None] + 32, ss[None, :] + 64) - np.maximum(cs[:, None], ss[None, :]), 0, None).astype(np.float32) / 32
    vx = np.zeros((128, 33), np.float32)
    vx[:127, 0] = 1.0
    vx[:127, 1:] = ov
    c["vcx_tail"] = vx.astype(bf)
    E = (np.arange(S)[None, :] // 64 == np.arange(32)[:, None]).astype(np.float32)
    c["emat"] = E.astype(bf)
    t = np.arange(S)
    jb = np.arange(32)[None, :]
    cur = (t // 64)[:, None]
    valid = ss[None, :] <= t[:, None]
    forced = valid & ((jb == 0) | (jb == cur) | (jb == cur - 1))
    add = np.where(valid, 1.0e4 * forced, -1.0e30).astype(np.float32)
    c["addtab"] = np.ascontiguousarray(add.reshape(NT, 128, 32).transpose(1, 0, 2))
    invA = (THETA ** (-np.arange(0, 16, 2) / 16)).astype(np.float32)
    invB = (THETA ** (-np.arange(0, 32, 2) / 32)).astype(np.float32)
    c["invA"] = np.ascontiguousarray(np.broadcast_to(invA, (128, 8)))
    c["invB"] = np.ascontiguousarray(np.broadcast_to(invB, (128, 16)))
    return c


_CONST_SPECS = {
    "ident": ([128, 128], BF16), "tri_le": ([128, 128], BF16), "tri_gt": ([128, 128], BF16),
    "toep": ([128, 2816], BF16), "cmaskT": ([128, S], BF16), "vcx_tail": ([128, 33], BF16),
    "emat": ([32, S], BF16), "addtab": ([128, NT, 32], F32), "invA": ([128, 8], F32), "invB": ([128, 16], F32),
}

_IN_SPECS = {
    "x": ([S, D], F32), "c_l": ([128, 16], F32), "pos_l": ([128, NT], I32), "posc": ([128, 1], I32),
    "w_ada": ([D, 3 * D], F32), "b_sh": ([1, D], F32), "b_sc": ([1, D], F32), "b_gate": ([1, D], F32),
    "g_pre_r": ([1, D], F32), "g_post": ([1, D], F32), "w_in": ([D, N_IN], F32),
    "pe_ckT": ([64, 32], F32), "pe_cvT": ([64, 32], F32), "w_ck1": ([2048, 256], F32), "w_ck2": ([256, 64], F32),
    "w_cv1": ([2048, 256], F32), "w_cv2": ([256, 64], F32), "w_br_a": ([1024, D], F32), "w_br_b": ([1024, D], F32),
    "w_out": ([D, D], F32),
}

_SCRATCH = {
    "QAT": ([1024, S], BF16), "KST": ([256, S], BF16), "KWT": ([256, S], BF16), "QBT": ([1024, S], BF16),
    "KBT": ([1024, S], BF16), "VS": ([S, 256], BF16), "VW": ([S, 256], BF16), "VB": ([S, 1024], BF16),
    "ZA": ([S, 1024], BF16), "ZB": ([S, 1024], BF16), "SMA": ([D, S], BF16), "SMB": ([D, S], BF16),
    "MT": ([D, S], BF16),
}


def build_nc(debug=None, stop_after=None, start_from=None):
    debug = debug or ()
    nc = bass.Bass("TRN2", target_bir_lowering=False)
    I = {n: nc.dram_tensor(n, sh, dt, kind="ExternalInput").ap() for n, (sh, dt) in {**_IN_SPECS, **_CONST_SPECS}.items()}
    y_out = nc.dram_tensor("y", [S, D], F32, kind="ExternalOutput").ap()
    SC = {}
    for n, (sh, dt) in _SCRATCH.items():
        kind = "ExternalOutput" if n in debug else "Internal"
        if start_from and n != "MT":
            kind = "ExternalInput"
        SC[n] = T(nc.dram_tensor(n, sh, dt, kind=kind).ap(), n)
    dbg_out = {}

    def dbg(name, shape, dt=F32):
        dbg_out[name] = T(nc.dram_tensor(name, shape, dt, kind="ExternalOutput").ap(), name)
        return dbg_out[name]

    with ExitStack() as st:
        k = K(nc, st)
        yT = T(y_out, "y")

        def const(name, q="sp"):
            sh, dt = _CONST_SPECS[name]
            t = k.sb("c_" + name, sh, dt)
            k.dma(q, t[:], I[name][:], t)
            return t

        ident = const("ident")
        tri_le = const("tri_le")
        tri_gt = const("tri_gt")
        toep = const("toep")
        cmaskT = const("cmaskT")
        addtab = const("addtab")
        invA = const("invA")
        invB = const("invB")

        PA = [k.ps(f"pa{i}", [128, 512], F32) for i in range(7)]
        PB = [k.ps(f"pb{i}", [128, 1024], BF16) for i in range(1)]
        pa_i = [0]
        pb_i = [0]

        def nextA(n=3):
            pa_i[0] = (pa_i[0] + 1) % n
            return PA[pa_i[0]]

        def nextB(n=1):
            pb_i[0] = (pb_i[0] + 1) % n
            return PB[pb_i[0]]

        G = k.sb("G", [128, NT, 48], F32)
        gg_bc = k.sb("gg_bc", [128, D], F32)
        cosA = k.sb("cosA", [128, NT, 8], F32)
        sinA = k.sb("sinA", [128, NT, 8], F32)
        cosB = k.sb("cosB", [128, NT, 16], F32)
        sinB = k.sb("sinB", [128, NT, 16], F32)
        cosC = k.sb("cosC", [128, 8], F32)
        sinC = k.sb("sinC", [128, 8], F32)
        kcmpT = k.sb("kcmpT", [64, 4, 128], BF16)
        vcx = k.sb("vcx", [128, 4, 97], BF16)
        k.ms("pool", kcmpT[:], 0.0, [kcmpT])
        k.ms("pool", vcx[:], 0.0, [vcx])
        eps_t = k.sb("eps_t", [128, 1], F32)
        k.ms("dve", eps_t[:], EPS, [eps_t])

        def sincos(pos_i32, ncol, inv, cos_t, sin_t, tagn, stk):
            nf = inv.h.shape[1]
            posf = k.sb("posf" + tagn, [128, ncol], F32, stk)
            k.cp("dve", posf[:], pos_i32[:], [pos_i32], [posf])
            ang = k.sb("ang" + tagn, [128, ncol, nf], F32, stk)
            k.tt("dve", ang[:], posf[:].rearrange("p (c o) -> p c o", o=1).to_broadcast([128, ncol, nf]),
                 inv[:].rearrange("p (o f) -> p o f", o=1).to_broadcast([128, ncol, nf]), ALU.mult, [posf, inv], [ang])
            for which, outt in ((0, sin_t), (1, cos_t)):
                a2 = k.sb(f"a2{tagn}{which}", [128, ncol, nf], F32, stk)
                if which == 1:
                    k.ts("dve", a2[:], ang[:], float(np.pi / 2), ALU.add, [ang], [a2])
                    src = a2
                else:
                    src = ang
                u = k.sb(f"u{tagn}{which}", [128, ncol, nf], F32, stk)
                k.ts("dve", u[:], src[:], float(1 / (2 * np.pi)), ALU.mult, [src], [u])
                ki = k.sb(f"ki{tagn}{which}", [128, ncol, nf], I32, stk)
                k.cp("dve", ki[:], u[:], [u], [ki])
                kf = k.sb(f"kf{tagn}{which}", [128, ncol, nf], F32, stk)
                k.cp("dve", kf[:], ki[:], [ki], [kf])
                rr = k.sb(f"rr{tagn}{which}", [128, ncol, nf], F32, stk)
                k.stt("dve", rr[:], kf[:], -float(2 * np.pi), src[:], ALU.mult, ALU.add, [kf, src], [rr])
                m = k.sb(f"m{tagn}{which}", [128, ncol, nf], F32, stk)
                k.ts("dve", m[:], rr[:], float(np.pi), ALU.is_gt, [rr], [m], s2=-float(2 * np.pi), op1=ALU.mult)
                k.tt("dve", rr[:], rr[:], m[:], ALU.add, [rr, m], [rr])
                k.ts("dve", m[:], rr[:], -float(np.pi), ALU.is_lt, [rr], [m], s2=float(2 * np.pi), op1=ALU.mult)
                k.tt("dve", rr[:], rr[:], m[:], ALU.add, [rr, m], [rr])
                k.ts("dve", rr[:], rr[:], 3.14159, ALU.min, [rr], [rr], s2=-3.14159, op1=ALU.max)
                k.act(outt[:] if len(outt.h.shape) == 3 else outt[:].rearrange("p (c f) -> p c f", c=1),
                      rr[:], AF.Sin, [rr], [outt])

        if start_from:
            dI = {n: nc.dram_tensor(n, sh, dt, kind="ExternalInput").ap() for n, (sh, dt) in
                  {"d_G": ([128, NT, 48], F32), "d_kcvT": ([8, 64, S], BF16), "d_gg": ([128, D], F32)}.items()}
            k.dma("sp", G[:], dI["d_G"][:, :, :], G)
            k.dma("sp", gg_bc[:], dI["d_gg"][:, :], gg_bc)
            sA = st.enter_context(ExitStack())
            kcvT = [k.sb(f"kcvT{m}", [64, S], BF16, sA) for m in range(8)]
            for m in range(8):
                k.dma("sp", kcvT[m][:], dI["d_kcvT"][m], kcvT[m])
            with ExitStack() as s0:
                posc_t = k.sb("posc_t", [128, 1], I32, s0)
                k.dma("sp", posc_t[:], I["posc"][:], posc_t)
                sincos(posc_t, 1, invA, cosC, sinC, "C", s0)
                k.barrier()
            return _tail(nc, k, st, I, SC, yT, dbg, dbg_out, debug, stop_after, locals())

        sA = st.enter_context(ExitStack())
        kcvT = [k.sb(f"kcvT{m}", [64, S], BF16, sA) for m in range(8)]
        sB = st.enter_context(ExitStack())
        hT_h = sB.enter_context(nc.sbuf_tensor("hT", [128, 16, S], BF16))
        hT = [T(hT_h, f"hT{tt}") for tt in range(NT)]
        sX = st.enter_context(ExitStack())
        gs_bc = k.sb("gs_bc", [128, D], F32, sX)
        sh_bc = k.sb("sh_bc", [128, D], F32, sX)

        with ExitStack() as s0:
            pos_t = k.sb("pos_t", [128, NT], I32, s0)
            k.dma("sp", pos_t[:], I["pos_l"][:], pos_t)
            posc_t = k.sb("posc_t", [128, 1], I32, s0)
            k.dma("sp", posc_t[:], I["posc"][:], posc_t)
            c32 = k.sb("c32", [128, 16], F32, s0)
            k.dma("sp", c32[:], I["c_l"][:], c32)
            c_bc = k.sb("c_bc", [128, 16, 128], BF16, s0)
            k.cp("dve", c_bc[:], c32[:].rearrange("p (k o) -> p k o", o=1).to_broadcast([128, 16, 128]), [c32], [c_bc])
            wa = [k.sb(f"wa{i}", [128, 16, 512], BF16, s0) for i in range(2)]
            r1 = [k.sb(f"r1_{i}", [128, 512], F32, s0) for i in range(2)]
            r2 = [k.sb(f"r2_{i}", [128, 512], F32, s0) for i in range(2)]
            wada_v = I["w_ada"].rearrange("(kc p) n -> p kc n", p=128)

            def load_wa(blk):
                k.dma("pool", wa[blk % 2][:], wada_v[:, :, blk * 512:(blk + 1) * 512], wa[blk % 2])
            load_wa(0)
            for blk in range(12):
                w = wa[blk % 2]
                if blk + 1 < 12:
                    load_wa(blk + 1)
                kind, cb = blk // 4, blk % 4
                cs = slice(cb * 512, (cb + 1) * 512)
                a1, a2 = r1[blk % 2], r2[blk % 2]
                k.dma("sp", a1[:], I[("b_sh", "b_sc", "b_gate")[kind]][0:1, cs].to_broadcast([128, 512]), a1)
                if kind >= 1:
                    k.dma("sp", a2[:], I[("g_pre_r", "g_post")[kind - 1]][0:1, cs].to_broadcast([128, 512]), a2)
                pg = nextA()
                for kc in range(16):
                    k.mm(pg[:], c_bc[:, kc, :], w[:, kc, :], kc == 0, kc == 15, [w, c_bc], [pg])
                if kind == 0:
                    k.tt("dve", sh_bc[:, cs], pg[:], a1[:], ALU.add, [pg, a1], [sh_bc])
                elif kind == 1:
                    k.tt("dve", gs_bc[:, cs], pg[:], a1[:], ALU.add, [pg, a1], [gs_bc])
                    k.stt("dve", gs_bc[:, cs], gs_bc[:, cs], 1.0, a2[:], ALU.add, ALU.mult, [gs_bc, a2], [gs_bc])
                else:
                    k.tt("dve", gg_bc[:, cs], pg[:], a1[:], ALU.add, [pg, a1], [gg_bc])
                    k.tt("dve", gg_bc[:, cs], gg_bc[:, cs], a2[:], ALU.mult, [gg_bc, a2], [gg_bc])
            sincos(pos_t, NT, invA, cosA, sinA, "A", s0)
            sincos(pos_t, NT, invB, cosB, sinB, "B", s0)
            sincos(posc_t, 1, invA, cosC, sinC, "C", s0)
            k.barrier()

        with ExitStack() as s1:
            xb = [k.sb(f"xb{i}", [128, D], F32, s1) for i in range(3)]
            xs = [k.sb(f"xs{i}", [128, D], F32, s1) for i in range(2)]
            xn = [k.sb(f"xn{i}", [128, D], BF16, s1) for i in range(2)]
            junk = k.sb("junk", [128, D], BF16, s1)
            ssq = k.sb("ssq", [128, NT], F32, s1)
            rstd = k.sb("rstd", [128, NT], F32, s1)
            k.ms("dve", ssq[:], 0.0, [ssq])
            def stA(tt):
                xt = xb[tt % 3]
                k.dma("sp", xt[:], I["x"][tt * 128:(tt + 1) * 128, :], xt)
                k.act(junk[:], xt[:], AF.Square, [xt, ssq], [junk, ssq], accum=ssq[:, tt:tt + 1])
                k.act(rstd[:, tt:tt + 1], ssq[:, tt:tt + 1], AF.Ln, [ssq, eps_t], [rstd], bias=eps_t[:, 0:1], scale=1.0 / D)
                k.act(rstd[:, tt:tt + 1], rstd[:, tt:tt + 1], AF.Exp, [rstd], [rstd], scale=-0.5)

            def stB(tt):
                xt, xst, xnt = xb[tt % 3], xs[tt % 2], xn[tt % 2]
                k.stt("dve", xst[:], xt[:], rstd[:, tt:tt + 1], gs_bc[:], ALU.mult, ALU.mult, [xt, rstd, gs_bc], [xst])
                k.tt("dve", xnt[:], xst[:], sh_bc[:], ALU.add, [xst, sh_bc], [xnt])

            def stC(tt):
                xnt = xn[tt % 2]
                for half in range(2):
                    pb = nextB()
                    for j in range(8):
                        kc = half * 8 + j
                        k.tr(pb[:, j * 128:(j + 1) * 128], xnt[:, kc * 128:(kc + 1) * 128], ident[:], [xnt, ident], [pb])
                    k.cp("act", hT_h[:, half * 8:(half + 1) * 8, tt * 128:(tt + 1) * 128],
                         pb[:, :].rearrange("p (j t) -> p j t", t=128), [pb], [hT[tt]])

            stA(0)
            stA(1)
            stB(0)
            for tt in range(NT):
                if tt + 2 < NT:
                    stA(tt + 2)
                if tt + 1 < NT:
                    stB(tt + 1)
                stC(tt)
            k.barrier()
        sX.close()

        if "hT" in debug:
            o = dbg("d_hT", [128, 16, S], BF16)
            k.dma("sp", o[:, :, :], hT_h[:], o, hT)

        if stop_after == "P1":
            return _finish(nc, k, yT, dbg_out, SC, debug)

        s2 = st.enter_context(ExitStack())
        wbf = [k.sb(f"wbf{i}", [128, 16, 512], BF16, s2) for i in range(3)]
        win_v = I["w_in"].rearrange("(kc p) n -> p kc n", p=128)
        blocks = []
        for i in range(2):
            blocks.append((C_QA + 512 * i, 512, "qa", i))
        blocks.append((C_KS, 512, "ksvs", 0))
        blocks.append((C_KW, 512, "kwvw", 0))
        blocks.append((C_GA, 48, "ga", 0))
        for i in range(2):
            blocks.append((C_ZA + 512 * i, 512, "za", i))
        for i in range(2):
            blocks.append((C_QB + 512 * i, 512, "qb", i))
        for i in range(2):
            blocks.append((C_KB + 512 * i, 512, "kb", i))
        for i in range(2):
            blocks.append((C_VB + 512 * i, 512, "vb", i))
        for i in range(2):
            blocks.append((C_ZB + 512 * i, 512, "zb", i))
        for i in range(4):
            blocks.append((C_MA + 512 * i, 512, "ma", i))
        for i in range(4):
            blocks.append((C_MB + 512 * i, 512, "mb", i))
        blocks.append((C_KC, 512, "kcvc", 0))
        if stop_after and stop_after.startswith("R:"):
            blocks = [b for b in blocks if b[2] in stop_after[2:].split("+")]
        if stop_after == "P2a":
            blocks = [b for b in blocks if b[2] in ("qa", "ksvs", "ga", "za")]
        if stop_after == "P2b":
            blocks = [b for b in blocks if b[2] in ("qb", "vb", "ma", "kcvc")]

        def load_w(bi):
            col0, ncols, _, _ = blocks[bi]
            w = wbf[bi % 3]
            k.dma("pool", w[:, :, 0:ncols], win_v[:, :, col0:col0 + ncols], w)

        tok16 = [k.sb(f"tok16_{i}", [128, 512], BF16, s2) for i in range(2)]
        rt = [k.sb(f"rt{i}", [128, 4, 128], F32, s2) for i in range(2)]
        stgT = [k.sb(f"stgT{i}", [128, 4, 512], BF16, s2) for i in range(2)]
        fm16 = [k.sb(f"fm16_{i}", [128, 512], BF16, s2) for i in range(2)]
        cnt = {"tok": 0, "stg": 0, "fm": 0}

        import os as _os
        _DBG = _os.environ.get("KDBG", "")

        def rope(ps, out16, tt, nh, dh, cos_t, sin_t):
            if "norope" in _DBG:
                return
            half = dh // 8
            pv = ps[:, 0:nh * dh].rearrange("p (h d) -> p h d", d=dh)
            ov = out16[:, 0:nh * dh].rearrange("p (h d) -> p h d", d=dh)
            r = rt[cnt["tok"] % 2]
            n = nh * half
            cb = cos_t[:, tt, :].rearrange("p (o f) -> p o f", o=1).to_broadcast([128, nh, half])
            sb_ = sin_t[:, tt, :].rearrange("p (o f) -> p o f", o=1).to_broadcast([128, nh, half])
            tv = [r[:, i, 0:n].rearrange("p (h f) -> p h f", f=half) for i in range(4)]
            x1 = pv[:, :, 0:half]
            x2 = pv[:, :, half:2 * half]
            lvl = 6
            for i_ in range(1, 7):
                if f"rope{i_}" in _DBG:
                    lvl = i_
            if lvl >= 1:
                k.tt("dve", tv[0], x1, cb, ALU.mult, [ps, cos_t], [r])
            if lvl >= 2:
                k.tt("dve", tv[1], x2, sb_, ALU.mult, [ps, sin_t], [r])
            if lvl >= 3:
                k.tt("dve", tv[2], x2, cb, ALU.mult, [ps, cos_t], [r])
            if lvl >= 4:
                k.tt("dve", tv[3], x1, sb_, ALU.mult, [ps, sin_t], [r])
            if lvl >= 5:
                k.tt("dve", ov[:, :, 0:half], tv[0], tv[1], ALU.subtract, [r], [out16])
            if lvl >= 6:
                k.tt("dve", ov[:, :, half:2 * half], tv[2], tv[3], ALU.add, [r], [out16])

        def proj_tok(w, ncols, tt):
            ps = nextA()
            for kc in range(16):
                k.mm(ps[:, 0:ncols], hT_h[:, kc, tt * 128:(tt + 1) * 128], w[:, kc, 0:ncols], kc == 0, kc == 15, [hT[tt], w], [ps])
            return ps

        def transposes_to_stage(src16, ncol128, tt, stg):
            pb = nextB()
            for j in range(ncol128):
                k.tr(pb[:, j * 128:(j + 1) * 128], src16[:, j * 128:(j + 1) * 128], ident[:], [src16, ident], [pb])
            q4 = tt % 4
            eng = "act" if tt % 2 == 0 else "dve"
            k.cp(eng, stg[:, 0:ncol128, q4 * 128:(q4 + 1) * 128],
                 pb[:, 0:ncol128 * 128].rearrange("p (j t) -> p j t", t=128), [pb], [stg])

        load_w(0)
        if len(blocks) > 1:
            load_w(1)
        for bi, (col0, ncols, role, idx) in enumerate(blocks):
            if bi + 2 < len(blocks):
                load_w(bi + 2)
            w = wbf[bi % 3]
            if role in ("qa", "qb", "kb"):
                nh, dh, cos_t, sin_t = (8, 64, cosA, sinA) if role == "qa" else (4, 128, cosB, sinB)
                dstT = SC["QAT"] if role == "qa" else (SC["QBT"] if role == "qb" else SC["KBT"])
                deferred = []
                for tt in range(NT):
                    ps = proj_tok(w, 512, tt)
                    for f_ in deferred:
                        f_()
                    deferred = []
                    o16 = tok16[cnt["tok"] % 2]
                    k.cp("act", o16[:], ps[:], [ps], [o16])
                    rope(ps, o16, tt, nh, dh, cos_t, sin_t)
                    cnt["tok"] += 1

                    def fin(o16=o16, tt=tt):
                        stg = stgT[cnt["stg"] % 2]
                        transposes_to_stage(o16, 4, tt, stg)
                        if tt % 4 == 3:
                            tc_ = tt // 4
                            k.dma("sp", dstT.h[idx * 512:(idx + 1) * 512, tc_ * 512:(tc_ + 1) * 512].rearrange("(j p) t -> p j t", p=128),
                                  stg[:], dstT, [stg])
                            cnt["stg"] += 1
                    deferred.append(fin)
                for f_ in deferred:
                    f_()
            elif role in ("ksvs", "kwvw"):
                dstK = SC["KST"] if role == "ksvs" else SC["KWT"]
                dstV = SC["VS"] if role == "ksvs" else SC["VW"]
                deferred = []
                for tt in range(NT):
                    ps = proj_tok(w, 512, tt)
                    for f_ in deferred:
                        f_()
                    deferred = []
                    o16 = tok16[cnt["tok"] % 2]
                    k.cp("act", o16[:], ps[:], [ps], [o16])
                    rope(ps, o16, tt, 4, 64, cosA, sinA)
                    cnt["tok"] += 1

                    def fin(o16=o16, tt=tt):
                        stg = stgT[cnt["stg"] % 2]
                        transposes_to_stage(o16, 2, tt, stg)
                        k.dma("sp", dstV.h[tt * 128:(tt + 1) * 128, :], o16[:, 256:512], dstV, [o16])
                        if tt % 4 == 3:
                            tc_ = tt // 4
                            k.dma("sp", dstK.h[:, tc_ * 512:(tc_ + 1) * 512].rearrange("(j p) t -> p j t", p=128),
                                  stg[:, 0:2, :], dstK, [stg])
                            cnt["stg"] += 1
                    deferred.append(fin)
                for f_ in deferred:
                    f_()
            elif role == "ga":
                for tt in range(NT):
                    ps = proj_tok(w, 48, tt)
                    k.act(G[:, tt, :], ps[:, 0:48], AF.Sigmoid, [ps], [G])
            elif role in ("za", "zb", "vb"):
                dst = SC["ZA"] if role == "za" else (SC["ZB"] if role == "zb" else SC["VB"])
                for tt in range(NT):
                    ps = proj_tok(w, 512, tt)
                    o16 = tok16[cnt["tok"] % 2]
                    if role == "vb":
                        k.cp("act", o16[:], ps[:], [ps], [o16])
                    else:
                        k.act(o16[:], ps[:], AF.Silu, [ps], [o16])
                    cnt["tok"] += 1
                    k.dma("sp", dst.h[tt * 128:(tt + 1) * 128, idx * 512:(idx + 1) * 512], o16[:], dst, [o16])
            elif role in ("ma", "mb"):
                dst = SC["SMA"] if role == "ma" else SC["SMB"]
                for j in range(4):
                    for tc_ in range(4):
                        ps = nextA()
                        for kc in range(16):
                            k.mm(ps[:], w[:, kc, j * 128:(j + 1) * 128], hT_h[:, kc, tc_ * 512:(tc_ + 1) * 512], kc == 0, kc == 15,
                                 [hT[4 * tc_ + i] for i in range(4)] + [w], [ps])
                        o16 = fm16[cnt["fm"] % 2]
                        k.act(o16[:], ps[:], AF.Sigmoid, [ps], [o16])
                        cnt["fm"] += 1
                        r0 = idx * 512 + j * 128
                        k.dma("sp", dst.h[r0:r0 + 128, tc_ * 512:(tc_ + 1) * 512], o16[:], dst, [o16])
            elif role == "kcvc":
                for m in range(8):
                    for tc_ in range(4):
                        ps = nextA()
                        for kc in range(16):
                            k.mm(ps[0:64, :], w[:, kc, m * 64:(m + 1) * 64], hT_h[:, kc, tc_ * 512:(tc_ + 1) * 512], kc == 0, kc == 15,
                                 [hT[4 * tc_ + i] for i in range(4)] + [w], [ps])
                        k.cp("act" if (m + tc_) % 2 == 0 else "dve", kcvT[m][:, tc_ * 512:(tc_ + 1) * 512], ps[0:64, :], [ps], [kcvT[m]])
        k.barrier()
        s2.close()
        sB.close()

        if "kcvT" in debug:
            o = dbg("d_kcvT", [8, 64, S], BF16)
            for m in range(8):
                k.dma("sp", o.h[m], kcvT[m][:], o, [kcvT[m]])
        if "G" in debug:
            o = dbg("d_G", [128, NT, 48])
            k.dma("sp", o[:, :, :], G[:], o, [G])

        if stop_after in ("P2", "P2a", "P2b") or (stop_after and stop_after.startswith("R:")):
            sA.close()
            return _finish(nc, k, yT, dbg_out, SC, debug)
        return _tail(nc, k, st, I, SC, yT, dbg, dbg_out, debug, stop_after, locals())


def _finish(nc, k, yT, dbg_out, SC, debug):
    deps = list(yT.w.values())
    for o in dbg_out.values():
        deps.extend(o.w.values())
    for n in debug:
        if n in SC:
            deps.extend(SC[n].w.values())
    k._wait("sp", deps)
    return nc


def _core_inputs(b, x, c, positions, w_ada, b_ada, g_pre, g_post, w_in, pe_ck, pe_cv, w_ck1, w_ck2,
                 w_cv1, w_cv2, w_br_a, w_br_b, w_out, consts):
    f = np.ascontiguousarray
    lay = lambda v: f(np.asarray(v, np.float32).reshape(16, 128).T)
    pos = np.asarray(positions[b], np.int32)
    posc = np.zeros((128, 1), np.int32)
    posc[:127, 0] = pos[31::16][:127]
    m = {
        "x": f(x[b]), "c_l": lay(c[b]), "pos_l": f(pos.reshape(NT, 128).T), "posc": posc,
        "w_ada": f(w_ada[0]), "b_sh": f(b_ada[0, 0:D].reshape(1, D)), "b_sc": f(b_ada[0, D:2 * D].reshape(1, D)),
        "b_gate": f(b_ada[0, 2 * D:3 * D].reshape(1, D)), "g_pre_r": f(g_pre[0].reshape(1, D)), "g_post": f(g_post[0].reshape(1, D)),
        "w_in": f(w_in[0]), "pe_ckT": f(pe_ck[0].T), "pe_cvT": f(pe_cv[0].T), "w_ck1": f(w_ck1[0]), "w_ck2": f(w_ck2[0]),
        "w_cv1": f(w_cv1[0]), "w_cv2": f(w_cv2[0]), "w_br_a": f(w_br_a[0]), "w_br_b": f(w_br_b[0]), "w_out": f(w_out[0]),
    }
    m.update(consts)
    return m


def kernel(**inputs):
    inputs = {k_: np.asarray(v) for k_, v in inputs.items()}
    consts = _consts()
    nc = build_nc()
    in_maps = [_core_inputs(b, consts=consts, **inputs) for b in range(8)]
    res = run_bass_kernel_spmd(nc, in_maps, core_ids=list(range(8)))
    return np.stack([np.asarray(r["y"], np.float32) for r in res.results], axis=0)


def _tail(nc, k, st, I, SC, yT, dbg, dbg_out, debug, stop_after, L):
    ident, tri_le, tri_gt, toep, cmaskT, addtab = (L[n] for n in ("ident", "tri_le", "tri_gt", "toep", "cmaskT", "addtab"))
    G, gg_bc, cosC, sinC, kcmpT, vcx, kcvT, sA, PA, nextA, nextB, eps_t = (
        L[n] for n in ("G", "gg_bc", "cosC", "sinC", "kcmpT", "vcx", "kcvT", "sA", "PA", "nextA", "nextB", "eps_t"))
    y_out = yT.h
    X = mybir.AxisListType.X

    def bc(ap2, n):
        q = ap2.shape[1]
        return ap2.rearrange("p (q o) -> p q o", o=1).to_broadcast([128, q, n])

    def v3(ap2):
        return ap2.rearrange("p (q o) -> p q o", o=1)

    with ExitStack() as sC:
        wc1 = [k.sb(f"wc1_{i}", [64, 32, 256], BF16, sC) for i in range(2)]
        wc2 = [k.sb(f"wc2_{i}", [128, 2, 64], BF16, sC) for i in range(2)]
        peT = [k.sb(f"peT{i}", [64, 32], BF16, sC) for i in range(2)]
        for i, (n1, n2, npe) in enumerate((("w_ck1", "w_ck2", "pe_ckT"), ("w_cv1", "w_cv2", "pe_cvT"))):
            k.dma("pool", wc1[i][:], I[n1].rearrange("(l d) h -> d l h", d=64), wc1[i])
            k.dma("pool", wc2[i][:], I[n2].rearrange("(hh p) n -> p hh n", p=128), wc2[i])
            k.dma("pool", peT[i][:], I[npe][:, :], peT[i])
        for g in range(4):
            k.dma("sp", vcx[:, g, 64:97], I["vcx_tail"][:, :], vcx)
        cbias = k.sb("cbias", [128, 4], F32, sC)
        xh = k.sb("xh", [128, 127], F32, sC)
        x2 = k.sb("x2", [128, 127], F32, sC)
        sg = k.sb("sg", [128, 127], F32, sC)
        hcT = k.sb("hcT", [128, 2, 128], BF16, sC)
        kc16 = k.sb("kc16", [128, 64], BF16, sC)
        rtc = k.sb("rtc", [128, 4, 8], F32, sC)
        psb = PA[3]
        for kv in range(2):
            for half in range(2):
                col = kv * 2 + half
                for l in range(32):
                    k.mm(psb[:, col:col + 1], wc1[kv][:, l, half * 128:(half + 1) * 128], peT[kv][:, l:l + 1], l == 0, l == 31,
                         [wc1[kv], peT[kv]], [psb])
        k.cp("dve", cbias[:], psb[:, 0:4], [psb], [cbias])
        for kv in range(2):
            for g in range(4):
                src = kcvT[kv * 4 + g]
                for half in range(2):
                    ps = nextA()
                    for l in range(32):
                        k.mm(ps[:, 0:127], wc1[kv][:, l, half * 128:(half + 1) * 128], src[:, l:l + 2017:16], l == 0, l == 31,
                             [wc1[kv], src], [ps])
                    cc = kv * 2 + half
                    k.act(xh[:], ps[:, 0:127], AF.Identity, [ps, cbias], [xh], bias=cbias[:, cc:cc + 1], scale=1.0)
                    k.tt("dve", x2[:], xh[:], xh[:], ALU.mult, [xh], [x2])
                    k.ts("dve", x2[:], x2[:], 0.044715, ALU.mult, [x2], [x2], s2=1.0, op1=ALU.add)
                    k.tt("dve", x2[:], x2[:], xh[:], ALU.mult, [x2, xh], [x2])
                    k.act(sg[:], x2[:], AF.Sigmoid, [x2], [sg], scale=1.5957691216)
                    k.tt("dve", hcT[:, half, 0:127], xh[:], sg[:], ALU.mult, [xh, sg], [hcT])
                ps2 = nextA()
                for half in range(2):
                    k.mm(ps2[0:127, 0:64], hcT[:, half, 0:127], wc2[kv][:, half, :], half == 0, half == 1, [hcT, wc2[kv]], [ps2])
                if kv == 0:
                    k.cp("act", kc16[0:127, :], ps2[0:127, 0:64], [ps2], [kc16])
                    x1 = ps2[0:127, 0:8]
                    x2_ = ps2[0:127, 8:16]
                    k.tt("dve", rtc[0:127, 0, :], x1, cosC[0:127, :], ALU.mult, [ps2, cosC], [rtc])
                    k.tt("dve", rtc[0:127, 1, :], x2_, sinC[0:127, :], ALU.mult, [ps2, sinC], [rtc])
                    k.tt("dve", rtc[0:127, 2, :], x2_, cosC[0:127, :], ALU.mult, [ps2, cosC], [rtc])
                    k.tt("dve", rtc[0:127, 3, :], x1, sinC[0:127, :], ALU.mult, [ps2, sinC], [rtc])
                    k.tt("dve", kc16[0:127, 0:8], rtc[0:127, 0, :], rtc[0:127, 1, :], ALU.subtract, [rtc], [kc16])
                    k.tt("dve", kc16[0:127, 8:16], rtc[0:127, 2, :], rtc[0:127, 3, :], ALU.add, [rtc], [kc16])
                    pb = nextB()
                    k.tr(pb[0:64, 0:127], kc16[0:127, :], ident[0:127, 0:127], [kc16, ident], [pb])
                    k.cp("dve", kcmpT[:, g, 0:127], pb[0:64, 0:127], [pb], [kcmpT])
                else:
                    k.cp("act", vcx[0:127, g, 0:64], ps2[0:127, 0:64], [ps2], [vcx])
        k.barrier()
    sA.close()

    if "cmp" in debug:
        o = dbg("d_kcmpT", [64, 4, 128], BF16)
        k.dma("sp", o[:, :, :], kcmpT[:], o, [kcmpT])
        o = dbg("d_vcx", [128, 4, 97], BF16)
        k.dma("sp", o[:, :, :], vcx[:], o, [vcx])
    if stop_after == "P3":
        return _finish(nc, k, yT, dbg_out, SC, debug)

    gatedAT = k.sb("gatedAT", [128, 8, S], BF16)
    gatedBT = k.sb("gatedBT", [128, 8, S], BF16)

    with ExitStack() as s4:
        ksT = [k.sb(f"ksT{i}", [96, S], BF16, s4) for i in range(2)]
        kwT = [k.sb(f"kwT{i}", [64, S], BF16, s4) for i in range(2)]
        vsg = [k.sb(f"vsg{i}", [128, 16, 65], BF16, s4) for i in range(2)]
        vwg = [k.sb(f"vwg{i}", [128, 16, 65], BF16, s4) for i in range(2)]
        qTh = [[s4.enter_context(nc.sbuf_tensor(f"qT{i}_{r}", [96, S], BF16)) for r in range(4)] for i in range(2)]
        qTq = [[T(qTh[i][r], f"qTq{i}_{r}") for r in range(4)] for i in range(2)]
        qTm = [[[T(qTh[i][r], f"qTm{i}_{r}_{qc}") for qc in range(4)] for r in range(4)] for i in range(2)]
        for i in range(2):
            k.dma("sp", ksT[i][64:96, :], I["emat"][:, :], ksT[i])
            k.ms("pool", vsg[i][:, :, 64:65], 1.0, [vsg[i]])
            k.ms("pool", vwg[i][:, :, 64:65], 1.0, [vwg[i]])
        zag = [k.sb(f"zag{i}", [128, 4, 256], BF16, s4) for i in range(2)]
        Pcs = [k.sb(f"Pc{i}", [128, 512], BF16, s4) for i in range(2)]
        Pt = [k.sb(f"Pt{i}", [128, 512], BF16, s4) for i in range(4)]
        oacc_h = s4.enter_context(nc.sbuf_tensor("oacc", [128, 4, 4, 64], F32))
        oacc = [T(oacc_h, f"oacc{r}") for r in range(4)]
        imp = k.sb("imp", [128, 4, 32], F32, s4)
        tmpi = k.sb("tmpi", [128, 4, 32], F32, s4)
        tmpo = [k.sb(f"tmpo{i}", [128, 4, 64], F32, s4) for i in range(2)]
        rden = k.sb("rden", [128, 4], F32, s4)
        fsc = k.sb("fsc", [128, 4], F32, s4)
        sc = k.sb("sc", [128, 32], F32, s4)
        sc2 = k.sb("sc2", [128, 32], F32, s4)
        m8 = k.sb("m8", [128, 8], F32, s4)
        m8b = k.sb("m8b", [128, 8], F32, s4)
        msk = k.sb("msk", [128, 32], F32, s4)
        mb16 = k.sb("mb16", [128, 4, 32], BF16, s4)
        gtok = [k.sb(f"gtok{i}", [128, 4, 256], BF16, s4) for i in range(2)]
        gAT = T(gatedAT.h, "gatedAT")
        cn = {"pt": 0, "to": 0, "acc": 0, "first": True, "pc": 0}

        def load_group(g):
            i = g % 2
            k.dma("sp", ksT[i][0:64, :], SC["KST"].h[g * 64:(g + 1) * 64, :], ksT[i], [SC["KST"]])
            k.dma("sp", kwT[i][:, :], SC["KWT"].h[g * 64:(g + 1) * 64, :], kwT[i], [SC["KWT"]])
            k.dma("sp", vsg[i][:, :, 0:64], SC["VS"].h[:, g * 64:(g + 1) * 64].rearrange("(kt p) d -> p kt d", p=128), vsg[i], [SC["VS"]])
            k.dma("sp", vwg[i][:, :, 0:64], SC["VW"].h[:, g * 64:(g + 1) * 64].rearrange("(kt p) d -> p kt d", p=128), vwg[i], [SC["VW"]])
            for r in range(4):
                h = 4 * g + r
                k.dma("sp", qTh[i][r][0:64, :], SC["QAT"].h[h * 64:(h + 1) * 64, :], qTq[i][r], [SC["QAT"]])

        def evac_branch(acc_bank, width, r, h, qc, br, first_write):
            av = acc_bank[:, 0:4 * width].rearrange("p (q c) -> p q c", c=width)
            if br == 0:
                k.ts("dve", v3(rden[:]), av[:, :, 64:65], 1e-30, ALU.max, [acc_bank], [rden])
                k.op("dve", lambda E: E.reciprocal(out=rden[:], in_=rden[:]), [rden], [rden])
            else:
                k.op("dve", lambda E: E.reciprocal(out=v3(rden[:]), in_=av[:, :, 64:65]), [acc_bank], [rden])
            k.tt("dve", fsc[:], rden[:], G[:, 4 * qc:4 * qc + 4, 3 * h + br], ALU.mult, [rden, G], [fsc])
            if first_write:
                k.tt("dve", oacc_h[:, r, :, :], av[:, :, 0:64], bc(fsc[:], 64), ALU.mult, [acc_bank, fsc], [oacc[r]])
            else:
                tm = tmpo[cn["to"] % 2]
                cn["to"] += 1
                k.tt("dve", tm[:], av[:, :, 0:64], bc(fsc[:], 64), ALU.mult, [acc_bank, fsc], [tm])
                k.tt("pool", oacc_h[:, r, :, :], oacc_h[:, r, :, :], tm[:], ALU.add, [oacc[r], tm], [oacc[r]])

        load_group(0)
        for g in range(4):
            i = g % 2
            if g + 1 < 4:
                load_group(g + 1)
            for qc in range(4):
                zi = (g * 4 + qc) % 2
                gi = zi
                qs = slice(qc * 512, (qc + 1) * 512)
                k.dma("sp", zag[zi][:], SC["ZA"].h[qc * 512:(qc + 1) * 512, g * 256:(g + 1) * 256].rearrange("(qi p) c -> p qi c", p=128),
                      zag[zi], [SC["ZA"]])
                use_mask = qc >= 2
                def hookA():
                    for qi in range(4):
                        tt_ = 4 * qc + qi
                        k.tt("dve", sc[:], imp[:, qi, :], addtab[:, tt_, :], ALU.add, [imp, addtab], [sc])
                        k.op("dve", lambda E: E.max(out=m8[:], in_=sc[:]), [sc], [m8])
                        k.op("dve", lambda E: E.match_replace(out=sc2[:], in_to_replace=m8[:], in_values=sc[:], imm_value=-3.0e38),
                             [sc, m8], [sc2])
                        k.op("dve", lambda E: E.max(out=m8b[:], in_=sc2[:]), [sc2], [m8b])
                        k.ts("dve", msk[:], sc[:], m8b[:, 7:8], ALU.is_ge, [sc, m8b], [msk])
                        k.ts("dve", mb16[:, qi, :], msk[:], 1.0, ALU.subtract, [msk], [mb16], s2=BIG, op1=ALU.mult)

                def hookB():
                    pbm = nextB()
                    for qi in range(4):
                        k.tr(pbm[0:32, qi * 128:(qi + 1) * 128], mb16[:, qi, :], ident[:], [mb16, ident], [pbm])
                    for r in range(4):
                        k.cp("dve" if r % 2 == 0 else "act", qTh[i][r][64:96, qs], pbm[0:32, 0:512], [pbm], [qTm[i][r][qc]])

                KK = 96 if use_mask else 64
                items = [("cmp", r, 0, True, True) for r in range(4)]
                for r in range(4):
                    kts = list(range(max(0, 4 * qc - 4), 4 * qc + 4))
                    for n_, kt in enumerate(kts):
                        items.append(("win", r, kt, n_ == 0, n_ == len(kts) - 1))
                for r in range(4):
                    for kt in range(4 * qc + 4):
                        items.append(("sel", r, kt, kt == 0, kt == 4 * qc + 3))

                def score(it):
                    br, r, kt, _, _ = it
                    if br == "sel" and r == 0 and kt == 0 and use_mask:
                        hookB()
                    ps_ = nextA(4)
                    if br == "cmp":
                        k.mm(ps_[0:127, :], kcmpT[:, g, 0:127], qTh[i][r][0:64, qs], True, True, [kcmpT, qTq[i][r]], [ps_])
                    elif br == "win":
                        k.mm(ps_[:], kwT[i][:, kt * 128:(kt + 1) * 128], qTh[i][r][0:64, qs], True, True, [kwT[i], qTq[i][r]], [ps_])
                    else:
                        dq = [qTq[i][r]] + ([qTm[i][r][qc]] if use_mask else [])
                        k.mm(ps_[:], ksT[i][0:KK, kt * 128:(kt + 1) * 128], qTh[i][r][0:KK, qs], True, True, [ksT[i]] + dq, [ps_])
                    if br != "cmp":
                        if kt >= 4 * qc:
                            qd = kt - 4 * qc
                            k.op("pe", lambda E: E.matmul(ps_[:, qd * 128:(qd + 1) * 128], lhsT=ident[:], rhs=tri_le[:], start=False, stop=True,
                                                          skip_group_check=True), [ident, tri_le], [ps_])
                        ql = kt + 4 - 4 * qc
                        if br == "win" and 0 <= ql <= 3:
                            k.op("pe", lambda E: E.matmul(ps_[:, ql * 128:(ql + 1) * 128], lhsT=ident[:], rhs=tri_gt[:], start=False, stop=True,
                                                          skip_group_check=True), [ident, tri_gt], [ps_])
                    return ps_

                def post(it, ps_):
                    br, r, kt, is_first, is_last = it
                    h = 4 * g + r
                    if is_first:
                        cn["acc"] += 1
                        cn["first"] = True
                    acc = PA[4 + cn["acc"] % 3]
                    if br == "cmp":
                        Pc = Pcs[cn["pc"] % 2]
                        cn["pc"] += 1
                        pov = acc[:, 0:388].rearrange("p (q c) -> p q c", c=97)
                        k.act(Pc[0:127, :], ps_[0:127, :], AF.Exp, [ps_], [Pc], scale=0.125)
                        k.tt("dve", Pc[0:127, :], Pc[0:127, :], cmaskT[0:127, qs], ALU.mult, [Pc, cmaskT], [Pc])
                        for qi in range(4):
                            k.mm(acc[:, qi * 97:(qi + 1) * 97], Pc[0:127, qi * 128:(qi + 1) * 128], vcx[0:127, g, :], True, True, [Pc, vcx], [acc])
                        evac_branch(acc, 97, r, h, qc, 0, True)
                        if use_mask:
                            if r == 0:
                                k.tt("dve", imp[:], pov[:, :, 65:97], bc(rden[:], 32), ALU.mult, [acc, rden], [imp])
                            else:
                                k.tt("dve", tmpi[:], pov[:, :, 65:97], bc(rden[:], 32), ALU.mult, [acc, rden], [tmpi])
                                k.tt("pool", imp[:], imp[:], tmpi[:], ALU.add, [imp, tmpi], [imp])
                            if r == 3:
                                hookA()
                        return
                    j0 = max(0, kt - 4 * qc)
                    j1 = 3 if br == "sel" else min(3, kt + 4 - 4 * qc)
                    P = Pt[cn["pt"] % 4]
                    cn["pt"] += 1
                    k.act(P[:, j0 * 128:(j1 + 1) * 128], ps_[:, j0 * 128:(j1 + 1) * 128], AF.Exp, [ps_], [P], scale=0.125)
                    vt = vwg[i] if br == "win" else vsg[i]
                    for qi in range(j0, j1 + 1):
                        fst = cn["first"]
                        k.op("pe", lambda E: E.matmul(acc[:, qi * 65:(qi + 1) * 65], lhsT=P[:, qi * 128:(qi + 1) * 128], rhs=vt[:, kt, :],
                                                      start=fst, stop=(kt == 4 * qc + qi), skip_group_check=True), [P, vt], [acc])
                        cn["first"] = False
                    if is_last:
                        evac_branch(acc, 65, r, h, qc, 2 if br == "win" else 1, False)
                        if br == "sel":
                            k.tt("dve", gtok[gi][:, :, r * 64:(r + 1) * 64], oacc_h[:, r, :, :], zag[zi][:, :, r * 64:(r + 1) * 64], ALU.mult,
                                 [oacc[r], zag[zi]], [gtok[gi]])

                LA = 3
                pend = [score(items[n_]) for n_ in range(min(LA, len(items)))]
                for n_, it in enumerate(items):
                    ps_ = pend.pop(0)
                    if n_ + LA < len(items):
                        pend.append(score(items[n_ + LA]))
                    post(it, ps_)
                pbt = nextB()
                for j in range(2):
                    for qi in range(4):
                        c0 = (j * 4 + qi) * 128
                        k.tr(pbt[:, c0:c0 + 128], gtok[gi][:, qi, j * 128:(j + 1) * 128], ident[:], [gtok[gi], ident], [pbt])
                k.cp("act", gatedAT.h[:, 2 * g:2 * g + 2, qs], pbt[:, :].rearrange("p (j q) -> p j q", j=2), [pbt], [gAT])
        k.barrier()

    if "gA" in debug:
        o = dbg("d_gAT", [128, 8, S], BF16)
        k.dma("sp", o[:, :, :], gatedAT.h[:], o, [gAT])
    if stop_after == "P4":
        return _finish(nc, k, yT, dbg_out, SC, debug)

    gBT = T(gatedBT.h, "gatedBT")
    with ExitStack() as s5:
        kbT = [k.sb(f"kbT{i}", [128, S], BF16, s5) for i in range(2)]
        qbT = [k.sb(f"qbT{i}", [128, S], BF16, s5) for i in range(2)]
        vbh = [k.sb(f"vbh{i}", [128, 16, 129], BF16, s5) for i in range(2)]
        for i in range(2):
            k.ms("pool", vbh[i][:, :, 128:129], 1.0, [vbh[i]])
        zbg = [k.sb(f"zbg{i}", [128, 4, 128], BF16, s5) for i in range(2)]
        Pt = [k.sb(f"PtB{i}", [128, 512], BF16, s5) for i in range(6)]
        gtb = [k.sb(f"gtb{i}", [128, 4, 128], BF16, s5) for i in range(2)]
        tmpb = [k.sb(f"tmpb{i}", [128, 2, 128], F32, s5) for i in range(2)]
        rdb = [k.sb(f"rdb{i}", [128, 2], F32, s5) for i in range(2)]
        cn = {"pt": 0, "tb": 0, "first": [True, True]}
        SCB = 128.0 ** -0.5

        def load_head(h):
            i = h % 2
            k.dma("sp", kbT[i][:, :], SC["KBT"].h[h * 128:(h + 1) * 128, :], kbT[i], [SC["KBT"]])
            k.dma("sp", qbT[i][:, :], SC["QBT"].h[h * 128:(h + 1) * 128, :], qbT[i], [SC["QBT"]])
            k.dma("sp", vbh[i][:, :, 0:128], SC["VB"].h[:, h * 128:(h + 1) * 128].rearrange("(kt p) d -> p kt d", p=128), vbh[i], [SC["VB"]])

        load_head(0)
        for h in range(8):
            i = h % 2
            if h + 1 < 8:
                load_head(h + 1)
            items = []
            for qc in range(4):
                for kt in range(4 * qc + 4):
                    items.append((qc, kt))

            def score(it):
                qc, kt = it
                ps_ = nextA()
                k.mm(ps_[:], kbT[i][:, kt * 128:(kt + 1) * 128], qbT[i][:, qc * 512:(qc + 1) * 512], True, True, [kbT[i], qbT[i]], [ps_])
                return ps_

            def post(it, ps_):
                qc, kt = it
                zi = (h * 4 + qc) % 2
                qs = slice(qc * 512, (qc + 1) * 512)
                banks = [PA[3 + 2 * (qc % 2)], PA[4 + 2 * (qc % 2)]]
                if kt == 0:
                    k.dma("sp", zbg[zi][:], SC["ZB"].h[qc * 512:(qc + 1) * 512, h * 128:(h + 1) * 128].rearrange("(qi p) c -> p qi c", p=128),
                          zbg[zi], [SC["ZB"]])
                    cn["first"] = [True, True]
                first = cn["first"]
                j0 = max(0, kt - 4 * qc)
                P = Pt[cn["pt"] % 6]
                cn["pt"] += 1
                k.act(P[:, j0 * 128:512], ps_[:, j0 * 128:512], AF.Exp, [ps_], [P], scale=SCB)
                off = 512 * qc - 128 * kt + 384 + j0 * 128
                k.tt("dve", P[:, j0 * 128:512], P[:, j0 * 128:512], toep[:, off:off + (4 - j0) * 128], ALU.mult,
                     [P, toep], [P])
                for qi in range(j0, 4):
                    b_ = qi // 2
                    c0 = (qi % 2) * 129
                    fst = first[b_]
                    k.op("pe", lambda E: E.matmul(banks[b_][:, c0:c0 + 129], lhsT=P[:, qi * 128:(qi + 1) * 128], rhs=vbh[i][:, kt, :],
                                                  start=fst, stop=(kt == 4 * qc + qi), skip_group_check=True), [P, vbh[i]], [banks[b_]])
                    first[b_] = False
                if kt == 4 * qc + 3:
                    for b_ in range(2):
                        bv = banks[b_][:, 0:258].rearrange("p (q c) -> p q c", c=129)
                        rd = rdb[cn["tb"] % 2]
                        tb = tmpb[cn["tb"] % 2]
                        cn["tb"] += 1
                        k.op("dve", lambda E: E.reciprocal(out=v3(rd[:]), in_=bv[:, :, 128:129]), [banks[b_]], [rd])
                        k.tt("dve", tb[:], bv[:, :, 0:128], bc(rd[:], 128), ALU.mult, [banks[b_], rd], [tb])
                        k.tt("pool", gtb[zi][:, 2 * b_:2 * b_ + 2, :], tb[:], zbg[zi][:, 2 * b_:2 * b_ + 2, :], ALU.mult, [tb, zbg[zi]], [gtb[zi]])
                    pbt = nextB()
                    for qi in range(4):
                        k.tr(pbt[:, qi * 128:(qi + 1) * 128], gtb[zi][:, qi, :], ident[:], [gtb[zi], ident], [pbt])
                    k.cp("act", gatedBT.h[:, h, qs], pbt[:, 0:512], [pbt], [gBT])

            LA = 2
            pend = [score(items[n_]) for n_ in range(min(LA, len(items)))]
            for n_, it in enumerate(items):
                ps_ = pend.pop(0)
                if n_ + LA < len(items):
                    pend.append(score(items[n_ + LA]))
                post(it, ps_)
        k.barrier()

    if "gB" in debug:
        o = dbg("d_gBT", [128, 8, S], BF16)
        k.dma("sp", o[:, :, :], gatedBT.h[:], o, [gBT])
    if stop_after == "P5":
        return _finish(nc, k, yT, dbg_out, SC, debug)

    gAT = T(gatedAT.h, "gatedAT2")
    with ExitStack() as s6o:
        wo = k.sb("wo", [128, 16, D], BF16, s6o)
        wov = I["w_out"].rearrange("(kc p) n -> p kc n", p=128)
        for cb in range(4):
            k.dma("pool", wo[:, :, cb * 512:(cb + 1) * 512], wov[:, :, cb * 512:(cb + 1) * 512], wo)
        with ExitStack() as s6:
            wj = [[k.sb(f"wj{a}{i}", [128, 8, 128], BF16, s6) for i in range(2)] for a in range(2)]
            sm = [[k.sb(f"sm{a}{i}", [128, 512], BF16, s6) for i in range(2)] for a in range(2)]
            m1 = [k.sb(f"m1_{i}", [128, 512], F32, s6) for i in range(2)]
            m2 = [k.sb(f"m2_{i}", [128, 512], F32, s6) for i in range(2)]
            m16 = [k.sb(f"m16_{i}", [128, 512], BF16, s6) for i in range(2)]
            wv = [I["w_br_a"].rearrange("(wc p) n -> p wc n", p=128), I["w_br_b"].rearrange("(wc p) n -> p wc n", p=128)]
            gsrc = [(gatedAT.h, gAT), (gatedBT.h, gBT)]
            smn = ["SMA", "SMB"]
            def load_wj(j):
                for a in range(2):
                    k.dma("pool", wj[a][j % 2][:], wv[a][:, :, j * 128:(j + 1) * 128], wj[a][j % 2])
            load_wj(0)
            for j in range(16):
                i = j % 2
                if j + 1 < 16:
                    load_wj(j + 1)
                for tc_ in range(4):
                    ii = (j * 4 + tc_) % 2
                    tcs = slice(tc_ * 512, (tc_ + 1) * 512)
                    pss = []
                    for a in range(2):
                        k.dma("sp", sm[a][ii][:], SC[smn[a]].h[j * 128:(j + 1) * 128, tcs], sm[a][ii], [SC[smn[a]]])
                        ps = nextA()
                        for wc in range(8):
                            k.mm(ps[:], wj[a][i][:, wc, :], gsrc[a][0][:, wc, tcs], wc == 0, wc == 7, [wj[a][i], gsrc[a][1]], [ps])
                        pss.append(ps)
                    k.tt("dve", m1[ii][:], pss[0][:], sm[0][ii][:], ALU.mult, [pss[0], sm[0][ii]], [m1[ii]])
                    k.tt("dve", m2[ii][:], pss[1][:], sm[1][ii][:], ALU.mult, [pss[1], sm[1][ii]], [m2[ii]])
                    k.tt("pool", m16[ii][:], m1[ii][:], m2[ii][:], ALU.add, [m1[ii], m2[ii]], [m16[ii]])
                    k.dma("act", SC["MT"].h[j * 128:(j + 1) * 128, tcs], m16[ii][:], SC["MT"], [m16[ii]])
            k.barrier()
        with ExitStack() as s7:
            mt = [k.sb(f"mt{i}", [128, 16, 128], BF16, s7) for i in range(2)]
            xt = [k.sb(f"xt{i}", [128, D], F32, s7) for i in range(2)]
            yt = [k.sb(f"yt{i}", [128, D], F32, s7) for i in range(2)]
            ssq4 = k.sb("ssq4", [128, 4], F32, s7)
            junk = k.sb("junk6", [128, 512], BF16, s7)
            rs = k.sb("rs6", [128, 1], F32, s7)
            def load6(tt):
                i = tt % 2
                k.dma("sp", mt[i][:], SC["MT"].h[:, tt * 128:(tt + 1) * 128].rearrange("(kc p) t -> p kc t", p=128), mt[i], [SC["MT"]])
                k.dma("sp", xt[i][:], I["x"][tt * 128:(tt + 1) * 128, :], xt[i])
            load6(0)
            for tt in range(NT):
                i = tt % 2
                if tt + 1 < NT:
                    load6(tt + 1)
                k.ms("dve", ssq4[:], 0.0, [ssq4])
                for cb in range(4):
                    ps = PA[cb]
                    for kc in range(16):
                        k.mm(ps[:], mt[i][:, kc, :], wo[:, kc, cb * 512:(cb + 1) * 512], kc == 0, kc == 15, [mt[i], wo], [ps])
                    k.act(junk[:], ps[:], AF.Square, [ps, ssq4], [junk, ssq4], accum=ssq4[:, cb:cb + 1])
                k.op("dve", lambda E: E.reduce_sum(out=rs[:], in_=ssq4[:], axis=X), [ssq4], [rs])
                k.act(rs[:], rs[:], AF.Ln, [rs, eps_t], [rs], bias=eps_t[:, 0:1], scale=1.0 / D)
                k.act(rs[:], rs[:], AF.Exp, [rs], [rs], scale=-0.5)
                for cb in range(4):
                    cs = slice(cb * 512, (cb + 1) * 512)
                    k.stt("dve", yt[i][:, cs], PA[cb][:], rs[:, 0:1], gg_bc[:, cs], ALU.mult, ALU.mult, [PA[cb], rs, gg_bc], [yt[i]])
                k.tt("pool", yt[i][:], yt[i][:], xt[i][:], ALU.add, [yt[i], xt[i]], [yt[i]])
                k.dma("sp", y_out[tt * 128:(tt + 1) * 128, :], yt[i][:], yT, [yt[i]])
            k.barrier()
    return _finish(nc, k, yT, dbg_out, SC, debug)
```
